# Optimizing a Trainium2 kernel written in Bass

```python
import jax, jax.numpy as jnp
from jax import lax
import numpy as np

D_MODEL = 1024
BATCH = 8
SEQ = 4096
DEPTH = 2

A_WIDTH = D_MODEL // 2
B_WIDTH = D_MODEL - A_WIDTH
A_HEADS = 8
B_HEADS = 8
A_CONV = 31
B_CONV = 3
IN_WIDTH = 2 * A_WIDTH + 3 * B_WIDTH
POOL_WINDOWS = (2, 4, 8, 16)
POOL_GROUPS = len(POOL_WINDOWS)
POOL_GROUP = D_MODEL // POOL_GROUPS
D_FF = ((8 * D_MODEL // 3 + 255) // 256) * 256
FFN_CONV = 3
RMS_EPS = 1e-6
LN_EPS = 1e-5
N_EVEN = (DEPTH + 1) // 2
N_ODD = DEPTH // 2

kernel_name = "hybrid_conv_pool_convffn_trunk"


def rmsnorm(x, g):
    xf = x.astype(jnp.float32)
    y = xf * lax.rsqrt(jnp.mean(xf * xf, axis=-1, keepdims=True) + RMS_EPS)
    return (y * g.astype(jnp.float32)).astype(x.dtype)


def layernorm(x, g, b):
    xf = x.astype(jnp.float32)
    mu = jnp.mean(xf, axis=-1, keepdims=True)
    xc = xf - mu
    var = jnp.mean(xc * xc, axis=-1, keepdims=True)
    y = xc * lax.rsqrt(var + LN_EPS) * g.astype(jnp.float32) + b.astype(jnp.float32)
    return y.astype(x.dtype)


def causal_dwconv(x, w):
    k, c = w.shape
    return lax.conv_general_dilated(
        x, w[:, None, :].astype(x.dtype), window_strides=(1,), padding=[(k - 1, 0)],
        dimension_numbers=("NWC", "WIO", "NWC"), feature_group_count=c)


def conv_mixer(h, w_in, conv_a, ln_a_g, ln_a_b, conv_b, w_out):
    z = h @ w_in
    a_val, a_gate, b_gate, c_gate, bc_val = jnp.split(
        z, [A_WIDTH, 2 * A_WIDTH, 2 * A_WIDTH + B_WIDTH, 2 * A_WIDTH + 2 * B_WIDTH], axis=-1)
    a = causal_dwconv(a_val * jax.nn.sigmoid(a_gate), conv_a)
    a = jax.nn.silu(layernorm(a, ln_a_g, ln_a_b))
    b = b_gate * causal_dwconv(c_gate * bc_val, conv_b)
    return jnp.concatenate([a, b], axis=-1) @ w_out


def pool_mixer(h, w_pool, pool_scale):
    s = h.shape[1]
    hf = h.astype(jnp.float32)
    cs = jnp.cumsum(hf, axis=1)
    t = jnp.arange(1, s + 1, dtype=jnp.float32)[:, None]
    outs = []
    for g, w in enumerate(POOL_WINDOWS):
        sl = slice(g * POOL_GROUP, (g + 1) * POOL_GROUP)
        c = cs[..., sl]
        prev = jnp.pad(c, ((0, 0), (w, 0), (0, 0)))[:, :s]
        mean = (c - prev) / jnp.minimum(t, float(w))
        outs.append(mean - hf[..., sl])
    p = jnp.stack(outs, axis=2).astype(h.dtype)
    y = jnp.einsum("bsgc,gcd->bsgd", p, w_pool).reshape(h.shape)
    return y * pool_scale


def conv_ffn(h, w_up, w_conv, w_down):
    u = causal_dwconv(h @ w_up, w_conv)
    g, v = jnp.split(u, 2, axis=-1)
    return (jax.nn.silu(g) * v) @ w_down


def setup_inputs(seed: int = 0) -> dict:
    key = jax.random.key(seed)
    ks = jax.random.split(key, 20)
    f32 = jnp.float32
    nrm = lambda k, shape, scale: jax.random.normal(k, shape, f32) * scale
    return {
        "x": nrm(ks[0], (BATCH, SEQ, D_MODEL), 1.0),
        "norm_mix_even": 1.0 + nrm(ks[1], (N_EVEN, D_MODEL), 0.02),
        "w_in": nrm(ks[2], (N_EVEN, D_MODEL, IN_WIDTH), D_MODEL ** -0.5),
        "conv_a": nrm(ks[3], (N_EVEN, A_CONV, A_WIDTH), A_CONV ** -0.5),
        "ln_a_g": 1.0 + nrm(ks[4], (N_EVEN, A_WIDTH), 0.02),
        "ln_a_b": nrm(ks[5], (N_EVEN, A_WIDTH), 0.02),
        "conv_b": nrm(ks[6], (N_EVEN, B_CONV, B_WIDTH), B_CONV ** -0.5),
        "w_out": nrm(ks[7], (N_EVEN, D_MODEL, D_MODEL), D_MODEL ** -0.5),
        "norm_mix_odd": 1.0 + nrm(ks[8], (N_ODD, D_MODEL), 0.02),
        "w_pool": nrm(ks[9], (N_ODD, POOL_GROUPS, POOL_GROUP, POOL_GROUP), POOL_GROUP ** -0.5),
        "pool_scale": 1.0 + nrm(ks[10], (N_ODD, D_MODEL), 0.1),
        "norm_ffn": 1.0 + nrm(ks[11], (DEPTH, D_MODEL), 0.02),
        "w_up": nrm(ks[12], (DEPTH, D_MODEL, 2 * D_FF), D_MODEL ** -0.5),
        "conv_ffn_w": nrm(ks[13], (DEPTH, FFN_CONV, 2 * D_FF), FFN_CONV ** -0.5),
        "w_down": nrm(ks[14], (DEPTH, D_FF, D_MODEL), D_FF ** -0.5),
        "norm_final": 1.0 + nrm(ks[15], (D_MODEL,), 0.02),
    }


def reference(x, norm_mix_even, w_in, conv_a, ln_a_g, ln_a_b, conv_b, w_out,
              norm_mix_odd, w_pool, pool_scale, norm_ffn, w_up, conv_ffn_w, w_down,
              norm_final):
    for layer in range(DEPTH):
        i = layer // 2
        if layer % 2 == 0:
            h = rmsnorm(x, norm_mix_even[i])
            x = x + conv_mixer(h, w_in[i], conv_a[i], ln_a_g[i], ln_a_b[i], conv_b[i], w_out[i])
        else:
            h = rmsnorm(x, norm_mix_odd[i])
            x = x + pool_mixer(h, w_pool[i], pool_scale[i])
        x = x + conv_ffn(rmsnorm(x, norm_ffn[layer]), w_up[layer], conv_ffn_w[layer], w_down[layer])
    return rmsnorm(x, norm_final)
```

```python
import numpy as np
import concourse.bass as bass
import concourse.mybir as mybir
from concourse.bass_utils import run_bass_kernel_spmd

F32 = mybir.dt.float32
BF16 = mybir.dt.bfloat16
ALU = mybir.AluOpType
AF = mybir.ActivationFunctionType

D = 1024
S = 4096
NCORE = 8
DFF = 2816
NF = DFF // 128
T = 512
NT = S // T
KC = D // 128
RMS_EPS = 1e-6
LN_EPS = 1e-5
NSLOT = 5
RINGW = 2816
POOL_W = (2, 4, 8, 16)

PC_GMIX0, PC_GFFN0, PC_GMIX1, PC_GFFN1, PC_GFIN = 0, 8, 16, 24, 32
PC_LNG, PC_LNB = 40, 44
PC_CONVB = 48
PC_PSCALE = 60
PC_CFFN = 68
NPAR = PC_CFFN + 2 * 44 * 3


class Res:
    __slots__ = ("name", "w", "r")

    def __init__(self, name):
        self.name = name
        self.w = None
        self.r = []


class Op:
    __slots__ = ("eng", "idx", "fn", "deps", "sig", "sem", "val", "dma")


class Prog:
    ENGS = ("pe", "act", "dve", "pool", "sp")

    def __init__(self):
        self.ops = {e: [] for e in self.ENGS}
        self.dma_count = {}

    def add(self, eng, fn, reads=(), writes=(), dma=None):
        op = Op()
        op.eng = eng
        op.idx = len(self.ops[eng])
        op.fn = fn
        op.sig = False
        op.dma = dma
        op.sem = None
        op.val = 0
        deps = set()
        for r in reads:
            if r.w is not None:
                deps.add(r.w)
        for w in writes:
            if w.w is not None:
                deps.add(w.w)
            for rd in w.r:
                deps.add(rd)
        deps.discard(op)
        op.deps = deps
        for r in reads:
            r.r.append(op)
        for w in writes:
            w.w = op
            w.r = []
        self.ops[eng].append(op)
        return op

    def _needs_wait(self, op, dep):
        if dep.dma is not None:
            return True
        if dep.eng != op.eng:
            return True
        if op.eng == "pe":
            return False
        return (op.idx - dep.idx) <= 1

    def finalize(self):
        for e in self.ENGS:
            for op in self.ops[e]:
                for d in op.deps:
                    if self._needs_wait(op, d):
                        d.sig = True
        dma_keys = []
        for e in self.ENGS:
            cnt = 0
            for op in self.ops[e]:
                if op.dma is not None:
                    if op.dma not in self.dma_count:
                        self.dma_count[op.dma] = 0
                        dma_keys.append(op.dma)
                    self.dma_count[op.dma] += 16
                    op.sem = "dma_" + op.dma
                    op.val = self.dma_count[op.dma]
                elif op.sig:
                    cnt += 1
                    op.sem = "eng_" + e
                    op.val = cnt
        return ["eng_" + e for e in self.ENGS if e != "sp"] + ["dma_" + k for k in dma_keys]

    def emit(self, eng_name, engine, sems):
        seen = {}
        nwait = 0
        for op in self.ops[eng_name]:
            need = {}
            for d in op.deps:
                if not self._needs_wait(op, d):
                    continue
                if seen.get(d.sem, 0) >= d.val:
                    continue
                if need.get(d.sem, 0) < d.val:
                    need[d.sem] = d.val
            for sname, v in need.items():
                engine.wait_ge(sems[sname], v)
                seen[sname] = v
                nwait += 1
            if op.fn is None:
                continue
            ins = op.fn(engine)
            if op.dma is not None:
                ins.then_inc(sems[op.sem], 16)
            elif op.sig:
                ins.then_inc(sems[op.sem], 1)
        return nwait


def build_program(nt_run=NT, stop_after="final"):
    nc = bass.Bass("TRN2", target_bir_lowering=False)
    P = Prog()

    def dram(name, shape, dt, kind):
        return nc.dram_tensor(name, shape, dt, kind=kind).ap()

    x_l = dram("x_l", [NT, 128, KC * T], F32, "ExternalInput")
    par_d = dram("par", [128, NPAR], F32, "ExternalInput")
    w32 = {
        "win": dram("win32", [20, 128, 1024], F32, "ExternalInput"),
        "wcv": dram("wcv32", [8, 128, 2048], F32, "ExternalInput"),
        "wout": dram("wout32", [8, 128, 1024], F32, "ExternalInput"),
        "wup0": dram("wup032", [NF, 128, 2048], F32, "ExternalInput"),
        "wdn0": dram("wdn032", [8, 128, 2816], F32, "ExternalInput"),
        "wpool": dram("wpool32", [1, 128, 2048], F32, "ExternalInput"),
        "wup1": dram("wup132", [NF, 128, 2048], F32, "ExternalInput"),
        "wdn1": dram("wdn132", [8, 128, 2816], F32, "ExternalInput"),
    }
    wbf = {k: dram(k + "bf", list(v.shape), BF16, "Internal") for k, v in w32.items()}
    out_l = dram("out_l", [NT, 128, KC * T], F32, "ExternalOutput")

    WS = {k: Res("ws_" + k) for k in w32}

    from contextlib import ExitStack
    es = ExitStack()

    def sb(name, shape, dt):
        return es.enter_context(nc.sbuf_tensor(name, shape, dt))

    with es:
        xres = [sb(f"xres{i}", [128, KC, T], F32) for i in range(2)]
        hbf = sb("hbf", [128, KC, T], BF16)
        sqbf = sb("sqbf", [128, KC, T], BF16)
        actb = sb("actb", [128, NF, T], BF16)
        yab = sb("yab", [128, KC, T], BF16)
        NUB, NCB, NSG, NTMP = 2, 2, 2, 3
        ubuf = [sb(f"ubuf{i}", [128, 2, T + 2], F32) for i in range(NUB)]
        cbuf = [sb(f"cbuf{i}", [128, 2, T], F32) for i in range(NCB)]
        sgb = [sb(f"sgb{i}", [128, T], F32) for i in range(NSG)]
        tmp = [sb(f"tmp{i}", [128, T], F32) for i in range(NTMP)]
        ring = [sb(f"ring{i}", [128, RINGW], BF16) for i in range(NSLOT)]
        ssum = sb("ssum", [128, T], F32)
        rstd = sb("rstd", [128, T], F32)
        mean = sb("mean", [128, T], F32)
        var = sb("var", [128, T], F32)
        rstd2 = sb("rstd2", [128, T], F32)
        glu = sb("glu", [128, 4, T + 30], BF16)
        af32 = sb("af32", [128, 4, T], F32)
        abf = sb("abf", [128, 4, T], BF16)
        asq = sb("asq", [128, 4, T], BF16)
        cv = sb("cv", [128, 4, T + 2], F32)
        accb = sb("accb", [128, T], F32)
        h1g = sb("h1g", [128, 2, T + 16], F32)
        sA = sb("sA", [128, 2, T + 16], F32)
        sB = sb("sB", [128, 2, T + 16], F32)
        t16 = sb("t16", [128, 2, 16], F32)
        halF = sb("halF", [128, 2 * NF, 2, 2], F32)
        h1hal = sb("h1hal", [128, KC, 16], F32)
        par = sb("par_sb", [128, NPAR], F32)
        ones = sb("ones", [128, 128], BF16)
        mhalf = sb("mhalf", [128, T], F32)
        invc = sb("invc", [128, 4, 16], F32)
        wpool = sb("wpool_sb", [128, 2048], BF16)
        ps = es.enter_context(nc.psum_tensor("ps", [128, 8, T], F32))

        XC = [[Res(f"x{b}_{c}") for c in range(KC)] for b in range(2)]
        HB = [Res(f"hbf{c}") for c in range(KC)]
        SQ = [Res("sq0"), Res("sq1")]
        ACTR = [Res(f"act{f}") for f in range(NF)]
        YAB = [Res(f"yab{c}") for c in range(KC)]
        UBG = [Res(f"ubg{i}") for i in range(NUB)]
        UBV = [Res(f"ubv{i}") for i in range(NUB)]
        UBH = [Res(f"ubh{i}") for i in range(NUB)]
        CBG = [Res(f"cbg{i}") for i in range(NCB)]
        CBV = [Res(f"cbv{i}") for i in range(NCB)]
        SG = [Res(f"sg{i}") for i in range(NSG)]
        TMP = [Res(f"tmp{i}") for i in range(NTMP)]
        RING = [Res(f"ring{i}") for i in range(NSLOT)]
        PSB = [Res(f"psb{i}") for i in range(8)]
        SS, RS, MEAN, VAR, RS2 = Res("ss"), Res("rs"), Res("mean"), Res("var"), Res("rs2")
        GLU = [Res(f"glu{j}") for j in range(4)]
        AFR = [Res(f"af{j}") for j in range(4)]
        ABF = [Res(f"abf{j}") for j in range(4)]
        ASQ = [Res(f"asq{j}") for j in range(4)]
        CV = [Res(f"cv{j}") for j in range(4)]
        ACCB = Res("accb")
        H1G, H1GH, SA_, SB_, T16 = Res("h1g"), Res("h1gh"), Res("sA"), Res("sB"), Res("t16")
        HALF = [Res(f"half{i}") for i in range(2 * NF)]
        H1HAL = [Res(f"h1hal{g}") for g in range(4)]
        PAR, ONES, MHALF, INVC, WPOOL = Res("par"), Res("ones"), Res("mhalf"), Res("invc"), Res("wpool")

        state = {"bank": 0, "ring": 0, "ub": 0, "cb": 0, "sg": 0, "tmp": 0}

        def rot(key, n):
            v = state[key] % n
            state[key] += 1
            return v

        def pcol(c):
            return par[:, c:c + 1]

        P.add("sp", lambda e: e.dma_start(out=par[:, :], in_=par_d[:, :]), writes=[PAR], dma="par")
        P.add("sp", lambda e: e.dma_start(out=xres[0][:, :, :], in_=x_l[0].rearrange("p (c t) -> p c t", c=KC)),
              writes=XC[0], dma="x0")
        P.add("dve", lambda e: e.memset(ones[:, :], 1.0), writes=[ONES])
        P.add("dve", lambda e: e.memset(mhalf[:, :], -0.5), writes=[MHALF])
        P.add("dve", lambda e: e.memset(glu[:, :, 0:30], 0.0), writes=GLU)
        P.add("dve", lambda e: e.memset(cv[:, :, 0:2], 0.0), writes=CV)
        P.add("dve", lambda e: e.memset(halF[:, :, :, :], 0.0), writes=HALF)
        P.add("dve", lambda e: e.memset(h1hal[:, :, :], 0.0), writes=H1HAL)

        def invc_fill(e):
            ins = None
            for gi, w in enumerate(POOL_W):
                ins = e.memset(invc[:, gi, :], 1.0 / w)
                for j in range(w - 1):
                    ins = e.memset(invc[:, gi, j:j + 1], 1.0 / (j + 1))
            return ins
        P.add("pool", invc_fill, writes=[INVC])

        for k in ("win", "wcv", "wout", "wup0", "wdn0", "wpool", "wup1", "wdn1"):
            nblk = w32[k].shape[0]
            for b in range(nblk):
                P.add("pool", lambda e, k=k, b=b: e.dma_start(out=wbf[k][b], in_=w32[k][b]),
                      writes=[WS[k]], dma="cv_" + k)
        P.add("sp", lambda e: e.dma_start(out=wpool[:, :], in_=wbf["wpool"][0]), reads=[WS["wpool"]],
              writes=[WPOOL], dma="wpool")

        def ring_load(kind, blk, ncols):
            s = rot("ring", NSLOT)
            P.add("sp", lambda e: e.dma_start(out=ring[s][:, 0:ncols], in_=wbf[kind][blk]),
                  reads=[WS[kind]], writes=[RING[s]], dma=f"ring{s}")
            return s

        def mm_group(pairs, reads):
            b = rot("bank", 8)
            n = len(pairs)

            def fn(e):
                ins = None
                for i, (l, r) in enumerate(pairs):
                    ins = e.matmul(ps[:, b, :], lhsT=l, rhs=r, start=(i == 0), stop=(i == n - 1))
                return ins
            P.add("pe", fn, reads=reads, writes=[PSB[b]])
            return b

        def emit_stats(xb):
            for hh in range(2):
                P.add("act", lambda e, hh=hh: e.activation(out=sqbf[:, 4 * hh:4 * hh + 4, :],
                                                           in_=xres[xb][:, 4 * hh:4 * hh + 4, :], func=AF.Square),
                      reads=XC[xb][4 * hh:4 * hh + 4], writes=[SQ[hh]])
            b = mm_group([(ones[:, :], sqbf[:, c, :]) for c in range(KC)], [ONES] + SQ)
            P.add("dve", lambda e: e.tensor_scalar(out=ssum[:, :], in0=ps[:, b, :], scalar1=1.0 / D, scalar2=RMS_EPS,
                                                   op0=ALU.mult, op1=ALU.add), reads=[PSB[b]], writes=[SS])
            P.add("pool", lambda e: e.tensor_tensor(out=rstd[:, :], in0=ssum[:, :], in1=mhalf[:, :], op=ALU.pow),
                  reads=[SS, MHALF], writes=[RS])

        def emit_norm_hbf(xb, gc):
            emit_stats(xb)
            for c in range(KC):
                P.add("dve", lambda e, c=c: e.scalar_tensor_tensor(out=hbf[:, c, :], in0=xres[xb][:, c, :],
                                                                   scalar=pcol(gc + c), in1=rstd[:, :],
                                                                   op0=ALU.mult, op1=ALU.mult),
                      reads=[XC[xb][c], RS, PAR], writes=[HB[c]])

        def resid_add(xb, d, b):
            P.add("dve", lambda e: e.tensor_tensor(out=xres[xb][:, d, :], in0=xres[xb][:, d, :], in1=ps[:, b, :],
                                                   op=ALU.add), reads=[PSB[b], XC[xb][d]], writes=[XC[xb][d]])

        def emit_mixer0(xb):
            emit_norm_hbf(xb, PC_GMIX0)

            def proj(blk):
                s = ring_load("win", blk, 1024)
                return mm_group([(ring[s][:, k * 128:(k + 1) * 128], hbf[:, k, :]) for k in range(KC)],
                                [RING[s]] + HB)
            for j in range(4):
                bgate = proj(4 + j)
                bval = proj(j)
                ta = rot("tmp", NTMP)
                P.add("act", lambda e, ta=ta, bgate=bgate: e.activation(out=tmp[ta][:, :], in_=ps[:, bgate, :],
                                                                        func=AF.Tanh, scale=0.5),
                      reads=[PSB[bgate]], writes=[TMP[ta]])
                tb = rot("tmp", NTMP)
                P.add("act", lambda e, tb=tb, bval=bval: e.activation(out=tmp[tb][:, :], in_=ps[:, bval, :],
                                                                      func=AF.Copy, scale=0.5),
                      reads=[PSB[bval]], writes=[TMP[tb]])
                P.add("dve", lambda e, ta=ta, tb=tb, j=j: e.scalar_tensor_tensor(
                    out=glu[:, j, 30:30 + T], in0=tmp[ta][:, :], scalar=1.0, in1=tmp[tb][:, :],
                    op0=ALU.add, op1=ALU.mult), reads=[TMP[ta], TMP[tb]], writes=[GLU[j]])
            for j in range(4):
                s0 = ring_load("wcv", 2 * j, 2048)
                s1 = ring_load("wcv", 2 * j + 1, 2048)
                pairs = []
                for i in range(31):
                    sl = s0 if i < 16 else s1
                    ii = i % 16
                    pairs.append((ring[sl][:, ii * 128:(ii + 1) * 128], glu[:, j, i:i + T]))
                b = mm_group(pairs, [RING[s0], RING[s1], GLU[j]])
                P.add("act", lambda e, j=j, b=b: e.activation(out=af32[:, j, :], in_=ps[:, b, :], func=AF.Copy),
                      reads=[PSB[b]], writes=[AFR[j]])
                P.add("act", lambda e, j=j, b=b: e.activation(out=asq[:, j, :], in_=ps[:, b, :], func=AF.Square),
                      reads=[PSB[b]], writes=[ASQ[j]])
                P.add("pool", lambda e, j=j: e.tensor_copy(out=abf[:, j, :], in_=af32[:, j, :]),
                      reads=[AFR[j]], writes=[ABF[j]])
                P.add("pool", lambda e, j=j: e.tensor_copy(out=glu[:, j, 0:30], in_=glu[:, j, T:T + 30]),
                      reads=[GLU[j]], writes=[GLU[j]])
            for j in range(4):
                bc = proj(12 + j)
                bbc = proj(16 + j)
                bb = proj(8 + j)
                tc = rot("tmp", NTMP)
                P.add("act", lambda e, tc=tc, bc=bc: e.activation(out=tmp[tc][:, :], in_=ps[:, bc, :], func=AF.Copy),
                      reads=[PSB[bc]], writes=[TMP[tc]])
                P.add("dve", lambda e, tc=tc, bbc=bbc, j=j: e.tensor_tensor(out=cv[:, j, 2:2 + T], in0=tmp[tc][:, :],
                                                                            in1=ps[:, bbc, :], op=ALU.mult),
                      reads=[TMP[tc], PSB[bbc]], writes=[CV[j]])
                wc = PC_CONVB + 3 * j
                P.add("dve", lambda e, j=j, wc=wc: e.tensor_scalar(out=accb[:, :], in0=cv[:, j, 2:2 + T],
                                                                   scalar1=pcol(wc + 2), scalar2=None, op0=ALU.mult),
                      reads=[CV[j], PAR], writes=[ACCB])
                P.add("dve", lambda e, j=j, wc=wc: e.scalar_tensor_tensor(out=accb[:, :], in0=cv[:, j, 1:1 + T],
                                                                          scalar=pcol(wc + 1), in1=accb[:, :],
                                                                          op0=ALU.mult, op1=ALU.add),
                      reads=[CV[j], PAR, ACCB], writes=[ACCB])
                P.add("dve", lambda e, j=j, wc=wc: e.scalar_tensor_tensor(out=accb[:, :], in0=cv[:, j, 0:T],
                                                                          scalar=pcol(wc), in1=accb[:, :],
                                                                          op0=ALU.mult, op1=ALU.add),
                      reads=[CV[j], PAR, ACCB], writes=[ACCB])
                P.add("dve", lambda e, j=j, bb=bb: e.tensor_tensor(out=yab[:, 4 + j, :], in0=accb[:, :],
                                                                   in1=ps[:, bb, :], op=ALU.mult),
                      reads=[ACCB, PSB[bb]], writes=[YAB[4 + j]])
                P.add("pool", lambda e, j=j: e.tensor_copy(out=cv[:, j, 0:2], in_=cv[:, j, T:T + 2]),
                      reads=[CV[j]], writes=[CV[j]])
            b1 = mm_group([(ones[:, :], abf[:, j, :]) for j in range(4)], [ONES] + ABF)
            b2 = mm_group([(ones[:, :], asq[:, j, :]) for j in range(4)], [ONES] + ASQ)
            P.add("dve", lambda e: e.tensor_scalar(out=mean[:, :], in0=ps[:, b1, :], scalar1=1.0 / 512, scalar2=None,
                                                   op0=ALU.mult), reads=[PSB[b1]], writes=[MEAN])
            tm = rot("tmp", NTMP)
            P.add("dve", lambda e: e.tensor_tensor(out=tmp[tm][:, :], in0=mean[:, :], in1=mean[:, :], op=ALU.mult),
                  reads=[MEAN], writes=[TMP[tm]])
            P.add("dve", lambda e: e.scalar_tensor_tensor(out=var[:, :], in0=ps[:, b2, :], scalar=1.0 / 512,
                                                          in1=tmp[tm][:, :], op0=ALU.mult, op1=ALU.subtract),
                  reads=[PSB[b2], TMP[tm]], writes=[VAR])
            P.add("pool", lambda e: e.tensor_scalar(out=var[:, :], in0=var[:, :], scalar1=LN_EPS, scalar2=None,
                                                    op0=ALU.add), reads=[VAR], writes=[VAR])
            P.add("pool", lambda e: e.tensor_tensor(out=rstd2[:, :], in0=var[:, :], in1=mhalf[:, :], op=ALU.pow),
                  reads=[VAR, MHALF], writes=[RS2])
            for j in range(4):
                t1 = rot("tmp", NTMP)
                P.add("dve", lambda e, j=j, t1=t1: e.tensor_tensor(out=tmp[t1][:, :], in0=af32[:, j, :],
                                                                   in1=mean[:, :], op=ALU.subtract),
                      reads=[AFR[j], MEAN], writes=[TMP[t1]])
                t2 = rot("tmp", NTMP)
                P.add("dve", lambda e, t1=t1, t2=t2: e.tensor_tensor(out=tmp[t2][:, :], in0=tmp[t1][:, :],
                                                                     in1=rstd2[:, :], op=ALU.mult),
                      reads=[TMP[t1], RS2], writes=[TMP[t2]])
                P.add("act", lambda e, j=j, t2=t2: e.activation(out=yab[:, j, :], in_=tmp[t2][:, :], func=AF.Silu,
                                                                scale=pcol(PC_LNG + j), bias=pcol(PC_LNB + j)),
                      reads=[TMP[t2], PAR], writes=[YAB[j]])
            for d in range(KC):
                s = ring_load("wout", d, 1024)
                b = mm_group([(ring[s][:, k * 128:(k + 1) * 128], yab[:, k, :]) for k in range(KC)],
                             [RING[s]] + YAB)
                resid_add(xb, d, b)

        def emit_ffn(xb, L):
            emit_norm_hbf(xb, PC_GFFN0 if L == 0 else PC_GFFN1)
            kup = "wup0" if L == 0 else "wup1"
            kdn = "wdn0" if L == 0 else "wdn1"
            for f in range(NF):
                s = ring_load(kup, f, 2048)
                bg = mm_group([(ring[s][:, k * 256:k * 256 + 128], hbf[:, k, :]) for k in range(KC)], [RING[s]] + HB)
                bv = mm_group([(ring[s][:, k * 256 + 128:k * 256 + 256], hbf[:, k, :]) for k in range(KC)],
                              [RING[s]] + HB)
                ui = rot("ub", NUB)
                ci = rot("cb", NCB)
                si = rot("sg", NSG)
                hi = L * NF + f
                P.add("pool", lambda e, ui=ui, hi=hi: e.tensor_copy(out=ubuf[ui][:, :, 0:2], in_=halF[:, hi, :, :]),
                      reads=[HALF[hi]], writes=[UBH[ui]])
                P.add("act", lambda e, ui=ui, bg=bg: e.activation(out=ubuf[ui][:, 0, 2:2 + T], in_=ps[:, bg, :],
                                                                  func=AF.Copy), reads=[PSB[bg]], writes=[UBG[ui]])
                P.add("act", lambda e, ui=ui, bv=bv: e.activation(out=ubuf[ui][:, 1, 2:2 + T], in_=ps[:, bv, :],
                                                                  func=AF.Copy), reads=[PSB[bv]], writes=[UBV[ui]])
                P.add("pool", lambda e, ui=ui, hi=hi: e.tensor_copy(out=halF[:, hi, :, :], in_=ubuf[ui][:, :, T:T + 2]),
                      reads=[UBG[ui], UBV[ui]], writes=[HALF[hi]])
                wg = PC_CFFN + (L * 44 + f) * 3
                wv = PC_CFFN + (L * 44 + NF + f) * 3
                for tap in (2, 1, 0):
                    for q, wq, R, CR in ((0, wg, UBG, CBG), (1, wv, UBV, CBV)):
                        if tap == 2:
                            P.add("dve", lambda e, ui=ui, ci=ci, q=q, wq=wq: e.tensor_scalar(
                                out=cbuf[ci][:, q, :], in0=ubuf[ui][:, q, 2:2 + T], scalar1=pcol(wq + 2),
                                scalar2=None, op0=ALU.mult), reads=[R[ui], PAR], writes=[CR[ci]])
                        else:
                            P.add("dve", lambda e, ui=ui, ci=ci, q=q, wq=wq, tap=tap: e.scalar_tensor_tensor(
                                out=cbuf[ci][:, q, :], in0=ubuf[ui][:, q, tap:tap + T], scalar=pcol(wq + tap),
                                in1=cbuf[ci][:, q, :], op0=ALU.mult, op1=ALU.add),
                                reads=[R[ui], UBH[ui], PAR, CR[ci]], writes=[CR[ci]])
                P.add("act", lambda e, ci=ci, si=si: e.activation(out=sgb[si][:, :], in_=cbuf[ci][:, 0, :],
                                                                  func=AF.Silu), reads=[CBG[ci]], writes=[SG[si]])
                P.add("pool", lambda e, ci=ci, si=si, f=f: e.tensor_tensor(out=actb[:, f, :], in0=sgb[si][:, :],
                                                                           in1=cbuf[ci][:, 1, :], op=ALU.mult),
                      reads=[SG[si], CBV[ci]], writes=[ACTR[f]])
            for d in range(KC):
                s = ring_load(kdn, d, 2816)
                b = mm_group([(ring[s][:, f * 128:(f + 1) * 128], actb[:, f, :]) for f in range(NF)],
                             [RING[s]] + ACTR)
                resid_add(xb, d, b)

        def emit_mixer1(xb, first_tile):
            emit_stats(xb)
            L_ = T + 16
            for gi, w in enumerate(POOL_W):
                c0 = 2 * gi
                P.add("pool", lambda e, c0=c0: e.tensor_copy(out=h1g[:, :, 0:16], in_=h1hal[:, c0:c0 + 2, :]),
                      reads=[H1HAL[gi]], writes=[H1GH])
                for q in range(2):
                    P.add("dve", lambda e, q=q, c0=c0: e.scalar_tensor_tensor(
                        out=h1g[:, q, 16:16 + T], in0=xres[xb][:, c0 + q, :], scalar=pcol(PC_GMIX1 + c0 + q),
                        in1=rstd[:, :], op0=ALU.mult, op1=ALU.mult),
                        reads=[XC[xb][c0 + q], RS, PAR], writes=[H1G])
                P.add("pool", lambda e, c0=c0: e.tensor_copy(out=h1hal[:, c0:c0 + 2, :], in_=h1g[:, :, T:T + 16]),
                      reads=[H1G], writes=[H1HAL[gi]])
                src, srcR = h1g, [H1G, H1GH]
                bufs = [(sA, SA_), (sB, SB_)]
                step, lo = 1, 0
                nb = 0
                while step < w:
                    dst, dstR = bufs[nb % 2]
                    nlo = lo + step
                    P.add("dve", lambda e, src=src, dst=dst, nlo=nlo, step=step: e.tensor_tensor(
                        out=dst[:, :, nlo:L_], in0=src[:, :, nlo:L_], in1=src[:, :, nlo - step:L_ - step], op=ALU.add),
                        reads=srcR, writes=[dstR])
                    src, srcR = dst, [dstR]
                    lo = nlo
                    step *= 2
                    nb += 1
                P.add("dve", lambda e, src=src, c0=c0, w=w: e.scalar_tensor_tensor(
                    out=yab[:, c0:c0 + 2, :], in0=src[:, :, 16:16 + T], scalar=1.0 / w, in1=h1g[:, :, 16:16 + T],
                    op0=ALU.mult, op1=ALU.subtract), reads=srcR + [H1G], writes=[YAB[c0], YAB[c0 + 1]])
                if first_tile:
                    for q in range(2):
                        P.add("dve", lambda e, src=src, q=q, gi=gi: e.tensor_tensor(
                            out=t16[:, q, :], in0=src[:, q, 16:32], in1=invc[:, gi, :], op=ALU.mult),
                            reads=srcR + [INVC], writes=[T16])
                    P.add("dve", lambda e, c0=c0: e.tensor_tensor(out=yab[:, c0:c0 + 2, 0:16], in0=t16[:, :, :],
                                                                  in1=h1g[:, :, 16:32], op=ALU.subtract),
                          reads=[T16, H1G], writes=[YAB[c0], YAB[c0 + 1]])
                for oc in range(2):
                    pairs = []
                    for k in range(2):
                        o = ((gi * 2 + k) * 256) + oc * 128
                        pairs.append((wpool[:, o:o + 128], yab[:, c0 + k, :]))
                    b = mm_group(pairs, [WPOOL, YAB[c0], YAB[c0 + 1]])
                    P.add("dve", lambda e, b=b, c=c0 + oc: e.scalar_tensor_tensor(
                        out=xres[xb][:, c, :], in0=ps[:, b, :], scalar=pcol(PC_PSCALE + c), in1=xres[xb][:, c, :],
                        op0=ALU.mult, op1=ALU.add), reads=[PSB[b], PAR, XC[xb][c0 + oc]], writes=[XC[xb][c0 + oc]])

        def emit_final(xb):
            emit_stats(xb)
            for c in range(KC):
                P.add("dve", lambda e, c=c: e.scalar_tensor_tensor(out=xres[xb][:, c, :], in0=xres[xb][:, c, :],
                                                                   scalar=pcol(PC_GFIN + c), in1=rstd[:, :],
                                                                   op0=ALU.mult, op1=ALU.mult),
                      reads=[XC[xb][c], RS, PAR], writes=[XC[xb][c]])

        stages = ["mix0", "ffn0", "mix1", "ffn1", "final"]
        nst = stages.index(stop_after) + 1
        store_ops = []

        def emit_store(t):
            xb_ = t % 2
            op = P.add("sp", lambda e: e.dma_start(out=out_l[t].rearrange("p (c t) -> p c t", c=KC),
                                                   in_=xres[xb_][:, :, :]), reads=XC[xb_], dma=f"st{xb_}")
            store_ops.append(op)

        def emit_xload(t):
            xb_ = t % 2
            P.add("sp", lambda e: e.dma_start(out=xres[xb_][:, :, :],
                                              in_=x_l[t].rearrange("p (c t) -> p c t", c=KC)),
                  writes=XC[xb_], dma=f"x{xb_}")

        for t in range(nt_run):
            xb = t % 2
            if nst >= 1:
                emit_mixer0(xb)
            if t >= 1:
                emit_store(t - 1)
            if nst >= 2:
                emit_ffn(xb, 0)
            if t + 1 < nt_run:
                emit_xload(t + 1)
            if nst >= 3:
                emit_mixer1(xb, t == 0)
            if nst >= 4:
                emit_ffn(xb, 1)
            if nst >= 5:
                emit_final(xb)
        emit_store(nt_run - 1)
        fence = P.add("sp", None)
        fence.deps = set(store_ops)

        sem_names = P.finalize()
        sems = {n: es.enter_context(nc.semaphore(n)) for n in sem_names}
        with nc.Block() as block:
            @block.tensor
            def _(e):
                P.emit("pe", e, sems)

            @block.scalar
            def _(e):
                P.emit("act", e, sems)

            @block.vector
            def _(e):
                P.emit("dve", e, sems)

            @block.gpsimd
            def _(e):
                P.emit("pool", e, sems)

            @block.sync
            def _(e):
                P.emit("sp", e, sems)
    return nc


def _blk(w, bc):
    K, N = w.shape
    nk = K // 128
    return np.ascontiguousarray(w.reshape(nk, 128, N // bc, bc).transpose(2, 1, 0, 3)).reshape(N // bc, 128, nk * bc)


def _col(v):
    return np.ascontiguousarray(v.reshape(-1, 128).T)


def prepare_inputs(x, norm_mix_even, w_in, conv_a, ln_a_g, ln_a_b, conv_b, w_out, norm_mix_odd, w_pool,
                   pool_scale, norm_ffn, w_up, conv_ffn_w, w_down, norm_final):
    f = np.float32
    par = np.zeros((128, NPAR), f)
    par[:, PC_GMIX0:PC_GMIX0 + 8] = _col(norm_mix_even[0])
    par[:, PC_GFFN0:PC_GFFN0 + 8] = _col(norm_ffn[0])
    par[:, PC_GMIX1:PC_GMIX1 + 8] = _col(norm_mix_odd[0])
    par[:, PC_GFFN1:PC_GFFN1 + 8] = _col(norm_ffn[1])
    par[:, PC_GFIN:PC_GFIN + 8] = _col(norm_final)
    par[:, PC_LNG:PC_LNG + 4] = _col(ln_a_g[0])
    par[:, PC_LNB:PC_LNB + 4] = _col(ln_a_b[0])
    par[:, PC_CONVB:PC_CONVB + 12] = conv_b[0].reshape(3, 4, 128).transpose(2, 1, 0).reshape(128, 12)
    par[:, PC_PSCALE:PC_PSCALE + 8] = _col(pool_scale[0])
    for L in range(2):
        c = conv_ffn_w[L].reshape(3, 44, 128).transpose(2, 1, 0).reshape(128, 132)
        par[:, PC_CFFN + L * 132:PC_CFFN + (L + 1) * 132] = c
    shared = {"par": par}
    shared["win32"] = _blk(w_in[0], 128)
    shared["wout32"] = _blk(w_out[0], 128)
    wcv = np.zeros((4, 2, 128, 16, 128), f)
    ca = conv_a[0].reshape(31, 4, 128)
    idx = np.arange(128)
    for i in range(31):
        for j in range(4):
            wcv[j, i // 16, idx, i % 16, idx] = ca[i, j]
    shared["wcv32"] = wcv.reshape(8, 128, 2048)
    for L in range(2):
        wu = w_up[L]
        g = wu[:, :DFF].reshape(D, NF, 128)
        v = wu[:, DFF:].reshape(D, NF, 128)
        gv = np.concatenate([g, v], axis=2).reshape(D, NF * 256)
        shared[f"wup{L}32"] = _blk(gv, 256)
        shared[f"wdn{L}32"] = _blk(w_down[L], 128)
    shared["wpool32"] = np.ascontiguousarray(w_pool[0].reshape(4, 2, 128, 256).transpose(2, 0, 1, 3)).reshape(1, 128, 2048)
    in_maps = []
    for b in range(x.shape[0]):
        xl = np.ascontiguousarray(x[b].reshape(NT, T, KC, 128).transpose(0, 3, 2, 1)).reshape(NT, 128, KC * T)
        m = dict(shared)
        m["x_l"] = xl
        in_maps.append(m)
    return in_maps


def assemble_output(res_list):
    outs = []
    for r in res_list:
        o = r["out_l"].reshape(NT, 128, KC, T).transpose(0, 3, 2, 1).reshape(S, D)
        outs.append(o)
    return np.ascontiguousarray(np.stack(outs, axis=0)).astype(np.float32)


def kernel(**inputs):
    inputs = {k: np.asarray(v, dtype=np.float32) for k, v in inputs.items()}
    in_maps = prepare_inputs(**inputs)
    nc = build_program()
    res = run_bass_kernel_spmd(nc, in_maps, core_ids=list(range(NCORE)))
    return assemble_output(res.results)
```

```python
import numpy as np
import concourse.bass as bass
import concourse.mybir as mybir
from concourse.bass_utils import run_bass_kernel_spmd

F32 = mybir.dt.float32
BF16 = mybir.dt.bfloat16
ALU = mybir.AluOpType
AF = mybir.ActivationFunctionType

D = 1024
S = 4096
NCORE = 8
DFF = 2816
NF = DFF // 128
T = 512
NT = S // T
KC = D // 128
RMS_EPS = 1e-6
LN_EPS = 1e-5
NSLOT = 5
RINGW = 2816
POOL_W = (2, 4, 8, 16)

PC_GMIX0, PC_GFFN0, PC_GMIX1, PC_GFFN1, PC_GFIN = 0, 8, 16, 24, 32
PC_LNG, PC_LNB = 40, 44
PC_CONVB = 48
PC_PSCALE = 60
PC_CFFN = 68
NPAR = PC_CFFN + 2 * 44 * 3


class Res:
    __slots__ = ("name", "w", "r")

    def __init__(self, name):
        self.name = name
        self.w = None
        self.r = []


class Op:
    __slots__ = ("eng", "idx", "fn", "deps", "sig", "sem", "val", "dma")


class Prog:
    ENGS = ("pe", "act", "dve", "pool", "sp")

    def __init__(self):
        self.ops = {e: [] for e in self.ENGS}
        self.dma_count = {}

    def add(self, eng, fn, reads=(), writes=(), dma=None):
        op = Op()
        op.eng = eng
        op.idx = len(self.ops[eng])
        op.fn = fn
        op.sig = False
        op.dma = dma
        op.sem = None
        op.val = 0
        deps = set()
        for r in reads:
            if r.w is not None:
                deps.add(r.w)
        for w in writes:
            if w.w is not None:
                deps.add(w.w)
            for rd in w.r:
                deps.add(rd)
        deps.discard(op)
        op.deps = deps
        for r in reads:
            r.r.append(op)
        for w in writes:
            w.w = op
            w.r = []
        self.ops[eng].append(op)
        return op

    def _needs_wait(self, op, dep):
        if dep.dma is not None:
            return True
        if dep.eng != op.eng:
            return True
        if op.eng == "pe":
            return False
        return (op.idx - dep.idx) <= 1

    def finalize(self):
        for e in self.ENGS:
            for op in self.ops[e]:
                for d in op.deps:
                    if self._needs_wait(op, d):
                        d.sig = True
        dma_keys = []
        for e in self.ENGS:
            cnt = 0
            for op in self.ops[e]:
                if op.dma is not None:
                    if op.dma not in self.dma_count:
                        self.dma_count[op.dma] = 0
                        dma_keys.append(op.dma)
                    self.dma_count[op.dma] += 16
                    op.sem = "dma_" + op.dma
                    op.val = self.dma_count[op.dma]
                elif op.sig:
                    cnt += 1
                    op.sem = "eng_" + e
                    op.val = cnt
        return ["eng_" + e for e in self.ENGS if e != "sp"] + ["dma_" + k for k in dma_keys]

    def emit(self, eng_name, engine, sems):
        seen = {}
        nwait = 0
        for op in self.ops[eng_name]:
            need = {}
            for d in op.deps:
                if not self._needs_wait(op, d):
                    continue
                if seen.get(d.sem, 0) >= d.val:
                    continue
                if need.get(d.sem, 0) < d.val:
                    need[d.sem] = d.val
            for sname, v in need.items():
                engine.wait_ge(sems[sname], v)
                seen[sname] = v
                nwait += 1
            if op.fn is None:
                continue
            ins = op.fn(engine)
            if op.dma is not None:
                ins.then_inc(sems[op.sem], 16)
            elif op.sig:
                ins.then_inc(sems[op.sem], 1)
        return nwait


def build_program(nt_run=NT, stop_after="final"):
    nc = bass.Bass("TRN2", target_bir_lowering=False)
    P = Prog()

    def dram(name, shape, dt, kind):
        return nc.dram_tensor(name, shape, dt, kind=kind).ap()

    x_l = dram("x_l", [NT, 128, KC * T], F32, "ExternalInput")
    par_d = dram("par", [128, NPAR], F32, "ExternalInput")
    w32 = {
        "win": dram("win32", [20, 128, 1024], F32, "ExternalInput"),
        "wcv": dram("wcv32", [8, 128, 2048], F32, "ExternalInput"),
        "wout": dram("wout32", [8, 128, 1024], F32, "ExternalInput"),
        "wup0": dram("wup032", [NF, 128, 2048], F32, "ExternalInput"),
        "wdn0": dram("wdn032", [8, 128, 2816], F32, "ExternalInput"),
        "wpool": dram("wpool32", [1, 128, 2048], F32, "ExternalInput"),
        "wup1": dram("wup132", [NF, 128, 2048], F32, "ExternalInput"),
        "wdn1": dram("wdn132", [8, 128, 2816], F32, "ExternalInput"),
    }
    wbf = {k: dram(k + "bf", list(v.shape), BF16, "Internal") for k, v in w32.items()}
    out_l = dram("out_l", [NT, 128, KC * T], F32, "ExternalOutput")

    WS = {k: Res("ws_" + k) for k in w32}

    from contextlib import ExitStack
    es = ExitStack()

    def sb(name, shape, dt):
        return es.enter_context(nc.sbuf_tensor(name, shape, dt))

    with es:
        xres = [sb(f"xres{i}", [128, KC, T], F32) for i in range(2)]
        hbf = sb("hbf", [128, KC, T], BF16)
        sqbf = sb("sqbf", [128, KC, T], BF16)
        actb = sb("actb", [128, NF, T], BF16)
        yab = sb("yab", [128, KC, T], BF16)
        NUB, NCB, NSG, NTMP = 2, 2, 2, 3
        ubuf = [sb(f"ubuf{i}", [128, 2, T + 2], F32) for i in range(NUB)]
        cbuf = [sb(f"cbuf{i}", [128, 2, T], F32) for i in range(NCB)]
        sgb = [sb(f"sgb{i}", [128, T], F32) for i in range(NSG)]
        tmp = [sb(f"tmp{i}", [128, T], F32) for i in range(NTMP)]
        ring = [sb(f"ring{i}", [128, RINGW], BF16) for i in range(NSLOT)]
        ssum = sb("ssum", [128, T], F32)
        rstd = sb("rstd", [128, T], F32)
        mean = sb("mean", [128, T], F32)
        var = sb("var", [128, T], F32)
        rstd2 = sb("rstd2", [128, T], F32)
        glu = sb("glu", [128, 4, T + 30], BF16)
        af32 = sb("af32", [128, 4, T], F32)
        abf = sb("abf", [128, 4, T], BF16)
        asq = sb("asq", [128, 4, T], BF16)
        cv = sb("cv", [128, 4, T + 2], F32)
        accb = sb("accb", [128, T], F32)
        h1g = sb("h1g", [128, 2, T + 16], F32)
        sA = sb("sA", [128, 2, T + 16], F32)
        sB = sb("sB", [128, 2, T + 16], F32)
        t16 = sb("t16", [128, 2, 16], F32)
        halF = sb("halF", [128, 2 * NF, 2, 2], F32)
        h1hal = sb("h1hal", [128, KC, 16], F32)
        par = sb("par_sb", [128, NPAR], F32)
        ones = sb("ones", [128, 128], BF16)
        epsc = sb("epsc", [128, 2], F32)
        invc = sb("invc", [128, 4, 16], F32)
        wpool = sb("wpool_sb", [128, 2048], BF16)
        ps = es.enter_context(nc.psum_tensor("ps", [128, 8, T], F32))

        XC = [[Res(f"x{b}_{c}") for c in range(KC)] for b in range(2)]
        HB = [Res(f"hbf{c}") for c in range(KC)]
        SQ = [Res("sq0"), Res("sq1")]
        ACTR = [Res(f"act{f}") for f in range(NF)]
        YAB = [Res(f"yab{c}") for c in range(KC)]
        UBG = [Res(f"ubg{i}") for i in range(NUB)]
        UBV = [Res(f"ubv{i}") for i in range(NUB)]
        UBH = [Res(f"ubh{i}") for i in range(NUB)]
        CBG = [Res(f"cbg{i}") for i in range(NCB)]
        CBV = [Res(f"cbv{i}") for i in range(NCB)]
        SG = [Res(f"sg{i}") for i in range(NSG)]
        TMP = [Res(f"tmp{i}") for i in range(NTMP)]
        RING = [Res(f"ring{i}") for i in range(NSLOT)]
        PSB = [Res(f"psb{i}") for i in range(8)]
        SS, RS, MEAN, VAR, RS2 = Res("ss"), Res("rs"), Res("mean"), Res("var"), Res("rs2")
        GLU = [Res(f"glu{j}") for j in range(4)]
        AFR = [Res(f"af{j}") for j in range(4)]
        ABF = [Res(f"abf{j}") for j in range(4)]
        ASQ = [Res(f"asq{j}") for j in range(4)]
        CV = [Res(f"cv{j}") for j in range(4)]
        ACCB = Res("accb")
        H1G, H1GH, SA_, SB_, T16 = Res("h1g"), Res("h1gh"), Res("sA"), Res("sB"), Res("t16")
        HALF = [Res(f"half{i}") for i in range(2 * NF)]
        H1HAL = [Res(f"h1hal{g}") for g in range(4)]
        PAR, ONES, MHALF, INVC, WPOOL = Res("par"), Res("ones"), Res("mhalf"), Res("invc"), Res("wpool")

        state = {"bank": 0, "ring": 0, "ub": 0, "cb": 0, "sg": 0, "tmp": 0}

        def rot(key, n):
            v = state[key] % n
            state[key] += 1
            return v

        def pcol(c):
            return par[:, c:c + 1]

        P.add("sp", lambda e: e.dma_start(out=par[:, :], in_=par_d[:, :]), writes=[PAR], dma="par")
        P.add("sp", lambda e: e.dma_start(out=xres[0][:, :, :], in_=x_l[0].rearrange("p (c t) -> p c t", c=KC)),
              writes=XC[0], dma="x0")
        P.add("dve", lambda e: e.memset(ones[:, :], 1.0), writes=[ONES])
        def eps_fill(e):
            e.memset(epsc[:, 0:1], RMS_EPS)
            return e.memset(epsc[:, 1:2], LN_EPS)
        P.add("dve", eps_fill, writes=[MHALF])
        P.add("dve", lambda e: e.memset(glu[:, :, 0:30], 0.0), writes=GLU)
        P.add("dve", lambda e: e.memset(cv[:, :, 0:2], 0.0), writes=CV)
        P.add("dve", lambda e: e.memset(halF[:, :, :, :], 0.0), writes=HALF)
        P.add("dve", lambda e: e.memset(h1hal[:, :, :], 0.0), writes=H1HAL)

        def invc_fill(e):
            ins = None
            for gi, w in enumerate(POOL_W):
                ins = e.memset(invc[:, gi, :], 1.0 / w)
                for j in range(w - 1):
                    ins = e.memset(invc[:, gi, j:j + 1], 1.0 / (j + 1))
            return ins
        P.add("pool", invc_fill, writes=[INVC])

        for k in ("win", "wcv", "wout", "wup0", "wdn0", "wpool", "wup1", "wdn1"):
            nblk = w32[k].shape[0]
            for b in range(nblk):
                P.add("pool", lambda e, k=k, b=b: e.dma_start(out=wbf[k][b], in_=w32[k][b]),
                      writes=[WS[k]], dma="cv_" + k)
        P.add("sp", lambda e: e.dma_start(out=wpool[:, :], in_=wbf["wpool"][0]), reads=[WS["wpool"]],
              writes=[WPOOL], dma="wpool")

        def ring_load(kind, blk, ncols):
            s = rot("ring", NSLOT)
            P.add("sp", lambda e: e.dma_start(out=ring[s][:, 0:ncols], in_=wbf[kind][blk]),
                  reads=[WS[kind]], writes=[RING[s]], dma=f"ring{s}")
            return s

        def mm_group(pairs, reads):
            b = rot("bank", 8)
            n = len(pairs)

            def fn(e):
                ins = None
                for i, (l, r) in enumerate(pairs):
                    ins = e.matmul(ps[:, b, :], lhsT=l, rhs=r, start=(i == 0), stop=(i == n - 1))
                return ins
            P.add("pe", fn, reads=reads, writes=[PSB[b]])
            return b

        def emit_stats(xb):
            for hh in range(2):
                P.add("act", lambda e, hh=hh: e.activation(out=sqbf[:, 4 * hh:4 * hh + 4, :],
                                                           in_=xres[xb][:, 4 * hh:4 * hh + 4, :], func=AF.Square),
                      reads=XC[xb][4 * hh:4 * hh + 4], writes=[SQ[hh]])
            b = mm_group([(ones[:, :], sqbf[:, c, :]) for c in range(KC)], [ONES] + SQ)
            P.add("act", lambda e: e.activation(out=ssum[:, :], in_=ps[:, b, :], func=AF.Sqrt, scale=1.0 / D,
                                                bias=epsc[:, 0:1]), reads=[PSB[b], MHALF], writes=[SS])
            P.add("dve", lambda e: e.reciprocal(out=rstd[:, :], in_=ssum[:, :]), reads=[SS], writes=[RS])

        def emit_norm_hbf(xb, gc):
            emit_stats(xb)
            for c in range(KC):
                P.add("dve", lambda e, c=c: e.scalar_tensor_tensor(out=hbf[:, c, :], in0=xres[xb][:, c, :],
                                                                   scalar=pcol(gc + c), in1=rstd[:, :],
                                                                   op0=ALU.mult, op1=ALU.mult),
                      reads=[XC[xb][c], RS, PAR], writes=[HB[c]])

        def resid_add(xb, d, b):
            P.add("dve", lambda e: e.tensor_tensor(out=xres[xb][:, d, :], in0=xres[xb][:, d, :], in1=ps[:, b, :],
                                                   op=ALU.add), reads=[PSB[b], XC[xb][d]], writes=[XC[xb][d]])

        def emit_mixer0(xb):
            emit_norm_hbf(xb, PC_GMIX0)

            def proj(blk):
                s = ring_load("win", blk, 1024)
                return mm_group([(ring[s][:, k * 128:(k + 1) * 128], hbf[:, k, :]) for k in range(KC)],
                                [RING[s]] + HB)
            for j in range(4):
                bgate = proj(4 + j)
                bval = proj(j)
                ta = rot("tmp", NTMP)
                P.add("act", lambda e, ta=ta, bgate=bgate: e.activation(out=tmp[ta][:, :], in_=ps[:, bgate, :],
                                                                        func=AF.Tanh, scale=0.5),
                      reads=[PSB[bgate]], writes=[TMP[ta]])
                tb = rot("tmp", NTMP)
                P.add("act", lambda e, tb=tb, bval=bval: e.activation(out=tmp[tb][:, :], in_=ps[:, bval, :],
                                                                      func=AF.Copy, scale=0.5),
                      reads=[PSB[bval]], writes=[TMP[tb]])
                P.add("dve", lambda e, ta=ta, tb=tb, j=j: e.scalar_tensor_tensor(
                    out=glu[:, j, 30:30 + T], in0=tmp[ta][:, :], scalar=1.0, in1=tmp[tb][:, :],
                    op0=ALU.add, op1=ALU.mult), reads=[TMP[ta], TMP[tb]], writes=[GLU[j]])
            for j in range(4):
                s0 = ring_load("wcv", 2 * j, 2048)
                s1 = ring_load("wcv", 2 * j + 1, 2048)
                pairs = []
                for i in range(31):
                    sl = s0 if i < 16 else s1
                    ii = i % 16
                    pairs.append((ring[sl][:, ii * 128:(ii + 1) * 128], glu[:, j, i:i + T]))
                b = mm_group(pairs, [RING[s0], RING[s1], GLU[j]])
                P.add("act", lambda e, j=j, b=b: e.activation(out=af32[:, j, :], in_=ps[:, b, :], func=AF.Copy),
                      reads=[PSB[b]], writes=[AFR[j]])
                P.add("act", lambda e, j=j, b=b: e.activation(out=asq[:, j, :], in_=ps[:, b, :], func=AF.Square),
                      reads=[PSB[b]], writes=[ASQ[j]])
                P.add("pool", lambda e, j=j: e.tensor_copy(out=abf[:, j, :], in_=af32[:, j, :]),
                      reads=[AFR[j]], writes=[ABF[j]])
                P.add("pool", lambda e, j=j: e.tensor_copy(out=glu[:, j, 0:30], in_=glu[:, j, T:T + 30]),
                      reads=[GLU[j]], writes=[GLU[j]])
            for j in range(4):
                bc = proj(12 + j)
                bbc = proj(16 + j)
                bb = proj(8 + j)
                tc = rot("tmp", NTMP)
                P.add("act", lambda e, tc=tc, bc=bc: e.activation(out=tmp[tc][:, :], in_=ps[:, bc, :], func=AF.Copy),
                      reads=[PSB[bc]], writes=[TMP[tc]])
                P.add("dve", lambda e, tc=tc, bbc=bbc, j=j: e.tensor_tensor(out=cv[:, j, 2:2 + T], in0=tmp[tc][:, :],
                                                                            in1=ps[:, bbc, :], op=ALU.mult),
                      reads=[TMP[tc], PSB[bbc]], writes=[CV[j]])
                wc = PC_CONVB + 3 * j
                P.add("dve", lambda e, j=j, wc=wc: e.tensor_scalar(out=accb[:, :], in0=cv[:, j, 2:2 + T],
                                                                   scalar1=pcol(wc + 2), scalar2=None, op0=ALU.mult),
                      reads=[CV[j], PAR], writes=[ACCB])
                P.add("dve", lambda e, j=j, wc=wc: e.scalar_tensor_tensor(out=accb[:, :], in0=cv[:, j, 1:1 + T],
                                                                          scalar=pcol(wc + 1), in1=accb[:, :],
                                                                          op0=ALU.mult, op1=ALU.add),
                      reads=[CV[j], PAR, ACCB], writes=[ACCB])
                P.add("dve", lambda e, j=j, wc=wc: e.scalar_tensor_tensor(out=accb[:, :], in0=cv[:, j, 0:T],
                                                                          scalar=pcol(wc), in1=accb[:, :],
                                                                          op0=ALU.mult, op1=ALU.add),
                      reads=[CV[j], PAR, ACCB], writes=[ACCB])
                P.add("dve", lambda e, j=j, bb=bb: e.tensor_tensor(out=yab[:, 4 + j, :], in0=accb[:, :],
                                                                   in1=ps[:, bb, :], op=ALU.mult),
                      reads=[ACCB, PSB[bb]], writes=[YAB[4 + j]])
                P.add("pool", lambda e, j=j: e.tensor_copy(out=cv[:, j, 0:2], in_=cv[:, j, T:T + 2]),
                      reads=[CV[j]], writes=[CV[j]])
            b1 = mm_group([(ones[:, :], abf[:, j, :]) for j in range(4)], [ONES] + ABF)
            b2 = mm_group([(ones[:, :], asq[:, j, :]) for j in range(4)], [ONES] + ASQ)
            P.add("dve", lambda e: e.tensor_scalar(out=mean[:, :], in0=ps[:, b1, :], scalar1=1.0 / 512, scalar2=None,
                                                   op0=ALU.mult), reads=[PSB[b1]], writes=[MEAN])
            tm = rot("tmp", NTMP)
            P.add("dve", lambda e: e.tensor_tensor(out=tmp[tm][:, :], in0=mean[:, :], in1=mean[:, :], op=ALU.mult),
                  reads=[MEAN], writes=[TMP[tm]])
            P.add("dve", lambda e: e.scalar_tensor_tensor(out=var[:, :], in0=ps[:, b2, :], scalar=1.0 / 512,
                                                          in1=tmp[tm][:, :], op0=ALU.mult, op1=ALU.subtract),
                  reads=[PSB[b2], TMP[tm]], writes=[VAR])
            P.add("act", lambda e: e.activation(out=ssum[:, :], in_=var[:, :], func=AF.Sqrt, scale=1.0,
                                                bias=epsc[:, 1:2]), reads=[VAR, MHALF], writes=[SS])
            P.add("dve", lambda e: e.reciprocal(out=rstd2[:, :], in_=ssum[:, :]), reads=[SS], writes=[RS2])
            for j in range(4):
                t1 = rot("tmp", NTMP)
                P.add("dve", lambda e, j=j, t1=t1: e.tensor_tensor(out=tmp[t1][:, :], in0=af32[:, j, :],
                                                                   in1=mean[:, :], op=ALU.subtract),
                      reads=[AFR[j], MEAN], writes=[TMP[t1]])
                t2 = rot("tmp", NTMP)
                P.add("dve", lambda e, t1=t1, t2=t2: e.tensor_tensor(out=tmp[t2][:, :], in0=tmp[t1][:, :],
                                                                     in1=rstd2[:, :], op=ALU.mult),
                      reads=[TMP[t1], RS2], writes=[TMP[t2]])
                P.add("act", lambda e, j=j, t2=t2: e.activation(out=yab[:, j, :], in_=tmp[t2][:, :], func=AF.Silu,
                                                                scale=pcol(PC_LNG + j), bias=pcol(PC_LNB + j)),
                      reads=[TMP[t2], PAR], writes=[YAB[j]])
            for d in range(KC):
                s = ring_load("wout", d, 1024)
                b = mm_group([(ring[s][:, k * 128:(k + 1) * 128], yab[:, k, :]) for k in range(KC)],
                             [RING[s]] + YAB)
                resid_add(xb, d, b)

        def emit_ffn(xb, L):
            emit_norm_hbf(xb, PC_GFFN0 if L == 0 else PC_GFFN1)
            kup = "wup0" if L == 0 else "wup1"
            kdn = "wdn0" if L == 0 else "wdn1"
            for f in range(NF):
                s = ring_load(kup, f, 2048)
                bg = mm_group([(ring[s][:, k * 256:k * 256 + 128], hbf[:, k, :]) for k in range(KC)], [RING[s]] + HB)
                bv = mm_group([(ring[s][:, k * 256 + 128:k * 256 + 256], hbf[:, k, :]) for k in range(KC)],
                              [RING[s]] + HB)
                ui = rot("ub", NUB)
                ci = rot("cb", NCB)
                si = rot("sg", NSG)
                hi = L * NF + f
                P.add("pool", lambda e, ui=ui, hi=hi: e.tensor_copy(out=ubuf[ui][:, :, 0:2], in_=halF[:, hi, :, :]),
                      reads=[HALF[hi]], writes=[UBH[ui]])
                P.add("act", lambda e, ui=ui, bg=bg: e.activation(out=ubuf[ui][:, 0, 2:2 + T], in_=ps[:, bg, :],
                                                                  func=AF.Copy), reads=[PSB[bg]], writes=[UBG[ui]])
                P.add("act", lambda e, ui=ui, bv=bv: e.activation(out=ubuf[ui][:, 1, 2:2 + T], in_=ps[:, bv, :],
                                                                  func=AF.Copy), reads=[PSB[bv]], writes=[UBV[ui]])
                P.add("pool", lambda e, ui=ui, hi=hi: e.tensor_copy(out=halF[:, hi, :, :], in_=ubuf[ui][:, :, T:T + 2]),
                      reads=[UBG[ui], UBV[ui]], writes=[HALF[hi]])
                wg = PC_CFFN + (L * 44 + f) * 3
                wv = PC_CFFN + (L * 44 + NF + f) * 3
                for tap in (2, 1, 0):
                    for q, wq, R, CR in ((0, wg, UBG, CBG), (1, wv, UBV, CBV)):
                        if tap == 2:
                            P.add("dve", lambda e, ui=ui, ci=ci, q=q, wq=wq: e.tensor_scalar(
                                out=cbuf[ci][:, q, :], in0=ubuf[ui][:, q, 2:2 + T], scalar1=pcol(wq + 2),
                                scalar2=None, op0=ALU.mult), reads=[R[ui], PAR], writes=[CR[ci]])
                        else:
                            P.add("dve", lambda e, ui=ui, ci=ci, q=q, wq=wq, tap=tap: e.scalar_tensor_tensor(
                                out=cbuf[ci][:, q, :], in0=ubuf[ui][:, q, tap:tap + T], scalar=pcol(wq + tap),
                                in1=cbuf[ci][:, q, :], op0=ALU.mult, op1=ALU.add),
                                reads=[R[ui], UBH[ui], PAR, CR[ci]], writes=[CR[ci]])
                P.add("act", lambda e, ci=ci, si=si: e.activation(out=sgb[si][:, :], in_=cbuf[ci][:, 0, :],
                                                                  func=AF.Silu), reads=[CBG[ci]], writes=[SG[si]])
                P.add("pool", lambda e, ci=ci, si=si, f=f: e.tensor_tensor(out=actb[:, f, :], in0=sgb[si][:, :],
                                                                           in1=cbuf[ci][:, 1, :], op=ALU.mult),
                      reads=[SG[si], CBV[ci]], writes=[ACTR[f]])
            for d in range(KC):
                s = ring_load(kdn, d, 2816)
                b = mm_group([(ring[s][:, f * 128:(f + 1) * 128], actb[:, f, :]) for f in range(NF)],
                             [RING[s]] + ACTR)
                resid_add(xb, d, b)

        def emit_mixer1(xb, first_tile):
            emit_stats(xb)
            L_ = T + 16
            for gi, w in enumerate(POOL_W):
                c0 = 2 * gi
                P.add("pool", lambda e, c0=c0: e.tensor_copy(out=h1g[:, :, 0:16], in_=h1hal[:, c0:c0 + 2, :]),
                      reads=[H1HAL[gi]], writes=[H1GH])
                for q in range(2):
                    P.add("dve", lambda e, q=q, c0=c0: e.scalar_tensor_tensor(
                        out=h1g[:, q, 16:16 + T], in0=xres[xb][:, c0 + q, :], scalar=pcol(PC_GMIX1 + c0 + q),
                        in1=rstd[:, :], op0=ALU.mult, op1=ALU.mult),
                        reads=[XC[xb][c0 + q], RS, PAR], writes=[H1G])
                P.add("pool", lambda e, c0=c0: e.tensor_copy(out=h1hal[:, c0:c0 + 2, :], in_=h1g[:, :, T:T + 16]),
                      reads=[H1G], writes=[H1HAL[gi]])
                src, srcR = h1g, [H1G, H1GH]
                bufs = [(sA, SA_), (sB, SB_)]
                step, lo = 1, 0
                nb = 0
                while step < w:
                    dst, dstR = bufs[nb % 2]
                    nlo = lo + step
                    P.add("dve", lambda e, src=src, dst=dst, nlo=nlo, step=step: e.tensor_tensor(
                        out=dst[:, :, nlo:L_], in0=src[:, :, nlo:L_], in1=src[:, :, nlo - step:L_ - step], op=ALU.add),
                        reads=srcR, writes=[dstR])
                    src, srcR = dst, [dstR]
                    lo = nlo
                    step *= 2
                    nb += 1
                P.add("dve", lambda e, src=src, c0=c0, w=w: e.scalar_tensor_tensor(
                    out=yab[:, c0:c0 + 2, :], in0=src[:, :, 16:16 + T], scalar=1.0 / w, in1=h1g[:, :, 16:16 + T],
                    op0=ALU.mult, op1=ALU.subtract), reads=srcR + [H1G], writes=[YAB[c0], YAB[c0 + 1]])
                if first_tile:
                    for q in range(2):
                        P.add("dve", lambda e, src=src, q=q, gi=gi: e.tensor_tensor(
                            out=t16[:, q, :], in0=src[:, q, 16:32], in1=invc[:, gi, :], op=ALU.mult),
                            reads=srcR + [INVC], writes=[T16])
                    P.add("dve", lambda e, c0=c0: e.tensor_tensor(out=yab[:, c0:c0 + 2, 0:16], in0=t16[:, :, :],
                                                                  in1=h1g[:, :, 16:32], op=ALU.subtract),
                          reads=[T16, H1G], writes=[YAB[c0], YAB[c0 + 1]])
                for oc in range(2):
                    pairs = []
                    for k in range(2):
                        o = ((gi * 2 + k) * 256) + oc * 128
                        pairs.append((wpool[:, o:o + 128], yab[:, c0 + k, :]))
                    b = mm_group(pairs, [WPOOL, YAB[c0], YAB[c0 + 1]])
                    P.add("dve", lambda e, b=b, c=c0 + oc: e.scalar_tensor_tensor(
                        out=xres[xb][:, c, :], in0=ps[:, b, :], scalar=pcol(PC_PSCALE + c), in1=xres[xb][:, c, :],
                        op0=ALU.mult, op1=ALU.add), reads=[PSB[b], PAR, XC[xb][c0 + oc]], writes=[XC[xb][c0 + oc]])

        def emit_final(xb):
            emit_stats(xb)
            for c in range(KC):
                P.add("dve", lambda e, c=c: e.scalar_tensor_tensor(out=xres[xb][:, c, :], in0=xres[xb][:, c, :],
                                                                   scalar=pcol(PC_GFIN + c), in1=rstd[:, :],
                                                                   op0=ALU.mult, op1=ALU.mult),
                      reads=[XC[xb][c], RS, PAR], writes=[XC[xb][c]])

        stages = ["mix0", "ffn0", "mix1", "ffn1", "final"]
        nst = stages.index(stop_after) + 1
        store_ops = []

        def emit_store(t):
            xb_ = t % 2
            op = P.add("sp", lambda e: e.dma_start(out=out_l[t].rearrange("p (c t) -> p c t", c=KC),
                                                   in_=xres[xb_][:, :, :]), reads=XC[xb_], dma=f"st{xb_}")
            store_ops.append(op)

        def emit_xload(t):
            xb_ = t % 2
            P.add("sp", lambda e: e.dma_start(out=xres[xb_][:, :, :],
                                              in_=x_l[t].rearrange("p (c t) -> p c t", c=KC)),
                  writes=XC[xb_], dma=f"x{xb_}")

        for t in range(nt_run):
            xb = t % 2
            if nst >= 1:
                emit_mixer0(xb)
            if t >= 1:
                emit_store(t - 1)
            if nst >= 2:
                emit_ffn(xb, 0)
            if t + 1 < nt_run:
                emit_xload(t + 1)
            if nst >= 3:
                emit_mixer1(xb, t == 0)
            if nst >= 4:
                emit_ffn(xb, 1)
            if nst >= 5:
                emit_final(xb)
        emit_store(nt_run - 1)
        fence = P.add("sp", None)
        fence.deps = set(store_ops)

        sem_names = P.finalize()
        sems = {n: es.enter_context(nc.semaphore(n)) for n in sem_names}
        with nc.Block() as block:
            @block.tensor
            def _(e):
                P.emit("pe", e, sems)

            @block.scalar
            def _(e):
                P.emit("act", e, sems)

            @block.vector
            def _(e):
                P.emit("dve", e, sems)

            @block.gpsimd
            def _(e):
                P.emit("pool", e, sems)

            @block.sync
            def _(e):
                P.emit("sp", e, sems)
    return nc


def _blk(w, bc):
    K, N = w.shape
    nk = K // 128
    return np.ascontiguousarray(w.reshape(nk, 128, N // bc, bc).transpose(2, 1, 0, 3)).reshape(N // bc, 128, nk * bc)


def _col(v):
    return np.ascontiguousarray(v.reshape(-1, 128).T)


def prepare_inputs(x, norm_mix_even, w_in, conv_a, ln_a_g, ln_a_b, conv_b, w_out, norm_mix_odd, w_pool,
                   pool_scale, norm_ffn, w_up, conv_ffn_w, w_down, norm_final):
    f = np.float32
    par = np.zeros((128, NPAR), f)
    par[:, PC_GMIX0:PC_GMIX0 + 8] = _col(norm_mix_even[0])
    par[:, PC_GFFN0:PC_GFFN0 + 8] = _col(norm_ffn[0])
    par[:, PC_GMIX1:PC_GMIX1 + 8] = _col(norm_mix_odd[0])
    par[:, PC_GFFN1:PC_GFFN1 + 8] = _col(norm_ffn[1])
    par[:, PC_GFIN:PC_GFIN + 8] = _col(norm_final)
    par[:, PC_LNG:PC_LNG + 4] = _col(ln_a_g[0])
    par[:, PC_LNB:PC_LNB + 4] = _col(ln_a_b[0])
    par[:, PC_CONVB:PC_CONVB + 12] = conv_b[0].reshape(3, 4, 128).transpose(2, 1, 0).reshape(128, 12)
    par[:, PC_PSCALE:PC_PSCALE + 8] = _col(pool_scale[0])
    for L in range(2):
        c = conv_ffn_w[L].reshape(3, 44, 128).transpose(2, 1, 0).reshape(128, 132)
        par[:, PC_CFFN + L * 132:PC_CFFN + (L + 1) * 132] = c
    shared = {"par": par}
    shared["win32"] = _blk(w_in[0], 128)
    shared["wout32"] = _blk(w_out[0], 128)
    wcv = np.zeros((4, 2, 128, 16, 128), f)
    ca = conv_a[0].reshape(31, 4, 128)
    idx = np.arange(128)
    for i in range(31):
        for j in range(4):
            wcv[j, i // 16, idx, i % 16, idx] = ca[i, j]
    shared["wcv32"] = wcv.reshape(8, 128, 2048)
    for L in range(2):
        wu = w_up[L]
        g = wu[:, :DFF].reshape(D, NF, 128)
        v = wu[:, DFF:].reshape(D, NF, 128)
        gv = np.concatenate([g, v], axis=2).reshape(D, NF * 256)
        shared[f"wup{L}32"] = _blk(gv, 256)
        shared[f"wdn{L}32"] = _blk(w_down[L], 128)
    shared["wpool32"] = np.ascontiguousarray(w_pool[0].reshape(4, 2, 128, 256).transpose(2, 0, 1, 3)).reshape(1, 128, 2048)
    in_maps = []
    for b in range(x.shape[0]):
        xl = np.ascontiguousarray(x[b].reshape(NT, T, KC, 128).transpose(0, 3, 2, 1)).reshape(NT, 128, KC * T)
        m = dict(shared)
        m["x_l"] = xl
        in_maps.append(m)
    return in_maps


def assemble_output(res_list):
    outs = []
    for r in res_list:
        o = r["out_l"].reshape(NT, 128, KC, T).transpose(0, 3, 2, 1).reshape(S, D)
        outs.append(o)
    return np.ascontiguousarray(np.stack(outs, axis=0)).astype(np.float32)


def kernel(**inputs):
    inputs = {k: np.asarray(v, dtype=np.float32) for k, v in inputs.items()}
    in_maps = prepare_inputs(**inputs)
    nc = build_program()
    res = run_bass_kernel_spmd(nc, in_maps, core_ids=list(range(NCORE)))
    return assemble_output(res.results)
```

```python
import numpy as np
import concourse.bass as bass
import concourse.mybir as mybir
from concourse.bass_utils import run_bass_kernel_spmd

F32 = mybir.dt.float32
BF16 = mybir.dt.bfloat16
ALU = mybir.AluOpType
AF = mybir.ActivationFunctionType

D = 1024
S = 4096
NCORE = 8
DFF = 2816
NF = DFF // 128
T = 512
NT = S // T
KC = D // 128
RMS_EPS = 1e-6
LN_EPS = 1e-5
NSLOT = 3
NRA = 4
RINGW = 2816
POOL_W = (2, 4, 8, 16)

PC_GMIX0, PC_GFFN0, PC_GMIX1, PC_GFFN1, PC_GFIN = 0, 8, 16, 24, 32
PC_LNG, PC_LNB = 40, 44
PC_CONVB = 48
PC_PSCALE = 60
PC_CFFN = 68
NPAR = PC_CFFN + 2 * 44 * 3


class Res:
    __slots__ = ("name", "w", "r")

    def __init__(self, name):
        self.name = name
        self.w = None
        self.r = []


class Op:
    __slots__ = ("eng", "idx", "fn", "deps", "sig", "sem", "val", "dma")


class Prog:
    ENGS = ("pe", "act", "dve", "pool", "sp")

    def __init__(self):
        self.ops = {e: [] for e in self.ENGS}
        self.dma_count = {}

    def add(self, eng, fn, reads=(), writes=(), dma=None):
        op = Op()
        op.eng = eng
        op.idx = len(self.ops[eng])
        op.fn = fn
        op.sig = False
        op.dma = dma
        op.sem = None
        op.val = 0
        deps = set()
        for r in reads:
            if r.w is not None:
                deps.add(r.w)
        for w in writes:
            if w.w is not None:
                deps.add(w.w)
            for rd in w.r:
                deps.add(rd)
        deps.discard(op)
        op.deps = deps
        for r in reads:
            r.r.append(op)
        for w in writes:
            w.w = op
            w.r = []
        self.ops[eng].append(op)
        return op

    def _needs_wait(self, op, dep):
        if dep.dma is not None:
            return True
        if dep.eng != op.eng:
            return True
        if op.eng == "pe":
            return False
        return (op.idx - dep.idx) <= 1

    def finalize(self):
        for e in self.ENGS:
            for op in self.ops[e]:
                for d in op.deps:
                    if self._needs_wait(op, d):
                        d.sig = True
        dma_keys = []
        for e in self.ENGS:
            cnt = 0
            for op in self.ops[e]:
                if op.dma is not None:
                    if op.dma not in self.dma_count:
                        self.dma_count[op.dma] = 0
                        dma_keys.append(op.dma)
                    self.dma_count[op.dma] += 16
                    op.sem = "dma_" + op.dma
                    op.val = self.dma_count[op.dma]
                elif op.sig:
                    cnt += 1
                    op.sem = "eng_" + e
                    op.val = cnt
        return ["eng_" + e for e in self.ENGS if e != "sp"] + ["dma_" + k for k in dma_keys]

    def emit(self, eng_name, engine, sems):
        seen = {}
        nwait = 0
        for op in self.ops[eng_name]:
            need = {}
            for d in op.deps:
                if not self._needs_wait(op, d):
                    continue
                if seen.get(d.sem, 0) >= d.val:
                    continue
                if need.get(d.sem, 0) < d.val:
                    need[d.sem] = d.val
            for sname, v in need.items():
                engine.wait_ge(sems[sname], v)
                seen[sname] = v
                nwait += 1
            if op.fn is None:
                continue
            ins = op.fn(engine)
            if op.dma is not None:
                ins.then_inc(sems[op.sem], 16)
            elif op.sig:
                ins.then_inc(sems[op.sem], 1)
        return nwait


def build_program(nt_run=NT, stop_after="final"):
    nc = bass.Bass("TRN2", target_bir_lowering=False)
    P = Prog()

    def dram(name, shape, dt, kind):
        return nc.dram_tensor(name, shape, dt, kind=kind).ap()

    x_l = dram("x_l", [NT, 128, KC * T], F32, "ExternalInput")
    par_d = dram("par", [128, NPAR], F32, "ExternalInput")
    w32 = {
        "win": dram("win32", [20, 128, 1024], F32, "ExternalInput"),
        "wcv": dram("wcv32", [8, 128, 2048], F32, "ExternalInput"),
        "wout": dram("wout32", [8, 128, 1024], F32, "ExternalInput"),
        "wup0": dram("wup032", [NF, 128, 2048], F32, "ExternalInput"),
        "wdn0": dram("wdn032", [8, 128, 2816], F32, "ExternalInput"),
        "wpool": dram("wpool32", [1, 128, 2048], F32, "ExternalInput"),
        "wup1": dram("wup132", [NF, 128, 2048], F32, "ExternalInput"),
        "wdn1": dram("wdn132", [8, 128, 2816], F32, "ExternalInput"),
    }
    wbf = {k: dram(k + "bf", list(v.shape), BF16, "Internal") for k, v in w32.items()}
    out_l = dram("out_l", [NT, 128, KC * T], F32, "ExternalOutput")


    from contextlib import ExitStack
    es = ExitStack()

    def sb(name, shape, dt):
        return es.enter_context(nc.sbuf_tensor(name, shape, dt))

    with es:
        xres = [sb(f"xres{i}", [128, KC, T], F32) for i in range(2)]
        hbf = sb("hbf", [128, KC, T], BF16)
        sqbf = sb("sqbf", [128, KC, T], BF16)
        actb = sb("actb", [128, NF, T], BF16)
        yab = sb("yab", [128, KC, T], BF16)
        NUB, NCB, NSG, NTMP = 2, 2, 2, 3
        ubuf = [sb(f"ubuf{i}", [128, 2, T + 2], F32) for i in range(NUB)]
        cbuf = [sb(f"cbuf{i}", [128, 2, T], F32) for i in range(NCB)]
        sgb = [sb(f"sgb{i}", [128, T], F32) for i in range(NSG)]
        tmp = [sb(f"tmp{i}", [128, T], F32) for i in range(NTMP)]
        ring = [sb(f"ring{i}", [128, RINGW], BF16) for i in range(NSLOT)]
        ringA = [sb(f"ringA{i}", [128, RINGW], BF16) for i in range(NRA)]
        ssum = sb("ssum", [128, T], F32)
        rstd = sb("rstd", [128, T], F32)
        mean = sb("mean", [128, T], F32)
        var = sb("var", [128, T], F32)
        rstd2 = sb("rstd2", [128, T], F32)
        glu = sb("glu", [128, 4, T + 30], BF16)
        af32 = sb("af32", [128, 4, T], F32)
        abf = sb("abf", [128, 4, T], BF16)
        asq = sb("asq", [128, 4, T], BF16)
        cv = sb("cv", [128, 4, T + 2], F32)
        accb = sb("accb", [128, T], F32)
        h1g = sb("h1g", [128, 2, T + 16], F32)
        sA = sb("sA", [128, 2, T + 16], F32)
        sB = sb("sB", [128, 2, T + 16], F32)
        t16 = sb("t16", [128, 2, 16], F32)
        halF = sb("halF", [128, 2 * NF, 2, 2], F32)
        h1hal = sb("h1hal", [128, KC, 16], F32)
        par = sb("par_sb", [128, NPAR], F32)
        ones = sb("ones", [128, 128], BF16)
        epsc = sb("epsc", [128, 2], F32)
        invc = sb("invc", [128, 4, 16], F32)
        wpool = sb("wpool_sb", [128, 2048], BF16)
        ps = es.enter_context(nc.psum_tensor("ps", [128, 8, T], F32))

        XC = [[Res(f"x{b}_{c}") for c in range(KC)] for b in range(2)]
        HB = [Res(f"hbf{c}") for c in range(KC)]
        SQ = [Res("sq0"), Res("sq1")]
        ACTR = [Res(f"act{f}") for f in range(NF)]
        YAB = [Res(f"yab{c}") for c in range(KC)]
        UBG = [Res(f"ubg{i}") for i in range(NUB)]
        UBV = [Res(f"ubv{i}") for i in range(NUB)]
        UBH = [Res(f"ubh{i}") for i in range(NUB)]
        CBG = [Res(f"cbg{i}") for i in range(NCB)]
        CBV = [Res(f"cbv{i}") for i in range(NCB)]
        SG = [Res(f"sg{i}") for i in range(NSG)]
        TMP = [Res(f"tmp{i}") for i in range(NTMP)]
        RING = [Res(f"ring{i}") for i in range(NSLOT)]
        RINGA = [Res(f"ringA{i}") for i in range(NRA)]
        PSB = [Res(f"psb{i}") for i in range(8)]
        SS, RS, MEAN, VAR, RS2 = Res("ss"), Res("rs"), Res("mean"), Res("var"), Res("rs2")
        GLU = [Res(f"glu{j}") for j in range(4)]
        AFR = [Res(f"af{j}") for j in range(4)]
        ABF = [Res(f"abf{j}") for j in range(4)]
        ASQ = [Res(f"asq{j}") for j in range(4)]
        CV = [Res(f"cv{j}") for j in range(4)]
        ACCB = Res("accb")
        H1G, H1GH, SA_, SB_, T16 = Res("h1g"), Res("h1gh"), Res("sA"), Res("sB"), Res("t16")
        HALF = [Res(f"half{i}") for i in range(2 * NF)]
        H1HAL = [Res(f"h1hal{g}") for g in range(4)]
        PAR, ONES, MHALF, INVC, WPOOL = Res("par"), Res("ones"), Res("mhalf"), Res("invc"), Res("wpool")

        state = {"bank": 0, "bank4": 0, "ring": 0, "ub": 0, "cb": 0, "sg": 0, "tmp": 0}

        def rot(key, n):
            v = state[key] % n
            state[key] += 1
            return v

        def pcol(c):
            return par[:, c:c + 1]

        P.add("sp", lambda e: e.dma_start(out=par[:, :], in_=par_d[:, :]), writes=[PAR], dma="par")
        P.add("sp", lambda e: e.dma_start(out=xres[0][:, :, :], in_=x_l[0].rearrange("p (c t) -> p c t", c=KC)),
              writes=XC[0], dma="x0")
        P.add("dve", lambda e: e.memset(ones[:, :], 1.0), writes=[ONES])
        def eps_fill(e):
            e.memset(epsc[:, 0:1], RMS_EPS)
            return e.memset(epsc[:, 1:2], LN_EPS)
        P.add("dve", eps_fill, writes=[MHALF])
        P.add("dve", lambda e: e.memset(glu[:, :, 0:30], 0.0), writes=GLU)
        P.add("dve", lambda e: e.memset(cv[:, :, 0:2], 0.0), writes=CV)
        P.add("dve", lambda e: e.memset(halF[:, :, :, :], 0.0), writes=HALF)
        P.add("dve", lambda e: e.memset(h1hal[:, :, :], 0.0), writes=H1HAL)

        def invc_fill(e):
            ins = None
            for gi, w in enumerate(POOL_W):
                ins = e.memset(invc[:, gi, :], 1.0 / w)
                for j in range(w - 1):
                    ins = e.memset(invc[:, gi, j:j + 1], 1.0 / (j + 1))
            return ins
        P.add("pool", invc_fill, writes=[INVC])

        CVG = {
            "win_a": ("win", range(0, 8)), "win_b": ("win", range(8, 20)), "wcv": ("wcv", range(8)),
            "wout": ("wout", range(8)), "wup0a": ("wup0", range(0, 11)), "wup0b": ("wup0", range(11, NF)),
            "wdn0": ("wdn0", range(8)), "wpool": ("wpool", range(1)), "wup1a": ("wup1", range(0, 11)),
            "wup1b": ("wup1", range(11, NF)), "wdn1": ("wdn1", range(8)),
        }
        GRP = {}
        for g, (k, blks) in CVG.items():
            for b in blks:
                GRP[(k, b)] = g
        WSG = {g: Res("wsg_" + g) for g in CVG}

        def emit_conv(groups):
            for g in groups:
                k, blks = CVG[g]
                last = None
                for b in blks:
                    last = P.add("pool", lambda e, k=k, b=b: e.dma_start(out=wbf[k][b], in_=w32[k][b]),
                                 dma="cv_" + g)
                WSG[g].w = last
                WSG[g].r = []

        emit_conv(["win_a", "win_b", "wcv", "wout", "wup0a"])

        def ring_load(kind, blk, ncols):
            s = rot("ring", NSLOT)
            P.add("sp", lambda e: e.dma_start(out=ring[s][:, 0:ncols], in_=wbf[kind][blk]),
                  reads=[WSG[GRP[(kind, blk)]]], writes=[RING[s]], dma=f"ring{s}")
            return s

        def pe_op(items, reads, banks):
            def fn(e):
                ins = None
                for (b, l, r, st, sp_) in items:
                    ins = e.matmul(ps[:, b, :], lhsT=l, rhs=r, start=st, stop=sp_)
                return ins
            P.add("pe", fn, reads=reads, writes=[PSB[b] for b in banks])

        def mm_group(pairs, reads, nbank=8):
            b = rot("bank", 8) if nbank == 8 else rot("bank4", 4)
            n = len(pairs)
            pe_op([(b, l, r, i == 0, i == n - 1) for i, (l, r) in enumerate(pairs)], reads, [b])
            return b

        def emit_stats(xb):
            for hh in range(2):
                P.add("act", lambda e, hh=hh: e.activation(out=sqbf[:, 4 * hh:4 * hh + 4, :],
                                                           in_=xres[xb][:, 4 * hh:4 * hh + 4, :], func=AF.Square),
                      reads=XC[xb][4 * hh:4 * hh + 4], writes=[SQ[hh]])
            b = mm_group([(ones[:, :], sqbf[:, c, :]) for c in range(KC)], [ONES] + SQ)
            P.add("act", lambda e: e.activation(out=ssum[:, :], in_=ps[:, b, :], func=AF.Sqrt, scale=1.0 / D,
                                                bias=epsc[:, 0:1]), reads=[PSB[b], MHALF], writes=[SS])
            P.add("dve", lambda e: e.reciprocal(out=rstd[:, :], in_=ssum[:, :]), reads=[SS], writes=[RS])

        def emit_norm_hbf(xb, gc):
            emit_stats(xb)
            for c in range(KC):
                P.add("dve", lambda e, c=c: e.scalar_tensor_tensor(out=hbf[:, c, :], in0=xres[xb][:, c, :],
                                                                   scalar=pcol(gc + c), in1=rstd[:, :],
                                                                   op0=ALU.mult, op1=ALU.mult),
                      reads=[XC[xb][c], RS, PAR], writes=[HB[c]])

        def resid_add(xb, d, b):
            P.add("dve", lambda e: e.tensor_tensor(out=xres[xb][:, d, :], in0=xres[xb][:, d, :], in1=ps[:, b, :],
                                                   op=ALU.add), reads=[PSB[b], XC[xb][d]], writes=[XC[xb][d]])

        def emit_mixer0(xb):
            emit_norm_hbf(xb, PC_GMIX0)

            def proj(blk):
                s = ring_load("win", blk, 1024)
                return mm_group([(ring[s][:, k * 128:(k + 1) * 128], hbf[:, k, :]) for k in range(KC)],
                                [RING[s]] + HB)
            for j in range(4):
                bgate = proj(4 + j)
                bval = proj(j)
                ta = rot("tmp", NTMP)
                P.add("act", lambda e, ta=ta, bgate=bgate: e.activation(out=tmp[ta][:, :], in_=ps[:, bgate, :],
                                                                        func=AF.Tanh, scale=0.5),
                      reads=[PSB[bgate]], writes=[TMP[ta]])
                tb = rot("tmp", NTMP)
                P.add("act", lambda e, tb=tb, bval=bval: e.activation(out=tmp[tb][:, :], in_=ps[:, bval, :],
                                                                      func=AF.Copy, scale=0.5),
                      reads=[PSB[bval]], writes=[TMP[tb]])
                P.add("dve", lambda e, ta=ta, tb=tb, j=j: e.scalar_tensor_tensor(
                    out=glu[:, j, 30:30 + T], in0=tmp[ta][:, :], scalar=1.0, in1=tmp[tb][:, :],
                    op0=ALU.add, op1=ALU.mult), reads=[TMP[ta], TMP[tb]], writes=[GLU[j]])
            for j in range(4):
                s0 = ring_load("wcv", 2 * j, 2048)
                s1 = ring_load("wcv", 2 * j + 1, 2048)
                pairs = []
                for i in range(31):
                    sl = s0 if i < 16 else s1
                    ii = i % 16
                    pairs.append((ring[sl][:, ii * 128:(ii + 1) * 128], glu[:, j, i:i + T]))
                b = mm_group(pairs, [RING[s0], RING[s1], GLU[j]])
                P.add("act", lambda e, j=j, b=b: e.activation(out=af32[:, j, :], in_=ps[:, b, :], func=AF.Copy),
                      reads=[PSB[b]], writes=[AFR[j]])
                P.add("act", lambda e, j=j, b=b: e.activation(out=asq[:, j, :], in_=ps[:, b, :], func=AF.Square),
                      reads=[PSB[b]], writes=[ASQ[j]])
                P.add("pool", lambda e, j=j: e.tensor_copy(out=abf[:, j, :], in_=af32[:, j, :]),
                      reads=[AFR[j]], writes=[ABF[j]])
                P.add("pool", lambda e, j=j: e.tensor_copy(out=glu[:, j, 0:30], in_=glu[:, j, T:T + 30]),
                      reads=[GLU[j]], writes=[GLU[j]])
            for j in range(4):
                bc = proj(12 + j)
                bbc = proj(16 + j)
                bb = proj(8 + j)
                tc = rot("tmp", NTMP)
                P.add("act", lambda e, tc=tc, bc=bc: e.activation(out=tmp[tc][:, :], in_=ps[:, bc, :], func=AF.Copy),
                      reads=[PSB[bc]], writes=[TMP[tc]])
                P.add("dve", lambda e, tc=tc, bbc=bbc, j=j: e.tensor_tensor(out=cv[:, j, 2:2 + T], in0=tmp[tc][:, :],
                                                                            in1=ps[:, bbc, :], op=ALU.mult),
                      reads=[TMP[tc], PSB[bbc]], writes=[CV[j]])
                wc = PC_CONVB + 3 * j
                P.add("dve", lambda e, j=j, wc=wc: e.tensor_scalar(out=accb[:, :], in0=cv[:, j, 2:2 + T],
                                                                   scalar1=pcol(wc + 2), scalar2=None, op0=ALU.mult),
                      reads=[CV[j], PAR], writes=[ACCB])
                P.add("dve", lambda e, j=j, wc=wc: e.scalar_tensor_tensor(out=accb[:, :], in0=cv[:, j, 1:1 + T],
                                                                          scalar=pcol(wc + 1), in1=accb[:, :],
                                                                          op0=ALU.mult, op1=ALU.add),
                      reads=[CV[j], PAR, ACCB], writes=[ACCB])
                P.add("dve", lambda e, j=j, wc=wc: e.scalar_tensor_tensor(out=accb[:, :], in0=cv[:, j, 0:T],
                                                                          scalar=pcol(wc), in1=accb[:, :],
                                                                          op0=ALU.mult, op1=ALU.add),
                      reads=[CV[j], PAR, ACCB], writes=[ACCB])
                P.add("dve", lambda e, j=j, bb=bb: e.tensor_tensor(out=yab[:, 4 + j, :], in0=accb[:, :],
                                                                   in1=ps[:, bb, :], op=ALU.mult),
                      reads=[ACCB, PSB[bb]], writes=[YAB[4 + j]])
                P.add("pool", lambda e, j=j: e.tensor_copy(out=cv[:, j, 0:2], in_=cv[:, j, T:T + 2]),
                      reads=[CV[j]], writes=[CV[j]])
            b1 = mm_group([(ones[:, :], abf[:, j, :]) for j in range(4)], [ONES] + ABF)
            b2 = mm_group([(ones[:, :], asq[:, j, :]) for j in range(4)], [ONES] + ASQ)
            P.add("dve", lambda e: e.tensor_scalar(out=mean[:, :], in0=ps[:, b1, :], scalar1=1.0 / 512, scalar2=None,
                                                   op0=ALU.mult), reads=[PSB[b1]], writes=[MEAN])
            tm = rot("tmp", NTMP)
            P.add("dve", lambda e: e.tensor_tensor(out=tmp[tm][:, :], in0=mean[:, :], in1=mean[:, :], op=ALU.mult),
                  reads=[MEAN], writes=[TMP[tm]])
            P.add("dve", lambda e: e.scalar_tensor_tensor(out=var[:, :], in0=ps[:, b2, :], scalar=1.0 / 512,
                                                          in1=tmp[tm][:, :], op0=ALU.mult, op1=ALU.subtract),
                  reads=[PSB[b2], TMP[tm]], writes=[VAR])
            P.add("act", lambda e: e.activation(out=ssum[:, :], in_=var[:, :], func=AF.Sqrt, scale=1.0,
                                                bias=epsc[:, 1:2]), reads=[VAR, MHALF], writes=[SS])
            P.add("dve", lambda e: e.reciprocal(out=rstd2[:, :], in_=ssum[:, :]), reads=[SS], writes=[RS2])
            for j in range(4):
                t1 = rot("tmp", NTMP)
                P.add("dve", lambda e, j=j, t1=t1: e.tensor_tensor(out=tmp[t1][:, :], in0=af32[:, j, :],
                                                                   in1=mean[:, :], op=ALU.subtract),
                      reads=[AFR[j], MEAN], writes=[TMP[t1]])
                t2 = rot("tmp", NTMP)
                P.add("dve", lambda e, t1=t1, t2=t2: e.tensor_tensor(out=tmp[t2][:, :], in0=tmp[t1][:, :],
                                                                     in1=rstd2[:, :], op=ALU.mult),
                      reads=[TMP[t1], RS2], writes=[TMP[t2]])
                P.add("act", lambda e, j=j, t2=t2: e.activation(out=yab[:, j, :], in_=tmp[t2][:, :], func=AF.Silu,
                                                                scale=pcol(PC_LNG + j), bias=pcol(PC_LNB + j)),
                      reads=[TMP[t2], PAR], writes=[YAB[j]])
            for d in range(KC):
                s = ring_load("wout", d, 1024)
                b = mm_group([(ring[s][:, k * 128:(k + 1) * 128], yab[:, k, :]) for k in range(KC)],
                             [RING[s]] + YAB)
                resid_add(xb, d, b)

        def emit_ffn(xb, L):
            emit_norm_hbf(xb, PC_GFFN0 if L == 0 else PC_GFFN1)
            kup = "wup0" if L == 0 else "wup1"
            kdn = "wdn0" if L == 0 else "wdn1"
            for d in range(NRA):
                P.add("sp", lambda e, d=d: e.dma_start(out=ringA[d][:, :], in_=wbf[kdn][d]),
                      reads=[WSG[GRP[(kdn, d)]]], writes=[RINGA[d]], dma=f"ringA{d}")
            slots = {}

            def stage1(f):
                s = ring_load(kup, f, 2048)
                bg = mm_group([(ring[s][:, k * 256:k * 256 + 128], hbf[:, k, :]) for k in range(KC)],
                              [RING[s]] + HB, nbank=4)
                bv = mm_group([(ring[s][:, k * 256 + 128:k * 256 + 256], hbf[:, k, :]) for k in range(KC)],
                              [RING[s]] + HB, nbank=4)
                ui = rot("ub", NUB)
                ci = rot("cb", NCB)
                slots[f] = ci
                hi = L * NF + f
                wg = PC_CFFN + (L * 44 + f) * 3
                wv = PC_CFFN + (L * 44 + NF + f) * 3
                P.add("pool", lambda e: e.tensor_copy(out=ubuf[ui][:, :, 0:2], in_=halF[:, hi, :, :]),
                      reads=[HALF[hi]], writes=[UBH[ui]])
                for q, bq, wq, R, CR in ((0, bg, wg, UBG, CBG), (1, bv, wv, UBV, CBV)):
                    P.add("act", lambda e, q=q, bq=bq: e.activation(out=ubuf[ui][:, q, 2:2 + T], in_=ps[:, bq, :],
                                                                    func=AF.Copy), reads=[PSB[bq]], writes=[R[ui]])
                    P.add("act", lambda e, q=q, bq=bq, wq=wq: e.activation(out=cbuf[ci][:, q, :], in_=ps[:, bq, :],
                                                                           func=AF.Copy, scale=pcol(wq + 2)),
                          reads=[PSB[bq], PAR], writes=[CR[ci]])
                P.add("pool", lambda e: e.tensor_copy(out=halF[:, hi, :, :], in_=ubuf[ui][:, :, T:T + 2]),
                      reads=[UBG[ui], UBV[ui]], writes=[HALF[hi]])
                for tap in (1, 0):
                    for q, wq, R, CR in ((0, wg, UBG, CBG), (1, wv, UBV, CBV)):
                        P.add("dve", lambda e, q=q, wq=wq, tap=tap: e.scalar_tensor_tensor(
                            out=cbuf[ci][:, q, :], in0=ubuf[ui][:, q, tap:tap + T], scalar=pcol(wq + tap),
                            in1=cbuf[ci][:, q, :], op0=ALU.mult, op1=ALU.add),
                            reads=[R[ui], UBH[ui], PAR, CR[ci]], writes=[CR[ci]])

            def stage2(f):
                ci = slots[f]
                si = rot("sg", NSG)
                P.add("act", lambda e: e.activation(out=sgb[si][:, :], in_=cbuf[ci][:, 0, :], func=AF.Silu),
                      reads=[CBG[ci]], writes=[SG[si]])
                P.add("pool", lambda e: e.tensor_tensor(out=actb[:, f, :], in0=sgb[si][:, :], in1=cbuf[ci][:, 1, :],
                                                        op=ALU.mult), reads=[SG[si], CBV[ci]], writes=[ACTR[f]])

            def stage3(f):
                pe_op([(4 + d, ringA[d][:, f * 128:(f + 1) * 128], actb[:, f, :], f == 0, f == NF - 1)
                       for d in range(NRA)], RINGA + [ACTR[f]], [4 + d for d in range(NRA)])

            for f in range(NF):
                stage1(f)
                if f >= 1:
                    stage2(f - 1)
                if f >= 2:
                    stage3(f - 2)
            stage2(NF - 1)
            stage3(NF - 2)
            for d in range(NRA, KC):
                s = ring_load(kdn, d, 2816)
                b = mm_group([(ring[s][:, f * 128:(f + 1) * 128], actb[:, f, :]) for f in range(NF)],
                             [RING[s]] + ACTR, nbank=4)
                resid_add(xb, d, b)
            stage3(NF - 1)
            for d in range(NRA):
                resid_add(xb, d, 4 + d)

        def emit_mixer1(xb, first_tile):
            emit_stats(xb)
            L_ = T + 16
            for gi, w in enumerate(POOL_W):
                c0 = 2 * gi
                P.add("pool", lambda e, c0=c0: e.tensor_copy(out=h1g[:, :, 0:16], in_=h1hal[:, c0:c0 + 2, :]),
                      reads=[H1HAL[gi]], writes=[H1GH])
                for q in range(2):
                    P.add("dve", lambda e, q=q, c0=c0: e.scalar_tensor_tensor(
                        out=h1g[:, q, 16:16 + T], in0=xres[xb][:, c0 + q, :], scalar=pcol(PC_GMIX1 + c0 + q),
                        in1=rstd[:, :], op0=ALU.mult, op1=ALU.mult),
                        reads=[XC[xb][c0 + q], RS, PAR], writes=[H1G])
                P.add("pool", lambda e, c0=c0: e.tensor_copy(out=h1hal[:, c0:c0 + 2, :], in_=h1g[:, :, T:T + 16]),
                      reads=[H1G], writes=[H1HAL[gi]])
                src, srcR = h1g, [H1G, H1GH]
                bufs = [(sA, SA_), (sB, SB_)]
                step, lo = 1, 0
                nb = 0
                while step < w:
                    dst, dstR = bufs[nb % 2]
                    nlo = lo + step
                    P.add("dve", lambda e, src=src, dst=dst, nlo=nlo, step=step: e.tensor_tensor(
                        out=dst[:, :, nlo:L_], in0=src[:, :, nlo:L_], in1=src[:, :, nlo - step:L_ - step], op=ALU.add),
                        reads=srcR, writes=[dstR])
                    src, srcR = dst, [dstR]
                    lo = nlo
                    step *= 2
                    nb += 1
                P.add("dve", lambda e, src=src, c0=c0, w=w: e.scalar_tensor_tensor(
                    out=yab[:, c0:c0 + 2, :], in0=src[:, :, 16:16 + T], scalar=1.0 / w, in1=h1g[:, :, 16:16 + T],
                    op0=ALU.mult, op1=ALU.subtract), reads=srcR + [H1G], writes=[YAB[c0], YAB[c0 + 1]])
                if first_tile:
                    for q in range(2):
                        P.add("dve", lambda e, src=src, q=q, gi=gi: e.tensor_tensor(
                            out=t16[:, q, :], in0=src[:, q, 16:32], in1=invc[:, gi, :], op=ALU.mult),
                            reads=srcR + [INVC], writes=[T16])
                    P.add("dve", lambda e, c0=c0: e.tensor_tensor(out=yab[:, c0:c0 + 2, 0:16], in0=t16[:, :, :],
                                                                  in1=h1g[:, :, 16:32], op=ALU.subtract),
                          reads=[T16, H1G], writes=[YAB[c0], YAB[c0 + 1]])
                for oc in range(2):
                    pairs = []
                    for k in range(2):
                        o = ((gi * 2 + k) * 256) + oc * 128
                        pairs.append((wpool[:, o:o + 128], yab[:, c0 + k, :]))
                    b = mm_group(pairs, [WPOOL, YAB[c0], YAB[c0 + 1]])
                    P.add("dve", lambda e, b=b, c=c0 + oc: e.scalar_tensor_tensor(
                        out=xres[xb][:, c, :], in0=ps[:, b, :], scalar=pcol(PC_PSCALE + c), in1=xres[xb][:, c, :],
                        op0=ALU.mult, op1=ALU.add), reads=[PSB[b], PAR, XC[xb][c0 + oc]], writes=[XC[xb][c0 + oc]])

        def emit_final(xb):
            emit_stats(xb)
            for c in range(KC):
                P.add("dve", lambda e, c=c: e.scalar_tensor_tensor(out=xres[xb][:, c, :], in0=xres[xb][:, c, :],
                                                                   scalar=pcol(PC_GFIN + c), in1=rstd[:, :],
                                                                   op0=ALU.mult, op1=ALU.mult),
                      reads=[XC[xb][c], RS, PAR], writes=[XC[xb][c]])

        stages = ["mix0", "ffn0", "mix1", "ffn1", "final"]
        nst = stages.index(stop_after) + 1
        store_ops = []

        def emit_store(t):
            xb_ = t % 2
            op = P.add("sp", lambda e: e.dma_start(out=out_l[t].rearrange("p (c t) -> p c t", c=KC),
                                                   in_=xres[xb_][:, :, :]), reads=XC[xb_], dma=f"st{xb_}")
            store_ops.append(op)

        def emit_xload(t):
            xb_ = t % 2
            P.add("sp", lambda e: e.dma_start(out=xres[xb_][:, :, :],
                                              in_=x_l[t].rearrange("p (c t) -> p c t", c=KC)),
                  writes=XC[xb_], dma=f"x{xb_}")

        for t in range(nt_run):
            xb = t % 2
            if nst >= 1:
                emit_mixer0(xb)
            if t >= 1:
                emit_store(t - 1)
            if t == 0:
                emit_conv(["wup0b", "wdn0", "wpool", "wup1a"])
            if nst >= 2:
                emit_ffn(xb, 0)
            if t + 1 < nt_run:
                emit_xload(t + 1)
            if t == 0:
                emit_conv(["wup1b", "wdn1"])
                P.add("sp", lambda e: e.dma_start(out=wpool[:, :], in_=wbf["wpool"][0]), reads=[WSG["wpool"]],
                      writes=[WPOOL], dma="wpool")
            if nst >= 3:
                emit_mixer1(xb, t == 0)
            if nst >= 4:
                emit_ffn(xb, 1)
            if nst >= 5:
                emit_final(xb)
        emit_store(nt_run - 1)
        fence = P.add("sp", None)
        fence.deps = set(store_ops)

        sem_names = P.finalize()
        sems = {n: es.enter_context(nc.semaphore(n)) for n in sem_names}
        with nc.Block() as block:
            @block.tensor
            def _(e):
                P.emit("pe", e, sems)

            @block.scalar
            def _(e):
                P.emit("act", e, sems)

            @block.vector
            def _(e):
                P.emit("dve", e, sems)

            @block.gpsimd
            def _(e):
                P.emit("pool", e, sems)

            @block.sync
            def _(e):
                P.emit("sp", e, sems)
    return nc


def _blk(w, bc):
    K, N = w.shape
    nk = K // 128
    return np.ascontiguousarray(w.reshape(nk, 128, N // bc, bc).transpose(2, 1, 0, 3)).reshape(N // bc, 128, nk * bc)


def _col(v):
    return np.ascontiguousarray(v.reshape(-1, 128).T)


def prepare_inputs(x, norm_mix_even, w_in, conv_a, ln_a_g, ln_a_b, conv_b, w_out, norm_mix_odd, w_pool,
                   pool_scale, norm_ffn, w_up, conv_ffn_w, w_down, norm_final):
    f = np.float32
    par = np.zeros((128, NPAR), f)
    par[:, PC_GMIX0:PC_GMIX0 + 8] = _col(norm_mix_even[0])
    par[:, PC_GFFN0:PC_GFFN0 + 8] = _col(norm_ffn[0])
    par[:, PC_GMIX1:PC_GMIX1 + 8] = _col(norm_mix_odd[0])
    par[:, PC_GFFN1:PC_GFFN1 + 8] = _col(norm_ffn[1])
    par[:, PC_GFIN:PC_GFIN + 8] = _col(norm_final)
    par[:, PC_LNG:PC_LNG + 4] = _col(ln_a_g[0])
    par[:, PC_LNB:PC_LNB + 4] = _col(ln_a_b[0])
    par[:, PC_CONVB:PC_CONVB + 12] = conv_b[0].reshape(3, 4, 128).transpose(2, 1, 0).reshape(128, 12)
    par[:, PC_PSCALE:PC_PSCALE + 8] = _col(pool_scale[0])
    for L in range(2):
        c = conv_ffn_w[L].reshape(3, 44, 128).transpose(2, 1, 0).reshape(128, 132)
        par[:, PC_CFFN + L * 132:PC_CFFN + (L + 1) * 132] = c
    shared = {"par": par}
    shared["win32"] = _blk(w_in[0], 128)
    shared["wout32"] = _blk(w_out[0], 128)
    wcv = np.zeros((4, 2, 128, 16, 128), f)
    ca = conv_a[0].reshape(31, 4, 128)
    idx = np.arange(128)
    for i in range(31):
        for j in range(4):
            wcv[j, i // 16, idx, i % 16, idx] = ca[i, j]
    shared["wcv32"] = wcv.reshape(8, 128, 2048)
    for L in range(2):
        wu = w_up[L]
        g = wu[:, :DFF].reshape(D, NF, 128)
        v = wu[:, DFF:].reshape(D, NF, 128)
        gv = np.concatenate([g, v], axis=2).reshape(D, NF * 256)
        shared[f"wup{L}32"] = _blk(gv, 256)
        shared[f"wdn{L}32"] = _blk(w_down[L], 128)
    shared["wpool32"] = np.ascontiguousarray(w_pool[0].reshape(4, 2, 128, 256).transpose(2, 0, 1, 3)).reshape(1, 128, 2048)
    in_maps = []
    for b in range(x.shape[0]):
        xl = np.ascontiguousarray(x[b].reshape(NT, T, KC, 128).transpose(0, 3, 2, 1)).reshape(NT, 128, KC * T)
        m = dict(shared)
        m["x_l"] = xl
        in_maps.append(m)
    return in_maps


def assemble_output(res_list):
    outs = []
    for r in res_list:
        o = r["out_l"].reshape(NT, 128, KC, T).transpose(0, 3, 2, 1).reshape(S, D)
        outs.append(o)
    return np.ascontiguousarray(np.stack(outs, axis=0)).astype(np.float32)


def kernel(**inputs):
    inputs = {k: np.asarray(v, dtype=np.float32) for k, v in inputs.items()}
    in_maps = prepare_inputs(**inputs)
    nc = build_program()
    res = run_bass_kernel_spmd(nc, in_maps, core_ids=list(range(NCORE)))
    return assemble_output(res.results)
```

```python
import numpy as np
import concourse.bass as bass
import concourse.mybir as mybir
from concourse.bass_utils import run_bass_kernel_spmd

F32 = mybir.dt.float32
BF16 = mybir.dt.bfloat16
ALU = mybir.AluOpType
AF = mybir.ActivationFunctionType

D = 1024
S = 4096
NCORE = 8
DFF = 2816
NF = DFF // 128
T = 512
NT = S // T
KC = D // 128
RMS_EPS = 1e-6
LN_EPS = 1e-5
NSLOT = 5
NRA = 3
STATB = 7
RINGW = 2816
POOL_W = (2, 4, 8, 16)

PC_GMIX0, PC_GFFN0, PC_GMIX1, PC_GFFN1, PC_GFIN = 0, 8, 16, 24, 32
PC_LNG, PC_LNB = 40, 44
PC_CONVB = 48
PC_PSCALE = 60
PC_CFFN = 68
NPAR = PC_CFFN + 2 * 44 * 3


class Res:
    __slots__ = ("name", "w", "r")

    def __init__(self, name):
        self.name = name
        self.w = None
        self.r = []


class Op:
    __slots__ = ("eng", "idx", "fn", "deps", "sig", "sem", "val", "dma")


class Prog:
    ENGS = ("pe", "act", "dve", "pool", "sp")

    def __init__(self):
        self.ops = {e: [] for e in self.ENGS}
        self.dma_count = {}

    def add(self, eng, fn, reads=(), writes=(), dma=None):
        op = Op()
        op.eng = eng
        op.idx = len(self.ops[eng])
        op.fn = fn
        op.sig = False
        op.dma = dma
        op.sem = None
        op.val = 0
        deps = set()
        for r in reads:
            if r.w is not None:
                deps.add(r.w)
        for w in writes:
            if w.w is not None:
                deps.add(w.w)
            for rd in w.r:
                deps.add(rd)
        deps.discard(op)
        op.deps = deps
        for r in reads:
            r.r.append(op)
        for w in writes:
            w.w = op
            w.r = []
        self.ops[eng].append(op)
        return op

    def _needs_wait(self, op, dep):
        if dep.dma is not None:
            return True
        if dep.eng != op.eng:
            return True
        if op.eng == "pe":
            return False
        return (op.idx - dep.idx) <= 1

    def finalize(self):
        for e in self.ENGS:
            for op in self.ops[e]:
                for d in op.deps:
                    if self._needs_wait(op, d):
                        d.sig = True
        dma_keys = []
        for e in self.ENGS:
            cnt = 0
            for op in self.ops[e]:
                if op.dma is not None:
                    if op.dma not in self.dma_count:
                        self.dma_count[op.dma] = 0
                        dma_keys.append(op.dma)
                    self.dma_count[op.dma] += 16
                    op.sem = "dma_" + op.dma
                    op.val = self.dma_count[op.dma]
                elif op.sig:
                    cnt += 1
                    op.sem = "eng_" + e
                    op.val = cnt
        return ["eng_" + e for e in self.ENGS if e != "sp"] + ["dma_" + k for k in dma_keys]

    def emit(self, eng_name, engine, sems):
        seen = {}
        nwait = 0
        for op in self.ops[eng_name]:
            need = {}
            for d in op.deps:
                if not self._needs_wait(op, d):
                    continue
                if seen.get(d.sem, 0) >= d.val:
                    continue
                if need.get(d.sem, 0) < d.val:
                    need[d.sem] = d.val
            for sname, v in need.items():
                engine.wait_ge(sems[sname], v)
                seen[sname] = v
                nwait += 1
            if op.fn is None:
                continue
            ins = op.fn(engine)
            if op.dma is not None:
                ins.then_inc(sems[op.sem], 16)
            elif op.sig:
                ins.then_inc(sems[op.sem], 1)
        return nwait


def build_program(nt_run=NT, stop_after="final"):
    nc = bass.Bass("TRN2", target_bir_lowering=False)
    P = Prog()

    def dram(name, shape, dt, kind):
        return nc.dram_tensor(name, shape, dt, kind=kind).ap()

    x_l = dram("x_l", [NT, 128, KC * T], F32, "ExternalInput")
    par_d = dram("par", [128, NPAR], F32, "ExternalInput")
    w32 = {
        "win": dram("win32", [20, 128, 1024], F32, "ExternalInput"),
        "wcv": dram("wcv32", [8, 128, 2048], F32, "ExternalInput"),
        "wout": dram("wout32", [8, 128, 1024], F32, "ExternalInput"),
        "wup0": dram("wup032", [NF, 128, 2048], F32, "ExternalInput"),
        "wdn0": dram("wdn032", [8, 128, 2816], F32, "ExternalInput"),
        "wpool": dram("wpool32", [1, 128, 2048], F32, "ExternalInput"),
        "wup1": dram("wup132", [NF, 128, 2048], F32, "ExternalInput"),
        "wdn1": dram("wdn132", [8, 128, 2816], F32, "ExternalInput"),
    }
    wbf = {k: dram(k + "bf", list(v.shape), BF16, "Internal") for k, v in w32.items()}
    out_l = dram("out_l", [NT, 128, KC * T], F32, "ExternalOutput")

    from contextlib import ExitStack
    es = ExitStack()

    def sb(name, shape, dt):
        return es.enter_context(nc.sbuf_tensor(name, shape, dt))

    with es:
        xres = [sb(f"xres{i}", [128, KC, T], F32) for i in range(2)]
        hbf = sb("hbf", [128, KC, T], BF16)
        NSQ = 3
        sqbf = sb("sqbf", [128, NSQ, T], BF16)
        actb = sb("actb", [128, NF, T], BF16)
        yab = sb("yab", [128, KC, T], BF16)
        NUB, NCB, NSG, NTMP = 2, 2, 2, 3
        ubuf = [sb(f"ubuf{i}", [128, 2, T + 2], F32) for i in range(NUB)]
        cbuf = [sb(f"cbuf{i}", [128, 2, T], F32) for i in range(NCB)]
        sgb = [sb(f"sgb{i}", [128, T], F32) for i in range(NSG)]
        tmp = [sb(f"tmp{i}", [128, T], F32) for i in range(NTMP)]
        ring = [sb(f"ring{i}", [128, RINGW], BF16) for i in range(NSLOT)]
        ringA = [sb(f"ringA{i}", [128, RINGW], BF16) for i in range(NRA)]
        ssum = sb("ssum", [128, T], F32)
        rstd = sb("rstd", [128, T], F32)
        mean = sb("mean", [128, T], F32)
        var = sb("var", [128, T], F32)
        rstd2 = sb("rstd2", [128, T], F32)
        glu = sb("glu", [128, 4, T + 30], BF16)
        abf = sb("abf", [128, 4, T], BF16)
        asq = sb("asq", [128, 4, T], BF16)
        accb = sb("accb", [128, T], F32)
        h1g = sb("h1g", [128, 2, T + 16], F32)
        sA = sb("sA", [128, 2, T + 16], F32)
        sB = sb("sB", [128, 2, T + 16], F32)
        t16 = sb("t16", [128, 2, 16], F32)
        halF = sb("halF", [128, 2 * NF, 2, 2], F32)
        bhal = sb("bhal", [128, 4, 2], F32)
        h1hal = sb("h1hal", [128, KC, 16], F32)
        par = sb("par_sb", [128, NPAR], F32)
        ones = sb("ones", [128, 128], BF16)
        epsc = sb("epsc", [128, 2], F32)
        dummy = sb("actdummy", [128, 2], F32)
        invc = sb("invc", [128, 4, 16], F32)
        wpool = sb("wpool_sb", [128, 2048], BF16)
        ps = es.enter_context(nc.psum_tensor("ps", [128, 8, T], F32))

        XC = [[Res(f"x{b}_{c}") for c in range(KC)] for b in range(2)]
        HB = [Res(f"hbf{c}") for c in range(KC)]
        SQ = [Res(f"sq{i}") for i in range(NSQ)]
        ACTR = [Res(f"act{f}") for f in range(NF)]
        YAB = [Res(f"yab{c}") for c in range(KC)]
        UBG = [Res(f"ubg{i}") for i in range(NUB)]
        UBV = [Res(f"ubv{i}") for i in range(NUB)]
        UBH = [Res(f"ubh{i}") for i in range(NUB)]
        CBG = [Res(f"cbg{i}") for i in range(NCB)]
        CBV = [Res(f"cbv{i}") for i in range(NCB)]
        SG = [Res(f"sg{i}") for i in range(NSG)]
        TMP = [Res(f"tmp{i}") for i in range(NTMP)]
        RING = [Res(f"ring{i}") for i in range(NSLOT)]
        RINGA = [Res(f"ringA{i}") for i in range(NRA)]
        PSB = [Res(f"psb{i}") for i in range(8)]
        SS, RS, MEAN, VAR, RS2 = Res("ss"), Res("rs"), Res("mean"), Res("var"), Res("rs2")
        GLU = [Res(f"glu{j}") for j in range(4)]
        ABF = [Res(f"abf{j}") for j in range(4)]
        ASQ = [Res(f"asq{j}") for j in range(4)]
        ACCB = Res("accb")
        H1G, H1GH, SA_, SB_, T16 = Res("h1g"), Res("h1gh"), Res("sA"), Res("sB"), Res("t16")
        HALF = [Res(f"half{i}") for i in range(2 * NF)]
        BHAL = [Res(f"bhal{j}") for j in range(4)]
        H1HAL = [Res(f"h1hal{g}") for g in range(4)]
        DUMMY = Res("dummy")
        PAR, ONES, EPSC, INVC, WPOOL = Res("par"), Res("ones"), Res("epsc"), Res("invc"), Res("wpool")

        def af_ap(j):
            return cbuf[j // 2][:, j % 2, :]

        def AF_R(j):
            return (CBG if j % 2 == 0 else CBV)[j // 2]

        def cv_ap(j, lo, hi):
            return ubuf[j // 2][:, j % 2, lo:hi]

        def CV_R(j):
            return (UBG if j % 2 == 0 else UBV)[j // 2]

        state = {"bank": 0, "bank4": 0, "ring": 0, "ub": 0, "cb": 0, "sg": 0, "tmp": 0, "sq": 0}

        def rot(key, n):
            v = state[key] % n
            state[key] += 1
            return v

        def pcol(c):
            return par[:, c:c + 1]

        P.add("sp", lambda e: e.dma_start(out=par[:, :], in_=par_d[:, :]), writes=[PAR], dma="par")
        P.add("sp", lambda e: e.dma_start(out=xres[0][:, :, :], in_=x_l[0].rearrange("p (c t) -> p c t", c=KC)),
              writes=XC[0], dma="x0")
        P.add("dve", lambda e: e.memset(ones[:, :], 1.0), writes=[ONES])

        def eps_fill(e):
            e.memset(epsc[:, 0:1], RMS_EPS)
            return e.memset(epsc[:, 1:2], LN_EPS)
        P.add("dve", eps_fill, writes=[EPSC])
        P.add("dve", lambda e: e.memset(glu[:, :, 0:30], 0.0), writes=GLU)
        P.add("dve", lambda e: e.memset(bhal[:, :, :], 0.0), writes=BHAL)
        P.add("dve", lambda e: e.memset(halF[:, :, :, :], 0.0), writes=HALF)
        P.add("dve", lambda e: e.memset(h1hal[:, :, :], 0.0), writes=H1HAL)

        def invc_fill(e):
            ins = None
            for gi, w in enumerate(POOL_W):
                ins = e.memset(invc[:, gi, :], 1.0 / w)
                for j in range(w - 1):
                    ins = e.memset(invc[:, gi, j:j + 1], 1.0 / (j + 1))
            return ins
        P.add("pool", invc_fill, writes=[INVC])

        CVG = {
            "win_a": ("win", range(0, 8)), "win_b": ("win", range(8, 20)), "wcv": ("wcv", range(8)),
            "wout": ("wout", range(8)), "wup0a": ("wup0", range(0, 11)), "wup0b": ("wup0", range(11, NF)),
            "wdn0": ("wdn0", range(8)), "wpool": ("wpool", range(1)), "wup1a": ("wup1", range(0, 11)),
            "wup1b": ("wup1", range(11, NF)), "wdn1": ("wdn1", range(8)),
        }
        GRP = {}
        for g, (k, blks) in CVG.items():
            for b in blks:
                GRP[(k, b)] = g
        WSG = {g: Res("wsg_" + g) for g in CVG}

        conv_ops = []
        CONV_WINDOW = 4

        def emit_conv(groups):
            for g in groups:
                k, blks = CVG[g]
                last = None
                for b in blks:
                    last = P.add("pool", lambda e, k=k, b=b: e.dma_start(out=wbf[k][b], in_=w32[k][b]),
                                 dma="cv_" + g)
                    if len(conv_ops) >= CONV_WINDOW:
                        last.deps.add(conv_ops[-CONV_WINDOW])
                    conv_ops.append(last)
                WSG[g].w = last
                WSG[g].r = []

        emit_conv(["win_a", "wcv", "win_b", "wout", "wup0a"])

        def ring_load(kind, blk, ncols):
            s = rot("ring", NSLOT)
            P.add("sp", lambda e: e.dma_start(out=ring[s][:, 0:ncols], in_=wbf[kind][blk]),
                  reads=[WSG[GRP[(kind, blk)]]], writes=[RING[s]], dma=f"ring{s}")
            return s

        def pe_op(items, reads, banks):
            def fn(e):
                ins = None
                for (b, l, r, st, sp_) in items:
                    ins = e.matmul(ps[:, b, :], lhsT=l, rhs=r, start=st, stop=sp_)
                return ins
            P.add("pe", fn, reads=reads, writes=[PSB[b] for b in banks])

        def mm_group(pairs, reads, nbank=7, split_reads=None):
            b = rot("bank", 7) if nbank == 7 else rot("bank4", 4)
            n = len(pairs)
            if split_reads is None:
                pe_op([(b, l, r, i == 0, i == n - 1) for i, (l, r) in enumerate(pairs)], reads, [b])
            else:
                for i, (l, r) in enumerate(pairs):
                    pe_op([(b, l, r, i == 0, i == n - 1)], reads + [split_reads[i]], [b])
            gcount["n"] += 1
            flush_stats()
            return b

        pend = []
        gcount = {"n": 0}

        def flush_stats(all_=False):
            while pend and (all_ or pend[0][0] < gcount["n"]):
                _, q, first, last = pend.pop(0)
                pe_op([(STATB, ones[:, :], sqbf[:, q, :], first, last)], [ONES, SQ[q]], [STATB])

        def stats_add(xb, c, first, last):
            q = rot("sq", NSQ)
            if any(p_[1] == q for p_ in pend):
                flush_stats(True)
            P.add("act", lambda e: e.activation(out=sqbf[:, q, :], in_=xres[xb][:, c, :], func=AF.Square),
                  reads=[XC[xb][c]], writes=[SQ[q]])
            pend.append((gcount["n"], q, first, last))

        def stats_finish():
            flush_stats(True)
            P.add("act", lambda e: e.activation(out=ssum[:, :], in_=ps[:, STATB, :], func=AF.Ln, scale=1.0 / D,
                                                bias=epsc[:, 0:1]), reads=[PSB[STATB], EPSC], writes=[SS])
            P.add("act", lambda e: e.activation(out=rstd[:, :], in_=ssum[:, :], func=AF.Exp, scale=-0.5),
                  reads=[SS], writes=[RS])

        def act_table_prefetch():
            P.add("act", lambda e: e.activation(out=dummy[:, 0:1], in_=epsc[:, 0:1], func=AF.Ln),
                  reads=[EPSC], writes=[DUMMY])

        def h_ops(xb, gc):
            for c in range(KC):
                eng = "dve"
                P.add(eng, lambda e, c=c: e.scalar_tensor_tensor(out=hbf[:, c, :], in0=xres[xb][:, c, :],
                                                                 scalar=pcol(gc + c), in1=rstd[:, :],
                                                                 op0=ALU.mult, op1=ALU.mult),
                      reads=[XC[xb][c], RS, PAR], writes=[HB[c]])

        def norm_full(xb, gc):
            for c in range(KC):
                stats_add(xb, c, c == 0, c == KC - 1)
            stats_finish()
            h_ops(xb, gc)

        def resid_add(xb, d, b):
            P.add("dve", lambda e: e.tensor_tensor(out=xres[xb][:, d, :], in0=xres[xb][:, d, :], in1=ps[:, b, :],
                                                   op=ALU.add), reads=[PSB[b], XC[xb][d]], writes=[XC[xb][d]])

        def emit_mixer0(xb):
            nproj = {"n": 0}

            def proj(blk):
                s = ring_load("win", blk, 1024)
                pairs = [(ring[s][:, k * 128:(k + 1) * 128], hbf[:, k, :]) for k in range(KC)]
                nproj["n"] += 1
                if nproj["n"] == 1:
                    return mm_group(pairs, [RING[s]], split_reads=HB)
                return mm_group(pairs, [RING[s]] + HB)
            for j in range(4):
                bgate = proj(4 + j)
                bval = proj(j)
                ta = rot("tmp", NTMP)
                P.add("act", lambda e, ta=ta, bgate=bgate: e.activation(out=tmp[ta][:, :], in_=ps[:, bgate, :],
                                                                        func=AF.Tanh, scale=0.5),
                      reads=[PSB[bgate]], writes=[TMP[ta]])
                tb = rot("tmp", NTMP)
                P.add("act", lambda e, tb=tb, bval=bval: e.activation(out=tmp[tb][:, :], in_=ps[:, bval, :],
                                                                      func=AF.Copy, scale=0.5),
                      reads=[PSB[bval]], writes=[TMP[tb]])
                P.add("dve", lambda e, ta=ta, tb=tb, j=j: e.scalar_tensor_tensor(
                    out=glu[:, j, 30:30 + T], in0=tmp[ta][:, :], scalar=1.0, in1=tmp[tb][:, :],
                    op0=ALU.add, op1=ALU.mult), reads=[TMP[ta], TMP[tb]], writes=[GLU[j]])
            for j in range(4):
                s0 = ring_load("wcv", 2 * j, 2048)
                s1 = ring_load("wcv", 2 * j + 1, 2048)
                pairs = []
                for i in range(31):
                    sl = s0 if i < 16 else s1
                    ii = i % 16
                    pairs.append((ring[sl][:, ii * 128:(ii + 1) * 128], glu[:, j, i:i + T]))
                b = mm_group(pairs, [RING[s0], RING[s1], GLU[j]])
                P.add("act", lambda e, j=j, b=b: e.activation(out=af_ap(j), in_=ps[:, b, :], func=AF.Copy),
                      reads=[PSB[b]], writes=[AF_R(j)])
                P.add("act", lambda e, j=j, b=b: e.activation(out=abf[:, j, :], in_=ps[:, b, :], func=AF.Copy),
                      reads=[PSB[b]], writes=[ABF[j]])
                P.add("act", lambda e, j=j, b=b: e.activation(out=asq[:, j, :], in_=ps[:, b, :], func=AF.Square),
                      reads=[PSB[b]], writes=[ASQ[j]])
                P.add("pool", lambda e, j=j: e.tensor_copy(out=glu[:, j, 0:30], in_=glu[:, j, T:T + 30]),
                      reads=[GLU[j]], writes=[GLU[j]])

            bbank = {}

            def b_pe(j):
                bc = proj(12 + j)
                bbc = proj(16 + j)
                bb = proj(8 + j)
                bbank[j] = (bc, bbc, bb)

            def b_ew(j):
                bc, bbc, bb = bbank[j]
                tc = rot("tmp", NTMP)
                ui = j // 2
                P.add("pool", lambda e: e.tensor_copy(out=cv_ap(j, 0, 2), in_=bhal[:, j, :]),
                      reads=[BHAL[j]], writes=[UBH[ui]])
                P.add("act", lambda e: e.activation(out=tmp[tc][:, :], in_=ps[:, bc, :], func=AF.Copy),
                      reads=[PSB[bc]], writes=[TMP[tc]])
                P.add("dve", lambda e: e.tensor_tensor(out=cv_ap(j, 2, 2 + T), in0=tmp[tc][:, :], in1=ps[:, bbc, :],
                                                       op=ALU.mult), reads=[TMP[tc], PSB[bbc]], writes=[CV_R(j)])
                P.add("pool", lambda e: e.tensor_copy(out=bhal[:, j, :], in_=cv_ap(j, T, T + 2)),
                      reads=[CV_R(j)], writes=[BHAL[j]])
                wc = PC_CONVB + 3 * j
                P.add("dve", lambda e: e.tensor_scalar(out=accb[:, :], in0=cv_ap(j, 2, 2 + T), scalar1=pcol(wc + 2),
                                                       scalar2=None, op0=ALU.mult),
                      reads=[CV_R(j), PAR], writes=[ACCB])
                for tap in (1, 0):
                    P.add("dve", lambda e, tap=tap: e.scalar_tensor_tensor(
                        out=accb[:, :], in0=cv_ap(j, tap, tap + T), scalar=pcol(wc + tap), in1=accb[:, :],
                        op0=ALU.mult, op1=ALU.add), reads=[CV_R(j), UBH[ui], PAR, ACCB], writes=[ACCB])
                P.add("dve", lambda e: e.tensor_tensor(out=yab[:, 4 + j, :], in0=accb[:, :], in1=ps[:, bb, :],
                                                       op=ALU.mult), reads=[ACCB, PSB[bb]], writes=[YAB[4 + j]])

            for j in (0, 1):
                b_pe(j)
                b_ew(j)
            b1 = mm_group([(ones[:, :], abf[:, j, :]) for j in range(4)], [ONES] + ABF)
            b2 = mm_group([(ones[:, :], asq[:, j, :]) for j in range(4)], [ONES] + ASQ)
            P.add("dve", lambda e: e.tensor_scalar(out=mean[:, :], in0=ps[:, b1, :], scalar1=1.0 / 512, scalar2=None,
                                                   op0=ALU.mult), reads=[PSB[b1]], writes=[MEAN])
            tm = rot("tmp", NTMP)
            P.add("dve", lambda e: e.tensor_tensor(out=tmp[tm][:, :], in0=mean[:, :], in1=mean[:, :], op=ALU.mult),
                  reads=[MEAN], writes=[TMP[tm]])
            P.add("dve", lambda e: e.scalar_tensor_tensor(out=var[:, :], in0=ps[:, b2, :], scalar=1.0 / 512,
                                                          in1=tmp[tm][:, :], op0=ALU.mult, op1=ALU.subtract),
                  reads=[PSB[b2], TMP[tm]], writes=[VAR])
            P.add("act", lambda e: e.activation(out=ssum[:, :], in_=var[:, :], func=AF.Ln, scale=1.0,
                                                bias=epsc[:, 1:2]), reads=[VAR, EPSC], writes=[SS])
            P.add("act", lambda e: e.activation(out=rstd2[:, :], in_=ssum[:, :], func=AF.Exp, scale=-0.5),
                  reads=[SS], writes=[RS2])
            for j in (2, 3):
                b_pe(j)
            for j in range(4):
                t1 = rot("tmp", NTMP)
                P.add("dve", lambda e, j=j, t1=t1: e.tensor_tensor(out=tmp[t1][:, :], in0=af_ap(j), in1=mean[:, :],
                                                                   op=ALU.subtract),
                      reads=[AF_R(j), MEAN], writes=[TMP[t1]])
                t2 = rot("tmp", NTMP)
                P.add("dve", lambda e, t1=t1, t2=t2: e.tensor_tensor(out=tmp[t2][:, :], in0=tmp[t1][:, :],
                                                                     in1=rstd2[:, :], op=ALU.mult),
                      reads=[TMP[t1], RS2], writes=[TMP[t2]])
                P.add("act", lambda e, j=j, t2=t2: e.activation(out=yab[:, j, :], in_=tmp[t2][:, :], func=AF.Silu,
                                                                scale=pcol(PC_LNG + j), bias=pcol(PC_LNB + j)),
                      reads=[TMP[t2], PAR], writes=[YAB[j]])
            act_table_prefetch()
            for j in (2, 3):
                b_ew(j)
            for d in range(KC):
                s = ring_load("wout", d, 1024)
                b = mm_group([(ring[s][:, k * 128:(k + 1) * 128], yab[:, k, :]) for k in range(KC)],
                             [RING[s]] + YAB)
                resid_add(xb, d, b)
                stats_add(xb, d, d == 0, d == KC - 1)

        def emit_ffn(xb, L, hoist=None):
            stats_finish()
            h_ops(xb, PC_GFFN0 if L == 0 else PC_GFFN1)
            kup = "wup0" if L == 0 else "wup1"
            kdn = "wdn0" if L == 0 else "wdn1"
            for d in range(NRA):
                P.add("sp", lambda e, d=d: e.dma_start(out=ringA[d][:, :], in_=wbf[kdn][d]),
                      reads=[WSG[GRP[(kdn, d)]]], writes=[RINGA[d]], dma=f"ringA{d}")
            slots = {}

            def stage1(f):
                s = ring_load(kup, f, 2048)
                if f == 0:
                    bg = mm_group([(ring[s][:, k * 256:k * 256 + 128], hbf[:, k, :]) for k in range(KC)],
                                  [RING[s]], nbank=4, split_reads=HB)
                else:
                    bg = mm_group([(ring[s][:, k * 256:k * 256 + 128], hbf[:, k, :]) for k in range(KC)],
                                  [RING[s]] + HB, nbank=4)
                bv = mm_group([(ring[s][:, k * 256 + 128:k * 256 + 256], hbf[:, k, :]) for k in range(KC)],
                              [RING[s]] + HB, nbank=4)
                ui = rot("ub", NUB)
                ci = rot("cb", NCB)
                slots[f] = ci
                hi = L * NF + f
                wg = PC_CFFN + (L * 44 + f) * 3
                wv = PC_CFFN + (L * 44 + NF + f) * 3
                P.add("pool", lambda e: e.tensor_copy(out=ubuf[ui][:, :, 0:2], in_=halF[:, hi, :, :]),
                      reads=[HALF[hi]], writes=[UBH[ui]])
                for q, bq, wq, R, CR in ((0, bg, wg, UBG, CBG), (1, bv, wv, UBV, CBV)):
                    P.add("act", lambda e, q=q, bq=bq: e.activation(out=ubuf[ui][:, q, 2:2 + T], in_=ps[:, bq, :],
                                                                    func=AF.Copy), reads=[PSB[bq]], writes=[R[ui]])
                    P.add("act", lambda e, q=q, bq=bq, wq=wq: e.activation(out=cbuf[ci][:, q, :], in_=ps[:, bq, :],
                                                                           func=AF.Copy, scale=pcol(wq + 2)),
                          reads=[PSB[bq], PAR], writes=[CR[ci]])
                P.add("pool", lambda e: e.tensor_copy(out=halF[:, hi, :, :], in_=ubuf[ui][:, :, T:T + 2]),
                      reads=[UBG[ui], UBV[ui]], writes=[HALF[hi]])
                for tap in (1, 0):
                    for q, wq, R, CR in ((0, wg, UBG, CBG), (1, wv, UBV, CBV)):
                        P.add("dve", lambda e, q=q, wq=wq, tap=tap: e.scalar_tensor_tensor(
                            out=cbuf[ci][:, q, :], in0=ubuf[ui][:, q, tap:tap + T], scalar=pcol(wq + tap),
                            in1=cbuf[ci][:, q, :], op0=ALU.mult, op1=ALU.add),
                            reads=[R[ui], UBH[ui], PAR, CR[ci]], writes=[CR[ci]])

            def stage2(f):
                ci = slots[f]
                si = rot("sg", NSG)
                P.add("act", lambda e: e.activation(out=sgb[si][:, :], in_=cbuf[ci][:, 0, :], func=AF.Silu),
                      reads=[CBG[ci]], writes=[SG[si]])
                P.add("pool", lambda e: e.tensor_tensor(out=actb[:, f, :], in0=sgb[si][:, :], in1=cbuf[ci][:, 1, :],
                                                        op=ALU.mult), reads=[SG[si], CBV[ci]], writes=[ACTR[f]])

            def stage3(f):
                pe_op([(4 + d, ringA[d][:, f * 128:(f + 1) * 128], actb[:, f, :], f == 0, f == NF - 1)
                       for d in range(NRA)], RINGA + [ACTR[f]], [4 + d for d in range(NRA)])

            for f in range(NF):
                stage1(f)
                if f >= 1:
                    stage2(f - 1)
                if f >= 2:
                    stage3(f - 2)
            stage2(NF - 1)
            act_table_prefetch()
            if hoist is not None:
                hoist()
            stage3(NF - 2)

            def second(d, first, last):
                s = ring_load(kdn, d, 2816)
                b = mm_group([(ring[s][:, f * 128:(f + 1) * 128], actb[:, f, :]) for f in range(NF)],
                             [RING[s]] + ACTR, nbank=4)
                resid_add(xb, d, b)
                stats_add(xb, d, first, last)

            for d in range(NRA, KC - 2):
                second(d, d == NRA, False)
            stage3(NF - 1)
            for d in range(NRA):
                resid_add(xb, d, 4 + d)
                stats_add(xb, d, False, False)
            for d in range(KC - 2, KC):
                second(d, False, d == KC - 1)

        def emit_mixer1(xb, first_tile):
            stats_finish()
            L_ = T + 16
            for gi, w in enumerate(POOL_W):
                c0 = 2 * gi
                P.add("pool", lambda e, c0=c0: e.tensor_copy(out=h1g[:, :, 0:16], in_=h1hal[:, c0:c0 + 2, :]),
                      reads=[H1HAL[gi]], writes=[H1GH])
                for q in range(2):
                    P.add("dve", lambda e, q=q, c0=c0: e.scalar_tensor_tensor(
                        out=h1g[:, q, 16:16 + T], in0=xres[xb][:, c0 + q, :], scalar=pcol(PC_GMIX1 + c0 + q),
                        in1=rstd[:, :], op0=ALU.mult, op1=ALU.mult),
                        reads=[XC[xb][c0 + q], RS, PAR], writes=[H1G])
                P.add("pool", lambda e, c0=c0: e.tensor_copy(out=h1hal[:, c0:c0 + 2, :], in_=h1g[:, :, T:T + 16]),
                      reads=[H1G], writes=[H1HAL[gi]])
                src, srcR = h1g, [H1G, H1GH]
                bufs = [(sA, SA_), (sB, SB_)]
                step, lo = 1, 0
                nb = 0
                while step < w:
                    dst, dstR = bufs[nb % 2]
                    nlo = lo + step
                    P.add("dve", lambda e, src=src, dst=dst, nlo=nlo, step=step: e.tensor_tensor(
                        out=dst[:, :, nlo:L_], in0=src[:, :, nlo:L_], in1=src[:, :, nlo - step:L_ - step], op=ALU.add),
                        reads=srcR, writes=[dstR])
                    src, srcR = dst, [dstR]
                    lo = nlo
                    step *= 2
                    nb += 1
                P.add("dve", lambda e, src=src, c0=c0, w=w: e.scalar_tensor_tensor(
                    out=yab[:, c0:c0 + 2, :], in0=src[:, :, 16:16 + T], scalar=1.0 / w, in1=h1g[:, :, 16:16 + T],
                    op0=ALU.mult, op1=ALU.subtract), reads=srcR + [H1G], writes=[YAB[c0], YAB[c0 + 1]])
                if first_tile:
                    for q in range(2):
                        P.add("dve", lambda e, src=src, q=q, gi=gi: e.tensor_tensor(
                            out=t16[:, q, :], in0=src[:, q, 16:32], in1=invc[:, gi, :], op=ALU.mult),
                            reads=srcR + [INVC], writes=[T16])
                    P.add("dve", lambda e, c0=c0: e.tensor_tensor(out=yab[:, c0:c0 + 2, 0:16], in0=t16[:, :, :],
                                                                  in1=h1g[:, :, 16:32], op=ALU.subtract),
                          reads=[T16, H1G], writes=[YAB[c0], YAB[c0 + 1]])
                for oc in range(2):
                    pairs = []
                    for k in range(2):
                        o = ((gi * 2 + k) * 256) + oc * 128
                        pairs.append((wpool[:, o:o + 128], yab[:, c0 + k, :]))
                    b = mm_group(pairs, [WPOOL, YAB[c0], YAB[c0 + 1]])
                    c = c0 + oc
                    P.add("dve", lambda e, b=b, c=c: e.scalar_tensor_tensor(
                        out=xres[xb][:, c, :], in0=ps[:, b, :], scalar=pcol(PC_PSCALE + c), in1=xres[xb][:, c, :],
                        op0=ALU.mult, op1=ALU.add), reads=[PSB[b], PAR, XC[xb][c]], writes=[XC[xb][c]])
                    stats_add(xb, c, c == 0, c == KC - 1)

        def emit_final(xb):
            stats_finish()
            for c in range(KC):
                P.add("dve", lambda e, c=c: e.scalar_tensor_tensor(out=xres[xb][:, c, :], in0=xres[xb][:, c, :],
                                                                   scalar=pcol(PC_GFIN + c), in1=rstd[:, :],
                                                                   op0=ALU.mult, op1=ALU.mult),
                      reads=[XC[xb][c], RS, PAR], writes=[XC[xb][c]])

        store_ops = []

        def emit_store(t):
            xb_ = t % 2
            op = P.add("sp", lambda e: e.dma_start(out=out_l[t].rearrange("p (c t) -> p c t", c=KC),
                                                   in_=xres[xb_][:, :, :]), reads=XC[xb_], dma=f"st{xb_}")
            store_ops.append(op)

        def emit_xload(t):
            xb_ = t % 2
            P.add("sp", lambda e: e.dma_start(out=xres[xb_][:, :, :],
                                              in_=x_l[t].rearrange("p (c t) -> p c t", c=KC)),
                  writes=XC[xb_], dma=f"x{xb_}")

        norm_full(0, PC_GMIX0)
        for t in range(nt_run):
            xb = t % 2
            emit_mixer0(xb)
            if t >= 1:
                emit_store(t - 1)
            if t == 0:
                emit_conv(["wup0b", "wdn0", "wpool", "wup1a"])
            emit_ffn(xb, 0)
            if t + 1 < nt_run:
                emit_xload(t + 1)
            if t == 0:
                emit_conv(["wup1b", "wdn1"])
                P.add("sp", lambda e: e.dma_start(out=wpool[:, :], in_=wbf["wpool"][0]), reads=[WSG["wpool"]],
                      writes=[WPOOL], dma="wpool")
            emit_mixer1(xb, t == 0)
            hoist = None
            if t + 1 < nt_run:
                hoist = (lambda nb=(t + 1) % 2: norm_full(nb, PC_GMIX0))
            emit_ffn(xb, 1, hoist=hoist)
            emit_final(xb)
        emit_store(nt_run - 1)
        fence = P.add("sp", None)
        fence.deps = set(store_ops)

        sem_names = P.finalize()
        sems = {n: es.enter_context(nc.semaphore(n)) for n in sem_names}
        with nc.Block() as block:
            @block.tensor
            def _(e):
                P.emit("pe", e, sems)

            @block.scalar
            def _(e):
                P.emit("act", e, sems)

            @block.vector
            def _(e):
                P.emit("dve", e, sems)

            @block.gpsimd
            def _(e):
                P.emit("pool", e, sems)

            @block.sync
            def _(e):
                P.emit("sp", e, sems)
    return nc


def _blk(w, bc):
    K, N = w.shape
    nk = K // 128
    return np.ascontiguousarray(w.reshape(nk, 128, N // bc, bc).transpose(2, 1, 0, 3)).reshape(N // bc, 128, nk * bc)


def _col(v):
    return np.ascontiguousarray(v.reshape(-1, 128).T)


def prepare_inputs(x, norm_mix_even, w_in, conv_a, ln_a_g, ln_a_b, conv_b, w_out, norm_mix_odd, w_pool,
                   pool_scale, norm_ffn, w_up, conv_ffn_w, w_down, norm_final):
    f = np.float32
    par = np.zeros((128, NPAR), f)
    par[:, PC_GMIX0:PC_GMIX0 + 8] = _col(norm_mix_even[0])
    par[:, PC_GFFN0:PC_GFFN0 + 8] = _col(norm_ffn[0])
    par[:, PC_GMIX1:PC_GMIX1 + 8] = _col(norm_mix_odd[0])
    par[:, PC_GFFN1:PC_GFFN1 + 8] = _col(norm_ffn[1])
    par[:, PC_GFIN:PC_GFIN + 8] = _col(norm_final)
    par[:, PC_LNG:PC_LNG + 4] = _col(ln_a_g[0])
    par[:, PC_LNB:PC_LNB + 4] = _col(ln_a_b[0])
    par[:, PC_CONVB:PC_CONVB + 12] = conv_b[0].reshape(3, 4, 128).transpose(2, 1, 0).reshape(128, 12)
    par[:, PC_PSCALE:PC_PSCALE + 8] = _col(pool_scale[0])
    for L in range(2):
        c = conv_ffn_w[L].reshape(3, 44, 128).transpose(2, 1, 0).reshape(128, 132)
        par[:, PC_CFFN + L * 132:PC_CFFN + (L + 1) * 132] = c
    shared = {"par": par}
    shared["win32"] = _blk(w_in[0], 128)
    shared["wout32"] = _blk(w_out[0], 128)
    wcv = np.zeros((4, 2, 128, 16, 128), f)
    ca = conv_a[0].reshape(31, 4, 128)
    idx = np.arange(128)
    for i in range(31):
        for j in range(4):
            wcv[j, i // 16, idx, i % 16, idx] = ca[i, j]
    shared["wcv32"] = wcv.reshape(8, 128, 2048)
    for L in range(2):
        wu = w_up[L]
        g = wu[:, :DFF].reshape(D, NF, 128)
        v = wu[:, DFF:].reshape(D, NF, 128)
        gv = np.concatenate([g, v], axis=2).reshape(D, NF * 256)
        shared[f"wup{L}32"] = _blk(gv, 256)
        shared[f"wdn{L}32"] = _blk(w_down[L], 128)
    shared["wpool32"] = np.ascontiguousarray(w_pool[0].reshape(4, 2, 128, 256).transpose(2, 0, 1, 3)).reshape(1, 128, 2048)
    in_maps = []
    for b in range(x.shape[0]):
        xl = np.ascontiguousarray(x[b].reshape(NT, T, KC, 128).transpose(0, 3, 2, 1)).reshape(NT, 128, KC * T)
        m = dict(shared)
        m["x_l"] = xl
        in_maps.append(m)
    return in_maps


def assemble_output(res_list):
    outs = []
    for r in res_list:
        o = r["out_l"].reshape(NT, 128, KC, T).transpose(0, 3, 2, 1).reshape(S, D)
        outs.append(o)
    return np.ascontiguousarray(np.stack(outs, axis=0)).astype(np.float32)


def kernel(**inputs):
    inputs = {k: np.asarray(v, dtype=np.float32) for k, v in inputs.items()}
    in_maps = prepare_inputs(**inputs)
    nc = build_program()
    res = run_bass_kernel_spmd(nc, in_maps, core_ids=list(range(NCORE)))
    return assemble_output(res.results)
```

```python
import numpy as np
import concourse.bass as bass
import concourse.mybir as mybir
from concourse.bass_utils import run_bass_kernel_spmd

F32 = mybir.dt.float32
BF16 = mybir.dt.bfloat16
ALU = mybir.AluOpType
AF = mybir.ActivationFunctionType

D = 1024
S = 4096
NCORE = 8
DFF = 2816
NF = DFF // 128
T = 512
NT = S // T
KC = D // 128
RMS_EPS = 1e-6
LN_EPS = 1e-5
NSLOT = 4
NRA = 3
STATB = 7
RINGW = 2816
POOL_W = (2, 4, 8, 16)

PC_GMIX0, PC_GFFN0, PC_GMIX1, PC_GFFN1, PC_GFIN = 0, 8, 16, 24, 32
PC_LNG, PC_LNB = 40, 44
PC_CONVB = 48
PC_PSCALE = 60
PC_CFFN = 68
NPAR = PC_CFFN + 2 * 44 * 3


class Res:
    __slots__ = ("name", "w", "r")

    def __init__(self, name):
        self.name = name
        self.w = None
        self.r = []


class Op:
    __slots__ = ("eng", "idx", "fn", "deps", "sig", "sem", "val", "dma")


class Prog:
    ENGS = ("pe", "act", "dve", "pool", "sp")

    def __init__(self):
        self.ops = {e: [] for e in self.ENGS}
        self.dma_count = {}

    def add(self, eng, fn, reads=(), writes=(), dma=None):
        op = Op()
        op.eng = eng
        op.idx = len(self.ops[eng])
        op.fn = fn
        op.sig = False
        op.dma = dma
        op.sem = None
        op.val = 0
        deps = set()
        for r in reads:
            if r.w is not None:
                deps.add(r.w)
        for w in writes:
            if w.w is not None:
                deps.add(w.w)
            for rd in w.r:
                deps.add(rd)
        deps.discard(op)
        op.deps = deps
        for r in reads:
            r.r.append(op)
        for w in writes:
            w.w = op
            w.r = []
        self.ops[eng].append(op)
        return op

    def _needs_wait(self, op, dep):
        if dep.dma is not None:
            return True
        if dep.eng != op.eng:
            return True
        if op.eng == "pe":
            return False
        return (op.idx - dep.idx) <= 1

    def finalize(self):
        for e in self.ENGS:
            for op in self.ops[e]:
                for d in op.deps:
                    if self._needs_wait(op, d):
                        d.sig = True
        dma_keys = []
        for e in self.ENGS:
            cnt = 0
            for op in self.ops[e]:
                if op.dma is not None:
                    if op.dma not in self.dma_count:
                        self.dma_count[op.dma] = 0
                        dma_keys.append(op.dma)
                    self.dma_count[op.dma] += 16
                    op.sem = "dma_" + op.dma
                    op.val = self.dma_count[op.dma]
                elif op.sig:
                    cnt += 1
                    op.sem = "eng_" + e
                    op.val = cnt
        return ["eng_" + e for e in self.ENGS if e != "sp"] + ["dma_" + k for k in dma_keys]

    def emit(self, eng_name, engine, sems):
        seen = {}
        nwait = 0
        for op in self.ops[eng_name]:
            need = {}
            for d in op.deps:
                if not self._needs_wait(op, d):
                    continue
                if seen.get(d.sem, 0) >= d.val:
                    continue
                if need.get(d.sem, 0) < d.val:
                    need[d.sem] = d.val
            for sname, v in need.items():
                engine.wait_ge(sems[sname], v)
                seen[sname] = v
                nwait += 1
            if op.fn is None:
                continue
            ins = op.fn(engine)
            if op.dma is not None:
                ins.then_inc(sems[op.sem], 16)
            elif op.sig:
                ins.then_inc(sems[op.sem], 1)
        return nwait


def build_program(nt_run=NT, stop_after="final"):
    nc = bass.Bass("TRN2", target_bir_lowering=False)
    P = Prog()

    def dram(name, shape, dt, kind):
        return nc.dram_tensor(name, shape, dt, kind=kind).ap()

    x_l = dram("x_l", [NT, 128, KC * T], F32, "ExternalInput")
    par_d = dram("par", [128, NPAR], F32, "ExternalInput")
    w32 = {
        "win": dram("win32", [20, 128, 1024], F32, "ExternalInput"),
        "wcv": dram("wcv32", [8, 128, 2048], F32, "ExternalInput"),
        "wout": dram("wout32", [8, 128, 1024], F32, "ExternalInput"),
        "wup0": dram("wup032", [NF, 128, 2048], F32, "ExternalInput"),
        "wdn0": dram("wdn032", [8, 128, 2816], F32, "ExternalInput"),
        "wpool": dram("wpool32", [1, 128, 2048], F32, "ExternalInput"),
        "wup1": dram("wup132", [NF, 128, 2048], F32, "ExternalInput"),
        "wdn1": dram("wdn132", [8, 128, 2816], F32, "ExternalInput"),
    }
    wbf = {k: dram(k + "bf", list(v.shape), BF16, "Internal") for k, v in w32.items()}
    out_l = dram("out_l", [NT, 128, KC * T], F32, "ExternalOutput")

    from contextlib import ExitStack
    es = ExitStack()

    def sb(name, shape, dt):
        return es.enter_context(nc.sbuf_tensor(name, shape, dt))

    with es:
        xres = [sb(f"xres{i}", [128, KC, T], F32) for i in range(2)]
        hbf = sb("hbf", [128, KC, T], BF16)
        NSQ = 3
        sqbf = sb("sqbf", [128, NSQ, T], BF16)
        actb = sb("actb", [128, NF, T], BF16)
        yab = sb("yab", [128, KC, T], BF16)
        NUB, NCB, NSG, NTMP = 2, 2, 2, 3
        ubuf = [sb(f"ubuf{i}", [128, 2, T + 2], F32) for i in range(NUB)]
        cbuf = [sb(f"cbuf{i}", [128, 2, T], F32) for i in range(NCB)]
        sgb = [sb(f"sgb{i}", [128, T], F32) for i in range(NSG)]
        tmp = [sb(f"tmp{i}", [128, T], F32) for i in range(NTMP)]
        ring = [sb(f"ring{i}", [128, RINGW], BF16) for i in range(NSLOT)]
        ringA = [sb(f"ringA{i}", [128, RINGW], BF16) for i in range(NRA)]
        ssum = sb("ssum", [128, T], F32)
        rstd = sb("rstd", [128, T], F32)
        mean = sb("mean", [128, T], F32)
        var = sb("var", [128, T], F32)
        rstd2 = sb("rstd2", [128, T], F32)
        glu = sb("glu", [128, 4, T + 30], BF16)
        abf = sb("abf", [128, 4, T], BF16)
        asq = sb("asq", [128, 4, T], BF16)
        accb = sb("accb", [128, T], F32)
        h1g = sb("h1g", [128, 2, T + 16], F32)
        sA = sb("sA", [128, 2, T + 16], F32)
        sB = sb("sB", [128, 2, T + 16], F32)
        t16 = sb("t16", [128, 2, 16], F32)
        halF = sb("halF", [128, 2 * NF, 2, 2], F32)
        bhal = sb("bhal", [128, 4, 2], F32)
        h1hal = sb("h1hal", [128, KC, 16], F32)
        par = sb("par_sb", [128, NPAR], F32)
        ones = sb("ones", [128, 128], BF16)
        epsc = sb("epsc", [128, 2], F32)
        dummy = sb("actdummy", [128, 2], F32)
        invc = sb("invc", [128, 4, 16], F32)
        wpool = sb("wpool_sb", [128, 2048], BF16)
        ps = es.enter_context(nc.psum_tensor("ps", [128, 8, T], F32))

        XC = [[Res(f"x{b}_{c}") for c in range(KC)] for b in range(2)]
        HB = [Res(f"hbf{c}") for c in range(KC)]
        SQ = [Res(f"sq{i}") for i in range(NSQ)]
        ACTR = [Res(f"act{f}") for f in range(NF)]
        YAB = [Res(f"yab{c}") for c in range(KC)]
        UBG = [Res(f"ubg{i}") for i in range(NUB)]
        UBV = [Res(f"ubv{i}") for i in range(NUB)]
        UBH = [Res(f"ubh{i}") for i in range(NUB)]
        CBG = [Res(f"cbg{i}") for i in range(NCB)]
        CBV = [Res(f"cbv{i}") for i in range(NCB)]
        SG = [Res(f"sg{i}") for i in range(NSG)]
        TMP = [Res(f"tmp{i}") for i in range(NTMP)]
        RING = [Res(f"ring{i}") for i in range(NSLOT)]
        RINGA = [Res(f"ringA{i}") for i in range(NRA)]
        PSB = [Res(f"psb{i}") for i in range(8)]
        SS, RS, MEAN, VAR, RS2 = Res("ss"), Res("rs"), Res("mean"), Res("var"), Res("rs2")
        GLU = [Res(f"glu{j}") for j in range(4)]
        ABF = [Res(f"abf{j}") for j in range(4)]
        ASQ = [Res(f"asq{j}") for j in range(4)]
        ACCB = Res("accb")
        H1G, H1GH, SA_, SB_, T16 = Res("h1g"), Res("h1gh"), Res("sA"), Res("sB"), Res("t16")
        HALF = [Res(f"half{i}") for i in range(2 * NF)]
        BHAL = [Res(f"bhal{j}") for j in range(4)]
        H1HAL = [Res(f"h1hal{g}") for g in range(4)]
        DUMMY = Res("dummy")
        PAR, ONES, EPSC, INVC, WPOOL = Res("par"), Res("ones"), Res("epsc"), Res("invc"), Res("wpool")

        def af_ap(j):
            return cbuf[j // 2][:, j % 2, :]

        def AF_R(j):
            return (CBG if j % 2 == 0 else CBV)[j // 2]

        def cv_ap(j, lo, hi):
            return ubuf[j // 2][:, j % 2, lo:hi]

        def CV_R(j):
            return (UBG if j % 2 == 0 else UBV)[j // 2]

        state = {"bank": 0, "bank4": 0, "ring": 0, "ub": 0, "cb": 0, "sg": 0, "tmp": 0, "sq": 0}

        def rot(key, n):
            v = state[key] % n
            state[key] += 1
            return v

        def pcol(c):
            return par[:, c:c + 1]

        P.add("sp", lambda e: e.dma_start(out=par[:, :], in_=par_d[:, :]), writes=[PAR], dma="par")
        P.add("sp", lambda e: e.dma_start(out=xres[0][:, :, :], in_=x_l[0].rearrange("p (c t) -> p c t", c=KC)),
              writes=XC[0], dma="x0")
        P.add("dve", lambda e: e.memset(ones[:, :], 1.0), writes=[ONES])

        def eps_fill(e):
            e.memset(epsc[:, 0:1], RMS_EPS)
            return e.memset(epsc[:, 1:2], LN_EPS)
        P.add("dve", eps_fill, writes=[EPSC])
        P.add("dve", lambda e: e.memset(glu[:, :, 0:30], 0.0), writes=GLU)
        P.add("dve", lambda e: e.memset(bhal[:, :, :], 0.0), writes=BHAL)
        P.add("dve", lambda e: e.memset(halF[:, :, :, :], 0.0), writes=HALF)
        P.add("dve", lambda e: e.memset(h1hal[:, :, :], 0.0), writes=H1HAL)

        def invc_fill(e):
            ins = None
            for gi, w in enumerate(POOL_W):
                ins = e.memset(invc[:, gi, :], 1.0 / w)
                for j in range(w - 1):
                    ins = e.memset(invc[:, gi, j:j + 1], 1.0 / (j + 1))
            return ins
        P.add("pool", invc_fill, writes=[INVC])

        CVG = {
            "win_a": ("win", range(0, 8)), "win_b": ("win", range(8, 20)), "wcv": ("wcv", range(8)),
            "wout": ("wout", range(8)), "wup0a": ("wup0", range(0, 11)), "wup0b": ("wup0", range(11, NF)),
            "wdn0": ("wdn0", range(8)), "wpool": ("wpool", range(1)), "wup1a": ("wup1", range(0, 11)),
            "wup1b": ("wup1", range(11, NF)), "wdn1": ("wdn1", range(8)),
        }
        GRP = {}
        for g, (k, blks) in CVG.items():
            for b in blks:
                GRP[(k, b)] = g
        WSG = {g: Res("wsg_" + g) for g in CVG}

        conv_ops = []
        CONV_WINDOW = 4

        def emit_conv(groups):
            for g in groups:
                k, blks = CVG[g]
                last = None
                for b in blks:
                    last = P.add("pool", lambda e, k=k, b=b: e.dma_start(out=wbf[k][b], in_=w32[k][b]),
                                 dma="cv_" + g)
                    if len(conv_ops) >= CONV_WINDOW:
                        last.deps.add(conv_ops[-CONV_WINDOW])
                    conv_ops.append(last)
                WSG[g].w = last
                WSG[g].r = []

        emit_conv(["win_a", "wcv", "win_b", "wout", "wup0a"])

        def ring_load(kind, blk, ncols):
            s = rot("ring", NSLOT)
            P.add("sp", lambda e: e.dma_start(out=ring[s][:, 0:ncols], in_=wbf[kind][blk]),
                  reads=[WSG[GRP[(kind, blk)]]], writes=[RING[s]], dma=f"ring{s}")
            return s

        def pe_op(items, reads, banks):
            def fn(e):
                ins = None
                for (b, l, r, st, sp_) in items:
                    ins = e.matmul(ps[:, b, :], lhsT=l, rhs=r, start=st, stop=sp_)
                return ins
            P.add("pe", fn, reads=reads, writes=[PSB[b] for b in banks])

        def mm_group(pairs, reads, nbank=7, split_reads=None):
            b = rot("bank", 7) if nbank == 7 else rot("bank4", 4)
            n = len(pairs)
            if split_reads is None:
                pe_op([(b, l, r, i == 0, i == n - 1) for i, (l, r) in enumerate(pairs)], reads, [b])
            else:
                for i, (l, r) in enumerate(pairs):
                    pe_op([(b, l, r, i == 0, i == n - 1)], reads + [split_reads[i]], [b])
            gcount["n"] += 1
            flush_stats()
            return b

        pend = []
        gcount = {"n": 0}

        def flush_stats(all_=False):
            while pend and (all_ or pend[0][0] < gcount["n"]):
                _, q, first, last = pend.pop(0)
                pe_op([(STATB, ones[:, :], sqbf[:, q, :], first, last)], [ONES, SQ[q]], [STATB])

        def stats_add(xb, c, first, last):
            q = rot("sq", NSQ)
            if any(p_[1] == q for p_ in pend):
                flush_stats(True)
            P.add("act", lambda e: e.activation(out=sqbf[:, q, :], in_=xres[xb][:, c, :], func=AF.Square),
                  reads=[XC[xb][c]], writes=[SQ[q]])
            pend.append((gcount["n"], q, first, last))

        def stats_finish():
            flush_stats(True)
            P.add("act", lambda e: e.activation(out=ssum[:, :], in_=ps[:, STATB, :], func=AF.Ln, scale=1.0 / D,
                                                bias=epsc[:, 0:1]), reads=[PSB[STATB], EPSC], writes=[SS])
            P.add("act", lambda e: e.activation(out=rstd[:, :], in_=ssum[:, :], func=AF.Exp, scale=-0.5),
                  reads=[SS], writes=[RS])

        def act_table_prefetch():
            P.add("act", lambda e: e.activation(out=dummy[:, 0:1], in_=epsc[:, 0:1], func=AF.Ln),
                  reads=[EPSC], writes=[DUMMY])

        def h_ops(xb, gc):
            for c in range(KC):
                eng = "dve"
                P.add(eng, lambda e, c=c: e.scalar_tensor_tensor(out=hbf[:, c, :], in0=xres[xb][:, c, :],
                                                                 scalar=pcol(gc + c), in1=rstd[:, :],
                                                                 op0=ALU.mult, op1=ALU.mult),
                      reads=[XC[xb][c], RS, PAR], writes=[HB[c]])

        def norm_full(xb, gc):
            for c in range(KC):
                stats_add(xb, c, c == 0, c == KC - 1)
            stats_finish()
            h_ops(xb, gc)

        def resid_add(xb, d, b):
            P.add("dve", lambda e: e.tensor_tensor(out=xres[xb][:, d, :], in0=xres[xb][:, d, :], in1=ps[:, b, :],
                                                   op=ALU.add), reads=[PSB[b], XC[xb][d]], writes=[XC[xb][d]])

        def emit_mixer0(xb):
            nproj = {"n": 0}

            def proj(blk):
                s = ring_load("win", blk, 1024)
                pairs = [(ring[s][:, k * 128:(k + 1) * 128], hbf[:, k, :]) for k in range(KC)]
                nproj["n"] += 1
                if nproj["n"] == 1:
                    return mm_group(pairs, [RING[s]], split_reads=HB)
                return mm_group(pairs, [RING[s]] + HB)
            for j in range(4):
                bgate = proj(4 + j)
                bval = proj(j)
                ta = rot("tmp", NTMP)
                P.add("act", lambda e, ta=ta, bgate=bgate: e.activation(out=tmp[ta][:, :], in_=ps[:, bgate, :],
                                                                        func=AF.Tanh, scale=0.5),
                      reads=[PSB[bgate]], writes=[TMP[ta]])
                tb = rot("tmp", NTMP)
                P.add("act", lambda e, tb=tb, bval=bval: e.activation(out=tmp[tb][:, :], in_=ps[:, bval, :],
                                                                      func=AF.Copy, scale=0.5),
                      reads=[PSB[bval]], writes=[TMP[tb]])
                P.add("dve", lambda e, ta=ta, tb=tb, j=j: e.scalar_tensor_tensor(
                    out=glu[:, j, 30:30 + T], in0=tmp[ta][:, :], scalar=1.0, in1=tmp[tb][:, :],
                    op0=ALU.add, op1=ALU.mult), reads=[TMP[ta], TMP[tb]], writes=[GLU[j]])
            for j in range(4):
                s0 = ring_load("wcv", 2 * j, 2048)
                s1 = ring_load("wcv", 2 * j + 1, 2048)
                pairs = []
                for i in range(31):
                    sl = s0 if i < 16 else s1
                    ii = i % 16
                    pairs.append((ring[sl][:, ii * 128:(ii + 1) * 128], glu[:, j, i:i + T]))
                b = mm_group(pairs, [RING[s0], RING[s1], GLU[j]])
                P.add("act", lambda e, j=j, b=b: e.activation(out=af_ap(j), in_=ps[:, b, :], func=AF.Copy),
                      reads=[PSB[b]], writes=[AF_R(j)])
                P.add("act", lambda e, j=j, b=b: e.activation(out=abf[:, j, :], in_=ps[:, b, :], func=AF.Copy),
                      reads=[PSB[b]], writes=[ABF[j]])
                P.add("act", lambda e, j=j, b=b: e.activation(out=asq[:, j, :], in_=ps[:, b, :], func=AF.Square),
                      reads=[PSB[b]], writes=[ASQ[j]])
                P.add("pool", lambda e, j=j: e.tensor_copy(out=glu[:, j, 0:30], in_=glu[:, j, T:T + 30]),
                      reads=[GLU[j]], writes=[GLU[j]])

            bbank = {}

            def b_pe(j):
                bc = proj(12 + j)
                bbc = proj(16 + j)
                bb = proj(8 + j)
                bbank[j] = (bc, bbc, bb)

            def b_ew(j):
                bc, bbc, bb = bbank[j]
                tc = rot("tmp", NTMP)
                ui = j // 2
                P.add("pool", lambda e: e.tensor_copy(out=cv_ap(j, 0, 2), in_=bhal[:, j, :]),
                      reads=[BHAL[j]], writes=[UBH[ui]])
                P.add("act", lambda e: e.activation(out=tmp[tc][:, :], in_=ps[:, bc, :], func=AF.Copy),
                      reads=[PSB[bc]], writes=[TMP[tc]])
                P.add("dve", lambda e: e.tensor_tensor(out=cv_ap(j, 2, 2 + T), in0=tmp[tc][:, :], in1=ps[:, bbc, :],
                                                       op=ALU.mult), reads=[TMP[tc], PSB[bbc]], writes=[CV_R(j)])
                P.add("pool", lambda e: e.tensor_copy(out=bhal[:, j, :], in_=cv_ap(j, T, T + 2)),
                      reads=[CV_R(j)], writes=[BHAL[j]])
                wc = PC_CONVB + 3 * j
                P.add("dve", lambda e: e.tensor_scalar(out=accb[:, :], in0=cv_ap(j, 2, 2 + T), scalar1=pcol(wc + 2),
                                                       scalar2=None, op0=ALU.mult),
                      reads=[CV_R(j), PAR], writes=[ACCB])
                for tap in (1, 0):
                    P.add("dve", lambda e, tap=tap: e.scalar_tensor_tensor(
                        out=accb[:, :], in0=cv_ap(j, tap, tap + T), scalar=pcol(wc + tap), in1=accb[:, :],
                        op0=ALU.mult, op1=ALU.add), reads=[CV_R(j), UBH[ui], PAR, ACCB], writes=[ACCB])
                P.add("dve", lambda e: e.tensor_tensor(out=yab[:, 4 + j, :], in0=accb[:, :], in1=ps[:, bb, :],
                                                       op=ALU.mult), reads=[ACCB, PSB[bb]], writes=[YAB[4 + j]])

            for j in (0, 1):
                b_pe(j)
                b_ew(j)
            b1 = mm_group([(ones[:, :], abf[:, j, :]) for j in range(4)], [ONES] + ABF)
            b2 = mm_group([(ones[:, :], asq[:, j, :]) for j in range(4)], [ONES] + ASQ)
            P.add("dve", lambda e: e.tensor_scalar(out=mean[:, :], in0=ps[:, b1, :], scalar1=1.0 / 512, scalar2=None,
                                                   op0=ALU.mult), reads=[PSB[b1]], writes=[MEAN])
            tm = rot("tmp", NTMP)
            P.add("dve", lambda e: e.tensor_tensor(out=tmp[tm][:, :], in0=mean[:, :], in1=mean[:, :], op=ALU.mult),
                  reads=[MEAN], writes=[TMP[tm]])
            P.add("dve", lambda e: e.scalar_tensor_tensor(out=var[:, :], in0=ps[:, b2, :], scalar=1.0 / 512,
                                                          in1=tmp[tm][:, :], op0=ALU.mult, op1=ALU.subtract),
                  reads=[PSB[b2], TMP[tm]], writes=[VAR])
            P.add("act", lambda e: e.activation(out=ssum[:, :], in_=var[:, :], func=AF.Ln, scale=1.0,
                                                bias=epsc[:, 1:2]), reads=[VAR, EPSC], writes=[SS])
            P.add("act", lambda e: e.activation(out=rstd2[:, :], in_=ssum[:, :], func=AF.Exp, scale=-0.5),
                  reads=[SS], writes=[RS2])
            for j in (2, 3):
                b_pe(j)
            for j in range(4):
                t1 = rot("tmp", NTMP)
                P.add("dve", lambda e, j=j, t1=t1: e.tensor_tensor(out=tmp[t1][:, :], in0=af_ap(j), in1=mean[:, :],
                                                                   op=ALU.subtract),
                      reads=[AF_R(j), MEAN], writes=[TMP[t1]])
                t2 = rot("tmp", NTMP)
                P.add("dve", lambda e, t1=t1, t2=t2: e.tensor_tensor(out=tmp[t2][:, :], in0=tmp[t1][:, :],
                                                                     in1=rstd2[:, :], op=ALU.mult),
                      reads=[TMP[t1], RS2], writes=[TMP[t2]])
                P.add("act", lambda e, j=j, t2=t2: e.activation(out=yab[:, j, :], in_=tmp[t2][:, :], func=AF.Silu,
                                                                scale=pcol(PC_LNG + j), bias=pcol(PC_LNB + j)),
                      reads=[TMP[t2], PAR], writes=[YAB[j]])
            act_table_prefetch()
            for j in (2, 3):
                b_ew(j)
            for d in range(KC):
                s = ring_load("wout", d, 1024)
                b = mm_group([(ring[s][:, k * 128:(k + 1) * 128], yab[:, k, :]) for k in range(KC)],
                             [RING[s]] + YAB)
                resid_add(xb, d, b)
                stats_add(xb, d, d == 0, d == KC - 1)

        def emit_ffn(xb, L, hoist=None):
            stats_finish()
            h_ops(xb, PC_GFFN0 if L == 0 else PC_GFFN1)
            kup = "wup0" if L == 0 else "wup1"
            kdn = "wdn0" if L == 0 else "wdn1"
            for d in range(NRA):
                P.add("sp", lambda e, d=d: e.dma_start(out=ringA[d][:, :], in_=wbf[kdn][d]),
                      reads=[WSG[GRP[(kdn, d)]]], writes=[RINGA[d]], dma=f"ringA{d}")
            slots = {}

            def stage1(f):
                s = ring_load(kup, f, 2048)
                if f == 0:
                    bg = mm_group([(ring[s][:, k * 256:k * 256 + 128], hbf[:, k, :]) for k in range(KC)],
                                  [RING[s]], nbank=4, split_reads=HB)
                else:
                    bg = mm_group([(ring[s][:, k * 256:k * 256 + 128], hbf[:, k, :]) for k in range(KC)],
                                  [RING[s]] + HB, nbank=4)
                bv = mm_group([(ring[s][:, k * 256 + 128:k * 256 + 256], hbf[:, k, :]) for k in range(KC)],
                              [RING[s]] + HB, nbank=4)
                ui = rot("ub", NUB)
                ci = rot("cb", NCB)
                slots[f] = ci
                hi = L * NF + f
                wg = PC_CFFN + (L * 44 + f) * 3
                wv = PC_CFFN + (L * 44 + NF + f) * 3
                P.add("pool", lambda e: e.tensor_copy(out=ubuf[ui][:, :, 0:2], in_=halF[:, hi, :, :]),
                      reads=[HALF[hi]], writes=[UBH[ui]])
                for q, bq, wq, R, CR in ((0, bg, wg, UBG, CBG), (1, bv, wv, UBV, CBV)):
                    P.add("act", lambda e, q=q, bq=bq: e.activation(out=ubuf[ui][:, q, 2:2 + T], in_=ps[:, bq, :],
                                                                    func=AF.Copy), reads=[PSB[bq]], writes=[R[ui]])
                    P.add("act", lambda e, q=q, bq=bq, wq=wq: e.activation(out=cbuf[ci][:, q, :], in_=ps[:, bq, :],
                                                                           func=AF.Copy, scale=pcol(wq + 2)),
                          reads=[PSB[bq], PAR], writes=[CR[ci]])
                P.add("pool", lambda e: e.tensor_copy(out=halF[:, hi, :, :], in_=ubuf[ui][:, :, T:T + 2]),
                      reads=[UBG[ui], UBV[ui]], writes=[HALF[hi]])
                for tap in (1, 0):
                    for q, wq, R, CR in ((0, wg, UBG, CBG), (1, wv, UBV, CBV)):
                        P.add("dve", lambda e, q=q, wq=wq, tap=tap: e.scalar_tensor_tensor(
                            out=cbuf[ci][:, q, :], in0=ubuf[ui][:, q, tap:tap + T], scalar=pcol(wq + tap),
                            in1=cbuf[ci][:, q, :], op0=ALU.mult, op1=ALU.add),
                            reads=[R[ui], UBH[ui], PAR, CR[ci]], writes=[CR[ci]])

            def stage2(f):
                ci = slots[f]
                si = rot("sg", NSG)
                P.add("act", lambda e: e.activation(out=sgb[si][:, :], in_=cbuf[ci][:, 0, :], func=AF.Silu),
                      reads=[CBG[ci]], writes=[SG[si]])
                P.add("pool", lambda e: e.tensor_tensor(out=actb[:, f, :], in0=sgb[si][:, :], in1=cbuf[ci][:, 1, :],
                                                        op=ALU.mult), reads=[SG[si], CBV[ci]], writes=[ACTR[f]])

            def stage3(f):
                pe_op([(4 + d, ringA[d][:, f * 128:(f + 1) * 128], actb[:, f, :], f == 0, f == NF - 1)
                       for d in range(NRA)], RINGA + [ACTR[f]], [4 + d for d in range(NRA)])

            for f in range(NF):
                stage1(f)
                if f >= 1:
                    stage2(f - 1)
                if f >= 2:
                    stage3(f - 2)
            stage2(NF - 1)
            act_table_prefetch()
            if hoist is not None:
                hoist()
            stage3(NF - 2)

            def second(d, first, last):
                s = ring_load(kdn, d, 2816)
                b = mm_group([(ring[s][:, f * 128:(f + 1) * 128], actb[:, f, :]) for f in range(NF)],
                             [RING[s]] + ACTR, nbank=4)
                resid_add(xb, d, b)
                stats_add(xb, d, first, last)

            for d in range(NRA, KC - 2):
                second(d, d == NRA, False)
            stage3(NF - 1)
            for d in range(NRA):
                resid_add(xb, d, 4 + d)
                stats_add(xb, d, False, False)
            for d in range(KC - 2, KC):
                second(d, False, d == KC - 1)

        def emit_mixer1(xb, first_tile):
            stats_finish()
            L_ = T + 16
            for gi, w in enumerate(POOL_W):
                c0 = 2 * gi
                P.add("pool", lambda e, c0=c0: e.tensor_copy(out=h1g[:, :, 0:16], in_=h1hal[:, c0:c0 + 2, :]),
                      reads=[H1HAL[gi]], writes=[H1GH])
                for q in range(2):
                    P.add("dve", lambda e, q=q, c0=c0: e.scalar_tensor_tensor(
                        out=h1g[:, q, 16:16 + T], in0=xres[xb][:, c0 + q, :], scalar=pcol(PC_GMIX1 + c0 + q),
                        in1=rstd[:, :], op0=ALU.mult, op1=ALU.mult),
                        reads=[XC[xb][c0 + q], RS, PAR], writes=[H1G])
                P.add("pool", lambda e, c0=c0: e.tensor_copy(out=h1hal[:, c0:c0 + 2, :], in_=h1g[:, :, T:T + 16]),
                      reads=[H1G], writes=[H1HAL[gi]])
                src, srcR = h1g, [H1G, H1GH]
                bufs = [(sA, SA_), (sB, SB_)]
                step, lo = 1, 0
                nb = 0
                while step < w:
                    dst, dstR = bufs[nb % 2]
                    nlo = lo + step
                    P.add("dve", lambda e, src=src, dst=dst, nlo=nlo, step=step: e.tensor_tensor(
                        out=dst[:, :, nlo:L_], in0=src[:, :, nlo:L_], in1=src[:, :, nlo - step:L_ - step], op=ALU.add),
                        reads=srcR, writes=[dstR])
                    src, srcR = dst, [dstR]
                    lo = nlo
                    step *= 2
                    nb += 1
                P.add("dve", lambda e, src=src, c0=c0, w=w: e.scalar_tensor_tensor(
                    out=yab[:, c0:c0 + 2, :], in0=src[:, :, 16:16 + T], scalar=1.0 / w, in1=h1g[:, :, 16:16 + T],
                    op0=ALU.mult, op1=ALU.subtract), reads=srcR + [H1G], writes=[YAB[c0], YAB[c0 + 1]])
                if first_tile:
                    for q in range(2):
                        P.add("dve", lambda e, src=src, q=q, gi=gi: e.tensor_tensor(
                            out=t16[:, q, :], in0=src[:, q, 16:32], in1=invc[:, gi, :], op=ALU.mult),
                            reads=srcR + [INVC], writes=[T16])
                    P.add("dve", lambda e, c0=c0: e.tensor_tensor(out=yab[:, c0:c0 + 2, 0:16], in0=t16[:, :, :],
                                                                  in1=h1g[:, :, 16:32], op=ALU.subtract),
                          reads=[T16, H1G], writes=[YAB[c0], YAB[c0 + 1]])
                for oc in range(2):
                    pairs = []
                    for k in range(2):
                        o = ((gi * 2 + k) * 256) + oc * 128
                        pairs.append((wpool[:, o:o + 128], yab[:, c0 + k, :]))
                    b = mm_group(pairs, [WPOOL, YAB[c0], YAB[c0 + 1]])
                    c = c0 + oc
                    P.add("dve", lambda e, b=b, c=c: e.scalar_tensor_tensor(
                        out=xres[xb][:, c, :], in0=ps[:, b, :], scalar=pcol(PC_PSCALE + c), in1=xres[xb][:, c, :],
                        op0=ALU.mult, op1=ALU.add), reads=[PSB[b], PAR, XC[xb][c]], writes=[XC[xb][c]])
                    stats_add(xb, c, c == 0, c == KC - 1)

        def emit_final(xb):
            stats_finish()
            for c in range(KC):
                P.add("dve", lambda e, c=c: e.scalar_tensor_tensor(out=xres[xb][:, c, :], in0=xres[xb][:, c, :],
                                                                   scalar=pcol(PC_GFIN + c), in1=rstd[:, :],
                                                                   op0=ALU.mult, op1=ALU.mult),
                      reads=[XC[xb][c], RS, PAR], writes=[XC[xb][c]])

        store_ops = []

        def emit_store(t):
            xb_ = t % 2
            op = P.add("sp", lambda e: e.dma_start(out=out_l[t].rearrange("p (c t) -> p c t", c=KC),
                                                   in_=xres[xb_][:, :, :]), reads=XC[xb_], dma=f"st{xb_}")
            store_ops.append(op)

        def emit_xload(t):
            xb_ = t % 2
            P.add("sp", lambda e: e.dma_start(out=xres[xb_][:, :, :],
                                              in_=x_l[t].rearrange("p (c t) -> p c t", c=KC)),
                  writes=XC[xb_], dma=f"x{xb_}")

        norm_full(0, PC_GMIX0)
        for t in range(nt_run):
            xb = t % 2
            emit_mixer0(xb)
            if t >= 1:
                emit_store(t - 1)
            if t == 0:
                emit_conv(["wup0b", "wdn0", "wpool", "wup1a"])
            emit_ffn(xb, 0)
            if t + 1 < nt_run:
                emit_xload(t + 1)
            if t == 0:
                emit_conv(["wup1b", "wdn1"])
                P.add("sp", lambda e: e.dma_start(out=wpool[:, :], in_=wbf["wpool"][0]), reads=[WSG["wpool"]],
                      writes=[WPOOL], dma="wpool")
            emit_mixer1(xb, t == 0)
            hoist = None
            if t + 1 < nt_run:
                hoist = (lambda nb=(t + 1) % 2: norm_full(nb, PC_GMIX0))
            emit_ffn(xb, 1, hoist=hoist)
            emit_final(xb)
        emit_store(nt_run - 1)
        fence = P.add("sp", None)
        fence.deps = set(store_ops)

        sem_names = P.finalize()
        sems = {n: es.enter_context(nc.semaphore(n)) for n in sem_names}
        with nc.Block() as block:
            @block.tensor
            def _(e):
                P.emit("pe", e, sems)

            @block.scalar
            def _(e):
                P.emit("act", e, sems)

            @block.vector
            def _(e):
                P.emit("dve", e, sems)

            @block.gpsimd
            def _(e):
                P.emit("pool", e, sems)

            @block.sync
            def _(e):
                P.emit("sp", e, sems)
    return nc


def _blk(w, bc):
    K, N = w.shape
    nk = K // 128
    return np.ascontiguousarray(w.reshape(nk, 128, N // bc, bc).transpose(2, 1, 0, 3)).reshape(N // bc, 128, nk * bc)


def _col(v):
    return np.ascontiguousarray(v.reshape(-1, 128).T)


def prepare_inputs(x, norm_mix_even, w_in, conv_a, ln_a_g, ln_a_b, conv_b, w_out, norm_mix_odd, w_pool,
                   pool_scale, norm_ffn, w_up, conv_ffn_w, w_down, norm_final):
    f = np.float32
    par = np.zeros((128, NPAR), f)
    par[:, PC_GMIX0:PC_GMIX0 + 8] = _col(norm_mix_even[0])
    par[:, PC_GFFN0:PC_GFFN0 + 8] = _col(norm_ffn[0])
    par[:, PC_GMIX1:PC_GMIX1 + 8] = _col(norm_mix_odd[0])
    par[:, PC_GFFN1:PC_GFFN1 + 8] = _col(norm_ffn[1])
    par[:, PC_GFIN:PC_GFIN + 8] = _col(norm_final)
    par[:, PC_LNG:PC_LNG + 4] = _col(ln_a_g[0])
    par[:, PC_LNB:PC_LNB + 4] = _col(ln_a_b[0])
    par[:, PC_CONVB:PC_CONVB + 12] = conv_b[0].reshape(3, 4, 128).transpose(2, 1, 0).reshape(128, 12)
    par[:, PC_PSCALE:PC_PSCALE + 8] = _col(pool_scale[0])
    for L in range(2):
        c = conv_ffn_w[L].reshape(3, 44, 128).transpose(2, 1, 0).reshape(128, 132)
        par[:, PC_CFFN + L * 132:PC_CFFN + (L + 1) * 132] = c
    shared = {"par": par}
    shared["win32"] = _blk(w_in[0], 128)
    shared["wout32"] = _blk(w_out[0], 128)
    wcv = np.zeros((4, 2, 128, 16, 128), f)
    ca = conv_a[0].reshape(31, 4, 128)
    idx = np.arange(128)
    for i in range(31):
        for j in range(4):
            wcv[j, i // 16, idx, i % 16, idx] = ca[i, j]
    shared["wcv32"] = wcv.reshape(8, 128, 2048)
    for L in range(2):
        wu = w_up[L]
        g = wu[:, :DFF].reshape(D, NF, 128)
        v = wu[:, DFF:].reshape(D, NF, 128)
        gv = np.concatenate([g, v], axis=2).reshape(D, NF * 256)
        shared[f"wup{L}32"] = _blk(gv, 256)
        shared[f"wdn{L}32"] = _blk(w_down[L], 128)
    shared["wpool32"] = np.ascontiguousarray(w_pool[0].reshape(4, 2, 128, 256).transpose(2, 0, 1, 3)).reshape(1, 128, 2048)
    in_maps = []
    for b in range(x.shape[0]):
        xl = np.ascontiguousarray(x[b].reshape(NT, T, KC, 128).transpose(0, 3, 2, 1)).reshape(NT, 128, KC * T)
        m = dict(shared)
        m["x_l"] = xl
        in_maps.append(m)
    return in_maps


def assemble_output(res_list):
    outs = []
    for r in res_list:
        o = r["out_l"].reshape(NT, 128, KC, T).transpose(0, 3, 2, 1).reshape(S, D)
        outs.append(o)
    return np.ascontiguousarray(np.stack(outs, axis=0)).astype(np.float32)


def kernel(**inputs):
    inputs = {k: np.asarray(v, dtype=np.float32) for k, v in inputs.items()}
    in_maps = prepare_inputs(**inputs)
    nc = build_program()
    res = run_bass_kernel_spmd(nc, in_maps, core_ids=list(range(NCORE)))
    return assemble_output(res.results)
```

```python
import numpy as np
import concourse.bass as bass
import concourse.mybir as mybir
from concourse.bass_utils import run_bass_kernel_spmd

F32 = mybir.dt.float32
BF16 = mybir.dt.bfloat16
ALU = mybir.AluOpType
AF = mybir.ActivationFunctionType

D = 1024
S = 4096
NCORE = 8
DFF = 2816
NF = DFF // 128
T = 512
NT = S // T
KC = D // 128
RMS_EPS = 1e-6
LN_EPS = 1e-5
NSLOT = 5
NRA = 3
STATB = 7
RINGW = 2816
POOL_W = (2, 4, 8, 16)

PC_GMIX0, PC_GFFN0, PC_GMIX1, PC_GFFN1, PC_GFIN = 0, 8, 16, 24, 32
PC_LNG, PC_LNB = 40, 44
PC_CONVB = 48
PC_PSCALE = 60
PC_CFFN = 68
NPAR = PC_CFFN + 2 * 44 * 3


class Res:
    __slots__ = ("name", "w", "r")

    def __init__(self, name):
        self.name = name
        self.w = None
        self.r = []


class Op:
    __slots__ = ("eng", "idx", "fn", "deps", "sig", "sem", "val", "dma")


class Prog:
    ENGS = ("pe", "act", "dve", "pool", "sp")

    def __init__(self):
        self.ops = {e: [] for e in self.ENGS}
        self.dma_count = {}

    def add(self, eng, fn, reads=(), writes=(), dma=None):
        op = Op()
        op.eng = eng
        op.idx = len(self.ops[eng])
        op.fn = fn
        op.sig = False
        op.dma = dma
        op.sem = None
        op.val = 0
        deps = set()
        for r in reads:
            if r.w is not None:
                deps.add(r.w)
        for w in writes:
            if w.w is not None:
                deps.add(w.w)
            for rd in w.r:
                deps.add(rd)
        deps.discard(op)
        op.deps = deps
        for r in reads:
            r.r.append(op)
        for w in writes:
            w.w = op
            w.r = []
        self.ops[eng].append(op)
        return op

    def _needs_wait(self, op, dep):
        if dep.dma is not None:
            return True
        if dep.eng != op.eng:
            return True
        if op.eng == "pe":
            return False
        return (op.idx - dep.idx) <= 1

    def finalize(self):
        for e in self.ENGS:
            for op in self.ops[e]:
                for d in op.deps:
                    if self._needs_wait(op, d):
                        d.sig = True
        dma_keys = []
        for e in self.ENGS:
            cnt = 0
            for op in self.ops[e]:
                if op.dma is not None:
                    if op.dma not in self.dma_count:
                        self.dma_count[op.dma] = 0
                        dma_keys.append(op.dma)
                    self.dma_count[op.dma] += 16
                    op.sem = "dma_" + op.dma
                    op.val = self.dma_count[op.dma]
                elif op.sig:
                    cnt += 1
                    op.sem = "eng_" + e
                    op.val = cnt
        return ["eng_" + e for e in self.ENGS if e != "sp"] + ["dma_" + k for k in dma_keys]

    def emit(self, eng_name, engine, sems):
        seen = {}
        nwait = 0
        for op in self.ops[eng_name]:
            need = {}
            for d in op.deps:
                if not self._needs_wait(op, d):
                    continue
                if seen.get(d.sem, 0) >= d.val:
                    continue
                if need.get(d.sem, 0) < d.val:
                    need[d.sem] = d.val
            for sname, v in need.items():
                engine.wait_ge(sems[sname], v)
                seen[sname] = v
                nwait += 1
            if op.fn is None:
                continue
            ins = op.fn(engine)
            if op.dma is not None:
                ins.then_inc(sems[op.sem], 16)
            elif op.sig:
                ins.then_inc(sems[op.sem], 1)
        return nwait


def build_program(nt_run=NT, stop_after="final"):
    nc = bass.Bass("TRN2", target_bir_lowering=False)
    P = Prog()

    def dram(name, shape, dt, kind):
        return nc.dram_tensor(name, shape, dt, kind=kind).ap()

    x_l = dram("x_l", [NT, 128, KC * T], F32, "ExternalInput")
    par_d = dram("par", [128, NPAR], F32, "ExternalInput")
    w32 = {
        "win": dram("win32", [20, 128, 1024], F32, "ExternalInput"),
        "wcv": dram("wcv32", [8, 128, 2048], F32, "ExternalInput"),
        "wout": dram("wout32", [8, 128, 1024], F32, "ExternalInput"),
        "wup0": dram("wup032", [NF, 128, 2048], F32, "ExternalInput"),
        "wdn0": dram("wdn032", [8, 128, 2816], F32, "ExternalInput"),
        "wpool": dram("wpool32", [1, 128, 2048], F32, "ExternalInput"),
        "wup1": dram("wup132", [NF, 128, 2048], F32, "ExternalInput"),
        "wdn1": dram("wdn132", [8, 128, 2816], F32, "ExternalInput"),
    }
    wbf = {k: dram(k + "bf", list(v.shape), BF16, "Internal") for k, v in w32.items()}
    out_l = dram("out_l", [NT, 128, KC * T], F32, "ExternalOutput")

    from contextlib import ExitStack
    es = ExitStack()

    def sb(name, shape, dt):
        return es.enter_context(nc.sbuf_tensor(name, shape, dt))

    with es:
        xres = [sb(f"xres{i}", [128, KC, T], F32) for i in range(2)]
        hbf = sb("hbf", [128, KC, T], BF16)
        NSQ = 3
        sqbf = sb("sqbf", [128, NSQ, T], BF16)
        actb = sb("actb", [128, NF, T], BF16)
        yab = sb("yab", [128, KC, T], BF16)
        NUB, NCB, NSG, NTMP = 2, 2, 2, 3
        ubuf = [sb(f"ubuf{i}", [128, 2, T + 2], F32) for i in range(NUB)]
        cbuf = [sb(f"cbuf{i}", [128, 2, T], F32) for i in range(NCB)]
        sgb = [sb(f"sgb{i}", [128, T], F32) for i in range(NSG)]
        tmp = [sb(f"tmp{i}", [128, T], F32) for i in range(NTMP)]
        ring = [sb(f"ring{i}", [128, RINGW], BF16) for i in range(NSLOT)]
        ringA = [sb(f"ringA{i}", [128, RINGW], BF16) for i in range(NRA)]
        ssum = sb("ssum", [128, T], F32)
        rstd = sb("rstd", [128, T], F32)
        mean = sb("mean", [128, T], F32)
        var = sb("var", [128, T], F32)
        rstd2 = sb("rstd2", [128, T], F32)
        glu = sb("glu", [128, 4, T + 30], BF16)
        abf = sb("abf", [128, 4, T], BF16)
        asq = sb("asq", [128, 4, T], BF16)
        accb = sb("accb", [128, T], F32)
        h1g = sb("h1g", [128, 2, T + 16], F32)
        sA = sb("sA", [128, 2, T + 16], F32)
        sB = sb("sB", [128, 2, T + 16], F32)
        t16 = sb("t16", [128, 2, 16], F32)
        halF = sb("halF", [128, 2 * NF, 2, 2], F32)
        bhal = sb("bhal", [128, 4, 2], F32)
        h1hal = sb("h1hal", [128, KC, 16], F32)
        par = sb("par_sb", [128, NPAR], F32)
        ones = sb("ones", [128, 128], BF16)
        epsc = sb("epsc", [128, 2], F32)
        dummy = sb("actdummy", [128, 2], F32)
        invc = sb("invc", [128, 4, 16], F32)
        wpool = sb("wpool_sb", [128, 2048], BF16)
        ps = es.enter_context(nc.psum_tensor("ps", [128, 8, T], F32))

        XC = [[Res(f"x{b}_{c}") for c in range(KC)] for b in range(2)]
        HB = [Res(f"hbf{c}") for c in range(KC)]
        SQ = [Res(f"sq{i}") for i in range(NSQ)]
        ACTR = [Res(f"act{f}") for f in range(NF)]
        YAB = [Res(f"yab{c}") for c in range(KC)]
        UBG = [Res(f"ubg{i}") for i in range(NUB)]
        UBV = [Res(f"ubv{i}") for i in range(NUB)]
        UBH = [Res(f"ubh{i}") for i in range(NUB)]
        CBG = [Res(f"cbg{i}") for i in range(NCB)]
        CBV = [Res(f"cbv{i}") for i in range(NCB)]
        SG = [Res(f"sg{i}") for i in range(NSG)]
        TMP = [Res(f"tmp{i}") for i in range(NTMP)]
        RING = [Res(f"ring{i}") for i in range(NSLOT)]
        RINGA = [Res(f"ringA{i}") for i in range(NRA)]
        PSB = [Res(f"psb{i}") for i in range(8)]
        SS, RS, MEAN, VAR, RS2 = Res("ss"), Res("rs"), Res("mean"), Res("var"), Res("rs2")
        GLU = [Res(f"glu{j}") for j in range(4)]
        ABF = [Res(f"abf{j}") for j in range(4)]
        ASQ = [Res(f"asq{j}") for j in range(4)]
        ACCB = Res("accb")
        H1G, H1GH, SA_, SB_, T16 = Res("h1g"), Res("h1gh"), Res("sA"), Res("sB"), Res("t16")
        HALF = [Res(f"half{i}") for i in range(2 * NF)]
        BHAL = [Res(f"bhal{j}") for j in range(4)]
        H1HAL = [Res(f"h1hal{g}") for g in range(4)]
        DUMMY = Res("dummy")
        PAR, ONES, EPSC, INVC, WPOOL = Res("par"), Res("ones"), Res("epsc"), Res("invc"), Res("wpool")

        def af_ap(j):
            return cbuf[j // 2][:, j % 2, :]

        def AF_R(j):
            return (CBG if j % 2 == 0 else CBV)[j // 2]

        def cv_ap(j, lo, hi):
            return ubuf[j // 2][:, j % 2, lo:hi]

        def CV_R(j):
            return (UBG if j % 2 == 0 else UBV)[j // 2]

        state = {"bank": 0, "bank4": 0, "ring": 0, "ub": 0, "cb": 0, "sg": 0, "tmp": 0, "sq": 0}

        def rot(key, n):
            v = state[key] % n
            state[key] += 1
            return v

        def pcol(c):
            return par[:, c:c + 1]

        P.add("sp", lambda e: e.dma_start(out=par[:, :], in_=par_d[:, :]), writes=[PAR], dma="par")
        P.add("sp", lambda e: e.dma_start(out=xres[0][:, :, :], in_=x_l[0].rearrange("p (c t) -> p c t", c=KC)),
              writes=XC[0], dma="x0")
        P.add("dve", lambda e: e.memset(ones[:, :], 1.0), writes=[ONES])

        def eps_fill(e):
            e.memset(epsc[:, 0:1], RMS_EPS)
            return e.memset(epsc[:, 1:2], LN_EPS)
        P.add("dve", eps_fill, writes=[EPSC])
        P.add("dve", lambda e: e.memset(glu[:, :, 0:30], 0.0), writes=GLU)
        P.add("dve", lambda e: e.memset(bhal[:, :, :], 0.0), writes=BHAL)
        P.add("dve", lambda e: e.memset(halF[:, :, :, :], 0.0), writes=HALF)
        P.add("dve", lambda e: e.memset(h1hal[:, :, :], 0.0), writes=H1HAL)

        def invc_fill(e):
            ins = None
            for gi, w in enumerate(POOL_W):
                ins = e.memset(invc[:, gi, :], 1.0 / w)
                for j in range(w - 1):
                    ins = e.memset(invc[:, gi, j:j + 1], 1.0 / (j + 1))
            return ins
        P.add("pool", invc_fill, writes=[INVC])

        CVG = {
            "win_a": ("win", range(0, 8)), "win_b": ("win", range(8, 20)), "wcv": ("wcv", range(8)),
            "wout": ("wout", range(8)), "wup0a": ("wup0", range(0, 11)), "wup0b": ("wup0", range(11, NF)),
            "wdn0": ("wdn0", range(8)), "wpool": ("wpool", range(1)), "wup1a": ("wup1", range(0, 11)),
            "wup1b": ("wup1", range(11, NF)), "wdn1": ("wdn1", range(8)),
        }
        GRP = {}
        for g, (k, blks) in CVG.items():
            for b in blks:
                GRP[(k, b)] = g
        WSG = {g: Res("wsg_" + g) for g in CVG}

        conv_ops = []
        CONV_WINDOW = 6

        def emit_conv(groups):
            for g in groups:
                k, blks = CVG[g]
                last = None
                for b in blks:
                    last = P.add("pool", lambda e, k=k, b=b: e.dma_start(out=wbf[k][b], in_=w32[k][b]),
                                 dma="cv_" + g)
                    if len(conv_ops) >= CONV_WINDOW:
                        last.deps.add(conv_ops[-CONV_WINDOW])
                    conv_ops.append(last)
                WSG[g].w = last
                WSG[g].r = []

        emit_conv(["win_a", "wcv", "win_b", "wout", "wup0a"])

        def ring_load(kind, blk, ncols):
            s = rot("ring", NSLOT)
            P.add("sp", lambda e: e.dma_start(out=ring[s][:, 0:ncols], in_=wbf[kind][blk]),
                  reads=[WSG[GRP[(kind, blk)]]], writes=[RING[s]], dma=f"ring{s}")
            return s

        def pe_op(items, reads, banks):
            def fn(e):
                ins = None
                for (b, l, r, st, sp_) in items:
                    ins = e.matmul(ps[:, b, :], lhsT=l, rhs=r, start=st, stop=sp_)
                return ins
            P.add("pe", fn, reads=reads, writes=[PSB[b] for b in banks])

        def mm_group(pairs, reads, nbank=7, split_reads=None):
            b = rot("bank", 7) if nbank == 7 else rot("bank4", 4)
            n = len(pairs)
            if split_reads is None:
                pe_op([(b, l, r, i == 0, i == n - 1) for i, (l, r) in enumerate(pairs)], reads, [b])
            else:
                for i, (l, r) in enumerate(pairs):
                    pe_op([(b, l, r, i == 0, i == n - 1)], reads + [split_reads[i]], [b])
            gcount["n"] += 1
            flush_stats()
            return b

        pend = []
        gcount = {"n": 0}

        def flush_stats(all_=False):
            while pend and (all_ or pend[0][0] < gcount["n"]):
                _, q, first, last = pend.pop(0)
                pe_op([(STATB, ones[:, :], sqbf[:, q, :], first, last)], [ONES, SQ[q]], [STATB])

        def stats_add(xb, c, first, last):
            q = rot("sq", NSQ)
            if any(p_[1] == q for p_ in pend):
                flush_stats(True)
            P.add("act", lambda e: e.activation(out=sqbf[:, q, :], in_=xres[xb][:, c, :], func=AF.Square),
                  reads=[XC[xb][c]], writes=[SQ[q]])
            pend.append((gcount["n"], q, first, last))

        def stats_finish():
            flush_stats(True)
            P.add("act", lambda e: e.activation(out=ssum[:, :], in_=ps[:, STATB, :], func=AF.Ln, scale=1.0 / D,
                                                bias=epsc[:, 0:1]), reads=[PSB[STATB], EPSC], writes=[SS])
            P.add("act", lambda e: e.activation(out=rstd[:, :], in_=ssum[:, :], func=AF.Exp, scale=-0.5),
                  reads=[SS], writes=[RS])

        def act_table_prefetch():
            P.add("act", lambda e: e.activation(out=dummy[:, 0:1], in_=epsc[:, 0:1], func=AF.Ln),
                  reads=[EPSC], writes=[DUMMY])

        def h_ops(xb, gc):
            for c in range(KC):
                eng = "dve"
                P.add(eng, lambda e, c=c: e.scalar_tensor_tensor(out=hbf[:, c, :], in0=xres[xb][:, c, :],
                                                                 scalar=pcol(gc + c), in1=rstd[:, :],
                                                                 op0=ALU.mult, op1=ALU.mult),
                      reads=[XC[xb][c], RS, PAR], writes=[HB[c]])

        def norm_full(xb, gc):
            for c in range(KC):
                stats_add(xb, c, c == 0, c == KC - 1)
            stats_finish()
            h_ops(xb, gc)

        def resid_add(xb, d, b):
            P.add("dve", lambda e: e.tensor_tensor(out=xres[xb][:, d, :], in0=xres[xb][:, d, :], in1=ps[:, b, :],
                                                   op=ALU.add), reads=[PSB[b], XC[xb][d]], writes=[XC[xb][d]])

        def emit_mixer0(xb):
            nproj = {"n": 0}

            def proj(blk):
                s = ring_load("win", blk, 1024)
                pairs = [(ring[s][:, k * 128:(k + 1) * 128], hbf[:, k, :]) for k in range(KC)]
                nproj["n"] += 1
                if nproj["n"] == 1:
                    return mm_group(pairs, [RING[s]], split_reads=HB)
                return mm_group(pairs, [RING[s]] + HB)
            for j in range(4):
                bgate = proj(4 + j)
                bval = proj(j)
                ta = rot("tmp", NTMP)
                P.add("act", lambda e, ta=ta, bgate=bgate: e.activation(out=tmp[ta][:, :], in_=ps[:, bgate, :],
                                                                        func=AF.Tanh, scale=0.5),
                      reads=[PSB[bgate]], writes=[TMP[ta]])
                tb = rot("tmp", NTMP)
                P.add("act", lambda e, tb=tb, bval=bval: e.activation(out=tmp[tb][:, :], in_=ps[:, bval, :],
                                                                      func=AF.Copy, scale=0.5),
                      reads=[PSB[bval]], writes=[TMP[tb]])
                P.add("dve", lambda e, ta=ta, tb=tb, j=j: e.scalar_tensor_tensor(
                    out=glu[:, j, 30:30 + T], in0=tmp[ta][:, :], scalar=1.0, in1=tmp[tb][:, :],
                    op0=ALU.add, op1=ALU.mult), reads=[TMP[ta], TMP[tb]], writes=[GLU[j]])
            for j in range(4):
                s0 = ring_load("wcv", 2 * j, 2048)
                s1 = ring_load("wcv", 2 * j + 1, 2048)
                pairs = []
                for i in range(31):
                    sl = s0 if i < 16 else s1
                    ii = i % 16
                    pairs.append((ring[sl][:, ii * 128:(ii + 1) * 128], glu[:, j, i:i + T]))
                b = mm_group(pairs, [RING[s0], RING[s1], GLU[j]])
                P.add("act", lambda e, j=j, b=b: e.activation(out=af_ap(j), in_=ps[:, b, :], func=AF.Copy),
                      reads=[PSB[b]], writes=[AF_R(j)])
                P.add("act", lambda e, j=j, b=b: e.activation(out=abf[:, j, :], in_=ps[:, b, :], func=AF.Copy),
                      reads=[PSB[b]], writes=[ABF[j]])
                P.add("act", lambda e, j=j, b=b: e.activation(out=asq[:, j, :], in_=ps[:, b, :], func=AF.Square),
                      reads=[PSB[b]], writes=[ASQ[j]])
                P.add("pool", lambda e, j=j: e.tensor_copy(out=glu[:, j, 0:30], in_=glu[:, j, T:T + 30]),
                      reads=[GLU[j]], writes=[GLU[j]])

            bbank = {}

            def b_pe(j):
                bc = proj(12 + j)
                bbc = proj(16 + j)
                bb = proj(8 + j)
                bbank[j] = (bc, bbc, bb)

            def b_ew(j):
                bc, bbc, bb = bbank[j]
                tc = rot("tmp", NTMP)
                ui = j // 2
                P.add("pool", lambda e: e.tensor_copy(out=cv_ap(j, 0, 2), in_=bhal[:, j, :]),
                      reads=[BHAL[j]], writes=[UBH[ui]])
                P.add("act", lambda e: e.activation(out=tmp[tc][:, :], in_=ps[:, bc, :], func=AF.Copy),
                      reads=[PSB[bc]], writes=[TMP[tc]])
                P.add("dve", lambda e: e.tensor_tensor(out=cv_ap(j, 2, 2 + T), in0=tmp[tc][:, :], in1=ps[:, bbc, :],
                                                       op=ALU.mult), reads=[TMP[tc], PSB[bbc]], writes=[CV_R(j)])
                P.add("pool", lambda e: e.tensor_copy(out=bhal[:, j, :], in_=cv_ap(j, T, T + 2)),
                      reads=[CV_R(j)], writes=[BHAL[j]])
                wc = PC_CONVB + 3 * j
                P.add("dve", lambda e: e.tensor_scalar(out=accb[:, :], in0=cv_ap(j, 2, 2 + T), scalar1=pcol(wc + 2),
                                                       scalar2=None, op0=ALU.mult),
                      reads=[CV_R(j), PAR], writes=[ACCB])
                for tap in (1, 0):
                    P.add("dve", lambda e, tap=tap: e.scalar_tensor_tensor(
                        out=accb[:, :], in0=cv_ap(j, tap, tap + T), scalar=pcol(wc + tap), in1=accb[:, :],
                        op0=ALU.mult, op1=ALU.add), reads=[CV_R(j), UBH[ui], PAR, ACCB], writes=[ACCB])
                P.add("dve", lambda e: e.tensor_tensor(out=yab[:, 4 + j, :], in0=accb[:, :], in1=ps[:, bb, :],
                                                       op=ALU.mult), reads=[ACCB, PSB[bb]], writes=[YAB[4 + j]])

            for j in (0, 1):
                b_pe(j)
                b_ew(j)
            b1 = mm_group([(ones[:, :], abf[:, j, :]) for j in range(4)], [ONES] + ABF)
            b2 = mm_group([(ones[:, :], asq[:, j, :]) for j in range(4)], [ONES] + ASQ)
            P.add("dve", lambda e: e.tensor_scalar(out=mean[:, :], in0=ps[:, b1, :], scalar1=1.0 / 512, scalar2=None,
                                                   op0=ALU.mult), reads=[PSB[b1]], writes=[MEAN])
            tm = rot("tmp", NTMP)
            P.add("dve", lambda e: e.tensor_tensor(out=tmp[tm][:, :], in0=mean[:, :], in1=mean[:, :], op=ALU.mult),
                  reads=[MEAN], writes=[TMP[tm]])
            P.add("dve", lambda e: e.scalar_tensor_tensor(out=var[:, :], in0=ps[:, b2, :], scalar=1.0 / 512,
                                                          in1=tmp[tm][:, :], op0=ALU.mult, op1=ALU.subtract),
                  reads=[PSB[b2], TMP[tm]], writes=[VAR])
            P.add("act", lambda e: e.activation(out=ssum[:, :], in_=var[:, :], func=AF.Ln, scale=1.0,
                                                bias=epsc[:, 1:2]), reads=[VAR, EPSC], writes=[SS])
            P.add("act", lambda e: e.activation(out=rstd2[:, :], in_=ssum[:, :], func=AF.Exp, scale=-0.5),
                  reads=[SS], writes=[RS2])
            for j in (2, 3):
                b_pe(j)
            for j in range(4):
                t1 = rot("tmp", NTMP)
                P.add("dve", lambda e, j=j, t1=t1: e.tensor_tensor(out=tmp[t1][:, :], in0=af_ap(j), in1=mean[:, :],
                                                                   op=ALU.subtract),
                      reads=[AF_R(j), MEAN], writes=[TMP[t1]])
                t2 = rot("tmp", NTMP)
                P.add("dve", lambda e, t1=t1, t2=t2: e.tensor_tensor(out=tmp[t2][:, :], in0=tmp[t1][:, :],
                                                                     in1=rstd2[:, :], op=ALU.mult),
                      reads=[TMP[t1], RS2], writes=[TMP[t2]])
                P.add("act", lambda e, j=j, t2=t2: e.activation(out=yab[:, j, :], in_=tmp[t2][:, :], func=AF.Silu,
                                                                scale=pcol(PC_LNG + j), bias=pcol(PC_LNB + j)),
                      reads=[TMP[t2], PAR], writes=[YAB[j]])
            act_table_prefetch()
            for j in (2, 3):
                b_ew(j)
            for d in range(KC):
                s = ring_load("wout", d, 1024)
                b = mm_group([(ring[s][:, k * 128:(k + 1) * 128], yab[:, k, :]) for k in range(KC)],
                             [RING[s]] + YAB)
                resid_add(xb, d, b)
                stats_add(xb, d, d == 0, d == KC - 1)

        def emit_ffn(xb, L, hoist=None):
            stats_finish()
            h_ops(xb, PC_GFFN0 if L == 0 else PC_GFFN1)
            kup = "wup0" if L == 0 else "wup1"
            kdn = "wdn0" if L == 0 else "wdn1"
            for d in range(NRA):
                P.add("sp", lambda e, d=d: e.dma_start(out=ringA[d][:, :], in_=wbf[kdn][d]),
                      reads=[WSG[GRP[(kdn, d)]]], writes=[RINGA[d]], dma=f"ringA{d}")
            slots = {}

            def stage1(f):
                s = ring_load(kup, f, 2048)
                if f == 0:
                    bg = mm_group([(ring[s][:, k * 256:k * 256 + 128], hbf[:, k, :]) for k in range(KC)],
                                  [RING[s]], nbank=4, split_reads=HB)
                else:
                    bg = mm_group([(ring[s][:, k * 256:k * 256 + 128], hbf[:, k, :]) for k in range(KC)],
                                  [RING[s]] + HB, nbank=4)
                bv = mm_group([(ring[s][:, k * 256 + 128:k * 256 + 256], hbf[:, k, :]) for k in range(KC)],
                              [RING[s]] + HB, nbank=4)
                ui = rot("ub", NUB)
                ci = rot("cb", NCB)
                slots[f] = ci
                hi = L * NF + f
                wg = PC_CFFN + (L * 44 + f) * 3
                wv = PC_CFFN + (L * 44 + NF + f) * 3
                P.add("pool", lambda e: e.tensor_copy(out=ubuf[ui][:, :, 0:2], in_=halF[:, hi, :, :]),
                      reads=[HALF[hi]], writes=[UBH[ui]])
                for q, bq, wq, R, CR in ((0, bg, wg, UBG, CBG), (1, bv, wv, UBV, CBV)):
                    P.add("act", lambda e, q=q, bq=bq: e.activation(out=ubuf[ui][:, q, 2:2 + T], in_=ps[:, bq, :],
                                                                    func=AF.Copy), reads=[PSB[bq]], writes=[R[ui]])
                    P.add("act", lambda e, q=q, bq=bq, wq=wq: e.activation(out=cbuf[ci][:, q, :], in_=ps[:, bq, :],
                                                                           func=AF.Copy, scale=pcol(wq + 2)),
                          reads=[PSB[bq], PAR], writes=[CR[ci]])
                P.add("pool", lambda e: e.tensor_copy(out=halF[:, hi, :, :], in_=ubuf[ui][:, :, T:T + 2]),
                      reads=[UBG[ui], UBV[ui]], writes=[HALF[hi]])
                for tap in (1, 0):
                    for q, wq, R, CR in ((0, wg, UBG, CBG), (1, wv, UBV, CBV)):
                        P.add("dve", lambda e, q=q, wq=wq, tap=tap: e.scalar_tensor_tensor(
                            out=cbuf[ci][:, q, :], in0=ubuf[ui][:, q, tap:tap + T], scalar=pcol(wq + tap),
                            in1=cbuf[ci][:, q, :], op0=ALU.mult, op1=ALU.add),
                            reads=[R[ui], UBH[ui], PAR, CR[ci]], writes=[CR[ci]])

            def stage2(f):
                ci = slots[f]
                si = rot("sg", NSG)
                P.add("act", lambda e: e.activation(out=sgb[si][:, :], in_=cbuf[ci][:, 0, :], func=AF.Silu),
                      reads=[CBG[ci]], writes=[SG[si]])
                P.add("pool", lambda e: e.tensor_tensor(out=actb[:, f, :], in0=sgb[si][:, :], in1=cbuf[ci][:, 1, :],
                                                        op=ALU.mult), reads=[SG[si], CBV[ci]], writes=[ACTR[f]])

            def stage3(f):
                pe_op([(4 + d, ringA[d][:, f * 128:(f + 1) * 128], actb[:, f, :], f == 0, f == NF - 1)
                       for d in range(NRA)], RINGA + [ACTR[f]], [4 + d for d in range(NRA)])

            for f in range(NF):
                stage1(f)
                if f >= 1:
                    stage2(f - 1)
                if f >= 2:
                    stage3(f - 2)
            stage2(NF - 1)
            act_table_prefetch()
            if hoist is not None:
                hoist()
            stage3(NF - 2)

            def second(d, first, last):
                s = ring_load(kdn, d, 2816)
                b = mm_group([(ring[s][:, f * 128:(f + 1) * 128], actb[:, f, :]) for f in range(NF)],
                             [RING[s]] + ACTR, nbank=4)
                resid_add(xb, d, b)
                stats_add(xb, d, first, last)

            for d in range(NRA, KC - 2):
                second(d, d == NRA, False)
            stage3(NF - 1)
            for d in range(NRA):
                resid_add(xb, d, 4 + d)
                stats_add(xb, d, False, False)
            for d in range(KC - 2, KC):
                second(d, False, d == KC - 1)

        def emit_mixer1(xb, first_tile):
            stats_finish()
            L_ = T + 16
            for gi, w in enumerate(POOL_W):
                c0 = 2 * gi
                P.add("pool", lambda e, c0=c0: e.tensor_copy(out=h1g[:, :, 0:16], in_=h1hal[:, c0:c0 + 2, :]),
                      reads=[H1HAL[gi]], writes=[H1GH])
                for q in range(2):
                    P.add("dve", lambda e, q=q, c0=c0: e.scalar_tensor_tensor(
                        out=h1g[:, q, 16:16 + T], in0=xres[xb][:, c0 + q, :], scalar=pcol(PC_GMIX1 + c0 + q),
                        in1=rstd[:, :], op0=ALU.mult, op1=ALU.mult),
                        reads=[XC[xb][c0 + q], RS, PAR], writes=[H1G])
                P.add("pool", lambda e, c0=c0: e.tensor_copy(out=h1hal[:, c0:c0 + 2, :], in_=h1g[:, :, T:T + 16]),
                      reads=[H1G], writes=[H1HAL[gi]])
                src, srcR = h1g, [H1G, H1GH]
                bufs = [(sA, SA_), (sB, SB_)]
                step, lo = 1, 0
                nb = 0
                while step < w:
                    dst, dstR = bufs[nb % 2]
                    nlo = lo + step
                    P.add("dve", lambda e, src=src, dst=dst, nlo=nlo, step=step: e.tensor_tensor(
                        out=dst[:, :, nlo:L_], in0=src[:, :, nlo:L_], in1=src[:, :, nlo - step:L_ - step], op=ALU.add),
                        reads=srcR, writes=[dstR])
                    src, srcR = dst, [dstR]
                    lo = nlo
                    step *= 2
                    nb += 1
                P.add("dve", lambda e, src=src, c0=c0, w=w: e.scalar_tensor_tensor(
                    out=yab[:, c0:c0 + 2, :], in0=src[:, :, 16:16 + T], scalar=1.0 / w, in1=h1g[:, :, 16:16 + T],
                    op0=ALU.mult, op1=ALU.subtract), reads=srcR + [H1G], writes=[YAB[c0], YAB[c0 + 1]])
                if first_tile:
                    for q in range(2):
                        P.add("dve", lambda e, src=src, q=q, gi=gi: e.tensor_tensor(
                            out=t16[:, q, :], in0=src[:, q, 16:32], in1=invc[:, gi, :], op=ALU.mult),
                            reads=srcR + [INVC], writes=[T16])
                    P.add("dve", lambda e, c0=c0: e.tensor_tensor(out=yab[:, c0:c0 + 2, 0:16], in0=t16[:, :, :],
                                                                  in1=h1g[:, :, 16:32], op=ALU.subtract),
                          reads=[T16, H1G], writes=[YAB[c0], YAB[c0 + 1]])
                for oc in range(2):
                    pairs = []
                    for k in range(2):
                        o = ((gi * 2 + k) * 256) + oc * 128
                        pairs.append((wpool[:, o:o + 128], yab[:, c0 + k, :]))
                    b = mm_group(pairs, [WPOOL, YAB[c0], YAB[c0 + 1]])
                    c = c0 + oc
                    P.add("dve", lambda e, b=b, c=c: e.scalar_tensor_tensor(
                        out=xres[xb][:, c, :], in0=ps[:, b, :], scalar=pcol(PC_PSCALE + c), in1=xres[xb][:, c, :],
                        op0=ALU.mult, op1=ALU.add), reads=[PSB[b], PAR, XC[xb][c]], writes=[XC[xb][c]])
                    stats_add(xb, c, c == 0, c == KC - 1)

        def emit_final(xb):
            stats_finish()
            for c in range(KC):
                P.add("dve", lambda e, c=c: e.scalar_tensor_tensor(out=xres[xb][:, c, :], in0=xres[xb][:, c, :],
                                                                   scalar=pcol(PC_GFIN + c), in1=rstd[:, :],
                                                                   op0=ALU.mult, op1=ALU.mult),
                      reads=[XC[xb][c], RS, PAR], writes=[XC[xb][c]])

        store_ops = []

        def emit_store(t):
            xb_ = t % 2
            op = P.add("sp", lambda e: e.dma_start(out=out_l[t].rearrange("p (c t) -> p c t", c=KC),
                                                   in_=xres[xb_][:, :, :]), reads=XC[xb_], dma=f"st{xb_}")
            store_ops.append(op)

        def emit_xload(t):
            xb_ = t % 2
            P.add("sp", lambda e: e.dma_start(out=xres[xb_][:, :, :],
                                              in_=x_l[t].rearrange("p (c t) -> p c t", c=KC)),
                  writes=XC[xb_], dma=f"x{xb_}")

        norm_full(0, PC_GMIX0)
        for t in range(nt_run):
            xb = t % 2
            emit_mixer0(xb)
            if t >= 1:
                emit_store(t - 1)
            if t == 0:
                emit_conv(["wup0b", "wdn0", "wpool", "wup1a"])
            emit_ffn(xb, 0)
            if t + 1 < nt_run:
                emit_xload(t + 1)
            if t == 0:
                emit_conv(["wup1b", "wdn1"])
                P.add("sp", lambda e: e.dma_start(out=wpool[:, :], in_=wbf["wpool"][0]), reads=[WSG["wpool"]],
                      writes=[WPOOL], dma="wpool")
            emit_mixer1(xb, t == 0)
            hoist = None
            if t + 1 < nt_run:
                hoist = (lambda nb=(t + 1) % 2: norm_full(nb, PC_GMIX0))
            emit_ffn(xb, 1, hoist=hoist)
            emit_final(xb)
        emit_store(nt_run - 1)
        fence = P.add("sp", None)
        fence.deps = set(store_ops)

        sem_names = P.finalize()
        sems = {n: es.enter_context(nc.semaphore(n)) for n in sem_names}
        with nc.Block() as block:
            @block.tensor
            def _(e):
                P.emit("pe", e, sems)

            @block.scalar
            def _(e):
                P.emit("act", e, sems)

            @block.vector
            def _(e):
                P.emit("dve", e, sems)

            @block.gpsimd
            def _(e):
                P.emit("pool", e, sems)

            @block.sync
            def _(e):
                P.emit("sp", e, sems)
    return nc


def _blk(w, bc):
    K, N = w.shape
    nk = K // 128
    return np.ascontiguousarray(w.reshape(nk, 128, N // bc, bc).transpose(2, 1, 0, 3)).reshape(N // bc, 128, nk * bc)


def _col(v):
    return np.ascontiguousarray(v.reshape(-1, 128).T)


def prepare_inputs(x, norm_mix_even, w_in, conv_a, ln_a_g, ln_a_b, conv_b, w_out, norm_mix_odd, w_pool,
                   pool_scale, norm_ffn, w_up, conv_ffn_w, w_down, norm_final):
    f = np.float32
    par = np.zeros((128, NPAR), f)
    par[:, PC_GMIX0:PC_GMIX0 + 8] = _col(norm_mix_even[0])
    par[:, PC_GFFN0:PC_GFFN0 + 8] = _col(norm_ffn[0])
    par[:, PC_GMIX1:PC_GMIX1 + 8] = _col(norm_mix_odd[0])
    par[:, PC_GFFN1:PC_GFFN1 + 8] = _col(norm_ffn[1])
    par[:, PC_GFIN:PC_GFIN + 8] = _col(norm_final)
    par[:, PC_LNG:PC_LNG + 4] = _col(ln_a_g[0])
    par[:, PC_LNB:PC_LNB + 4] = _col(ln_a_b[0])
    par[:, PC_CONVB:PC_CONVB + 12] = conv_b[0].reshape(3, 4, 128).transpose(2, 1, 0).reshape(128, 12)
    par[:, PC_PSCALE:PC_PSCALE + 8] = _col(pool_scale[0])
    for L in range(2):
        c = conv_ffn_w[L].reshape(3, 44, 128).transpose(2, 1, 0).reshape(128, 132)
        par[:, PC_CFFN + L * 132:PC_CFFN + (L + 1) * 132] = c
    shared = {"par": par}
    shared["win32"] = _blk(w_in[0], 128)
    shared["wout32"] = _blk(w_out[0], 128)
    wcv = np.zeros((4, 2, 128, 16, 128), f)
    ca = conv_a[0].reshape(31, 4, 128)
    idx = np.arange(128)
    for i in range(31):
        for j in range(4):
            wcv[j, i // 16, idx, i % 16, idx] = ca[i, j]
    shared["wcv32"] = wcv.reshape(8, 128, 2048)
    for L in range(2):
        wu = w_up[L]
        g = wu[:, :DFF].reshape(D, NF, 128)
        v = wu[:, DFF:].reshape(D, NF, 128)
        gv = np.concatenate([g, v], axis=2).reshape(D, NF * 256)
        shared[f"wup{L}32"] = _blk(gv, 256)
        shared[f"wdn{L}32"] = _blk(w_down[L], 128)
    shared["wpool32"] = np.ascontiguousarray(w_pool[0].reshape(4, 2, 128, 256).transpose(2, 0, 1, 3)).reshape(1, 128, 2048)
    in_maps = []
    for b in range(x.shape[0]):
        xl = np.ascontiguousarray(x[b].reshape(NT, T, KC, 128).transpose(0, 3, 2, 1)).reshape(NT, 128, KC * T)
        m = dict(shared)
        m["x_l"] = xl
        in_maps.append(m)
    return in_maps


def assemble_output(res_list):
    outs = []
    for r in res_list:
        o = r["out_l"].reshape(NT, 128, KC, T).transpose(0, 3, 2, 1).reshape(S, D)
        outs.append(o)
    return np.ascontiguousarray(np.stack(outs, axis=0)).astype(np.float32)


def kernel(**inputs):
    inputs = {k: np.asarray(v, dtype=np.float32) for k, v in inputs.items()}
    in_maps = prepare_inputs(**inputs)
    nc = build_program()
    res = run_bass_kernel_spmd(nc, in_maps, core_ids=list(range(NCORE)))
    return assemble_output(res.results)
```

```python
import numpy as np
import concourse.bass as bass
import concourse.mybir as mybir
from concourse.bass_utils import run_bass_kernel_spmd

F32 = mybir.dt.float32
BF16 = mybir.dt.bfloat16
ALU = mybir.AluOpType
AF = mybir.ActivationFunctionType

D = 1024
S = 4096
NCORE = 8
DFF = 2816
NF = DFF // 128
T = 512
NT = S // T
KC = D // 128
RMS_EPS = 1e-6
LN_EPS = 1e-5
NSLOT = 5
NRA = 3
STATB = 7
RINGW = 2816
POOL_W = (2, 4, 8, 16)

PC_GMIX0, PC_GFFN0, PC_GMIX1, PC_GFFN1, PC_GFIN = 0, 8, 16, 24, 32
PC_LNG, PC_LNB = 40, 44
PC_CONVB = 48
PC_PSCALE = 60
PC_CFFN = 68
NPAR = PC_CFFN + 2 * 44 * 3


class Res:
    __slots__ = ("name", "w", "r")

    def __init__(self, name):
        self.name = name
        self.w = None
        self.r = []


class Op:
    __slots__ = ("eng", "idx", "fn", "deps", "sig", "sem", "val", "dma")


class Prog:
    ENGS = ("pe", "act", "dve", "pool", "sp")

    def __init__(self):
        self.ops = {e: [] for e in self.ENGS}
        self.dma_count = {}

    def add(self, eng, fn, reads=(), writes=(), dma=None):
        op = Op()
        op.eng = eng
        op.idx = len(self.ops[eng])
        op.fn = fn
        op.sig = False
        op.dma = dma
        op.sem = None
        op.val = 0
        deps = set()
        for r in reads:
            if r.w is not None:
                deps.add(r.w)
        for w in writes:
            if w.w is not None:
                deps.add(w.w)
            for rd in w.r:
                deps.add(rd)
        deps.discard(op)
        op.deps = deps
        for r in reads:
            r.r.append(op)
        for w in writes:
            w.w = op
            w.r = []
        self.ops[eng].append(op)
        return op

    def _needs_wait(self, op, dep):
        if dep.dma is not None:
            return True
        if dep.eng != op.eng:
            return True
        if op.eng == "pe":
            return False
        return (op.idx - dep.idx) <= 1

    def finalize(self):
        for e in self.ENGS:
            for op in self.ops[e]:
                for d in op.deps:
                    if self._needs_wait(op, d):
                        d.sig = True
        dma_keys = []
        for e in self.ENGS:
            cnt = 0
            for op in self.ops[e]:
                if op.dma is not None:
                    if op.dma not in self.dma_count:
                        self.dma_count[op.dma] = 0
                        dma_keys.append(op.dma)
                    self.dma_count[op.dma] += 16
                    op.sem = "dma_" + op.dma
                    op.val = self.dma_count[op.dma]
                elif op.sig:
                    cnt += 1
                    op.sem = "eng_" + e
                    op.val = cnt
        return ["eng_" + e for e in self.ENGS if e != "sp"] + ["dma_" + k for k in dma_keys]

    def emit(self, eng_name, engine, sems):
        seen = {}
        nwait = 0
        for op in self.ops[eng_name]:
            need = {}
            for d in op.deps:
                if not self._needs_wait(op, d):
                    continue
                if seen.get(d.sem, 0) >= d.val:
                    continue
                if need.get(d.sem, 0) < d.val:
                    need[d.sem] = d.val
            for sname, v in need.items():
                engine.wait_ge(sems[sname], v)
                seen[sname] = v
                nwait += 1
            if op.fn is None:
                continue
            ins = op.fn(engine)
            if op.dma is not None:
                ins.then_inc(sems[op.sem], 16)
            elif op.sig:
                ins.then_inc(sems[op.sem], 1)
        return nwait


def build_program(nt_run=NT, stop_after="final"):
    nc = bass.Bass("TRN2", target_bir_lowering=False)
    P = Prog()

    def dram(name, shape, dt, kind):
        return nc.dram_tensor(name, shape, dt, kind=kind).ap()

    x_l = dram("x_l", [NT, 128, KC * T], F32, "ExternalInput")
    par_d = dram("par", [128, NPAR], F32, "ExternalInput")
    w32 = {
        "win": dram("win32", [20, 128, 1024], F32, "ExternalInput"),
        "wcv": dram("wcv32", [8, 128, 2048], F32, "ExternalInput"),
        "wout": dram("wout32", [8, 128, 1024], F32, "ExternalInput"),
        "wup0": dram("wup032", [NF, 128, 2048], F32, "ExternalInput"),
        "wdn0": dram("wdn032", [8, 128, 2816], F32, "ExternalInput"),
        "wpool": dram("wpool32", [1, 128, 2048], F32, "ExternalInput"),
        "wup1": dram("wup132", [NF, 128, 2048], F32, "ExternalInput"),
        "wdn1": dram("wdn132", [8, 128, 2816], F32, "ExternalInput"),
    }
    wbf = {k: dram(k + "bf", list(v.shape), BF16, "Internal") for k, v in w32.items()}
    out_l = dram("out_l", [NT, 128, KC * T], F32, "ExternalOutput")

    from contextlib import ExitStack
    es = ExitStack()

    def sb(name, shape, dt):
        return es.enter_context(nc.sbuf_tensor(name, shape, dt))

    with es:
        xres = [sb(f"xres{i}", [128, KC, T], F32) for i in range(2)]
        hbf = sb("hbf", [128, KC, T], BF16)
        NSQ = 3
        sqbf = sb("sqbf", [128, NSQ, T], BF16)
        actb = sb("actb", [128, NF, T], BF16)
        yab = sb("yab", [128, KC, T], BF16)
        NUB, NCB, NSG, NTMP = 2, 2, 2, 3
        ubuf = [sb(f"ubuf{i}", [128, 2, T + 2], F32) for i in range(NUB)]
        cbuf = [sb(f"cbuf{i}", [128, 2, T], F32) for i in range(NCB)]
        sgb = [sb(f"sgb{i}", [128, T], F32) for i in range(NSG)]
        tmp = [sb(f"tmp{i}", [128, T], F32) for i in range(NTMP)]
        ring = [sb(f"ring{i}", [128, RINGW], BF16) for i in range(NSLOT)]
        ringA = [sb(f"ringA{i}", [128, RINGW], BF16) for i in range(NRA)]
        ssum = sb("ssum", [128, T], F32)
        rstd = sb("rstd", [128, T], F32)
        mean = sb("mean", [128, T], F32)
        var = sb("var", [128, T], F32)
        rstd2 = sb("rstd2", [128, T], F32)
        glu = sb("glu", [128, 4, T + 30], BF16)
        abf = sb("abf", [128, 4, T], BF16)
        asq = sb("asq", [128, 4, T], BF16)
        accb = sb("accb", [128, T], F32)
        h1g = sb("h1g", [128, 2, T + 16], F32)
        sA = sb("sA", [128, 2, T + 16], F32)
        sB = sb("sB", [128, 2, T + 16], F32)
        t16 = sb("t16", [128, 2, 16], F32)
        halF = sb("halF", [128, 2 * NF, 2, 2], F32)
        bhal = sb("bhal", [128, 4, 2], F32)
        h1hal = sb("h1hal", [128, KC, 16], F32)
        par = sb("par_sb", [128, NPAR], F32)
        ones = sb("ones", [128, 128], BF16)
        epsc = sb("epsc", [128, 2], F32)
        dummy = sb("actdummy", [128, 2], F32)
        invc = sb("invc", [128, 4, 16], F32)
        wpool = sb("wpool_sb", [128, 2048], BF16)
        ps = es.enter_context(nc.psum_tensor("ps", [128, 8, T], F32))

        XC = [[Res(f"x{b}_{c}") for c in range(KC)] for b in range(2)]
        HB = [Res(f"hbf{c}") for c in range(KC)]
        SQ = [Res(f"sq{i}") for i in range(NSQ)]
        ACTR = [Res(f"act{f}") for f in range(NF)]
        YAB = [Res(f"yab{c}") for c in range(KC)]
        UBG = [Res(f"ubg{i}") for i in range(NUB)]
        UBV = [Res(f"ubv{i}") for i in range(NUB)]
        UBH = [Res(f"ubh{i}") for i in range(NUB)]
        CBG = [Res(f"cbg{i}") for i in range(NCB)]
        CBV = [Res(f"cbv{i}") for i in range(NCB)]
        SG = [Res(f"sg{i}") for i in range(NSG)]
        TMP = [Res(f"tmp{i}") for i in range(NTMP)]
        RING = [Res(f"ring{i}") for i in range(NSLOT)]
        RINGA = [Res(f"ringA{i}") for i in range(NRA)]
        PSB = [Res(f"psb{i}") for i in range(8)]
        SS, RS, MEAN, VAR, RS2 = Res("ss"), Res("rs"), Res("mean"), Res("var"), Res("rs2")
        GLU = [Res(f"glu{j}") for j in range(4)]
        ABF = [Res(f"abf{j}") for j in range(4)]
        ASQ = [Res(f"asq{j}") for j in range(4)]
        ACCB = Res("accb")
        H1G, H1GH, SA_, SB_, T16 = Res("h1g"), Res("h1gh"), Res("sA"), Res("sB"), Res("t16")
        HALF = [Res(f"half{i}") for i in range(2 * NF)]
        BHAL = [Res(f"bhal{j}") for j in range(4)]
        H1HAL = [Res(f"h1hal{g}") for g in range(4)]
        DUMMY = Res("dummy")
        PAR, ONES, EPSC, INVC, WPOOL = Res("par"), Res("ones"), Res("epsc"), Res("invc"), Res("wpool")

        def af_ap(j):
            return cbuf[j // 2][:, j % 2, :]

        def AF_R(j):
            return (CBG if j % 2 == 0 else CBV)[j // 2]

        def cv_ap(j, lo, hi):
            return ubuf[j // 2][:, j % 2, lo:hi]

        def CV_R(j):
            return (UBG if j % 2 == 0 else UBV)[j // 2]

        state = {"bank": 0, "bank4": 0, "ring": 0, "ub": 0, "cb": 0, "sg": 0, "tmp": 0, "sq": 0}

        def rot(key, n):
            v = state[key] % n
            state[key] += 1
            return v

        def pcol(c):
            return par[:, c:c + 1]

        P.add("sp", lambda e: e.dma_start(out=par[:, :], in_=par_d[:, :]), writes=[PAR], dma="par")
        P.add("sp", lambda e: e.dma_start(out=xres[0][:, :, :], in_=x_l[0].rearrange("p (c t) -> p c t", c=KC)),
              writes=XC[0], dma="x0")
        P.add("dve", lambda e: e.memset(ones[:, :], 1.0), writes=[ONES])

        def eps_fill(e):
            e.memset(epsc[:, 0:1], RMS_EPS)
            return e.memset(epsc[:, 1:2], LN_EPS)
        P.add("dve", eps_fill, writes=[EPSC])
        P.add("dve", lambda e: e.memset(glu[:, :, 0:30], 0.0), writes=GLU)
        P.add("dve", lambda e: e.memset(bhal[:, :, :], 0.0), writes=BHAL)
        P.add("dve", lambda e: e.memset(halF[:, :, :, :], 0.0), writes=HALF)
        P.add("dve", lambda e: e.memset(h1hal[:, :, :], 0.0), writes=H1HAL)

        def invc_fill(e):
            ins = None
            for gi, w in enumerate(POOL_W):
                ins = e.memset(invc[:, gi, :], 1.0 / w)
                for j in range(w - 1):
                    ins = e.memset(invc[:, gi, j:j + 1], 1.0 / (j + 1))
            return ins
        P.add("pool", invc_fill, writes=[INVC])

        CVG = {
            "win_a": ("win", range(0, 8)), "win_b": ("win", range(8, 20)), "wcv": ("wcv", range(8)),
            "wout": ("wout", range(8)), "wup0a": ("wup0", range(0, 11)), "wup0b": ("wup0", range(11, NF)),
            "wdn0": ("wdn0", range(8)), "wpool": ("wpool", range(1)), "wup1a": ("wup1", range(0, 11)),
            "wup1b": ("wup1", range(11, NF)), "wdn1": ("wdn1", range(8)),
        }
        GRP = {}
        for g, (k, blks) in CVG.items():
            for b in blks:
                GRP[(k, b)] = g
        WSG = {g: Res("wsg_" + g) for g in CVG}

        conv_ops = []
        CONV_WINDOW = 4

        def emit_conv(groups):
            for g in groups:
                k, blks = CVG[g]
                last = None
                for b in blks:
                    last = P.add("pool", lambda e, k=k, b=b: e.dma_start(out=wbf[k][b], in_=w32[k][b]),
                                 dma="cv_" + g)
                    if len(conv_ops) >= CONV_WINDOW:
                        last.deps.add(conv_ops[-CONV_WINDOW])
                    conv_ops.append(last)
                WSG[g].w = last
                WSG[g].r = []

        emit_conv(["win_a", "wcv", "win_b", "wout", "wup0a"])

        def ring_load(kind, blk, ncols):
            s = rot("ring", NSLOT)
            P.add("sp", lambda e: e.dma_start(out=ring[s][:, 0:ncols], in_=wbf[kind][blk]),
                  reads=[WSG[GRP[(kind, blk)]]], writes=[RING[s]], dma=f"ring{s}")
            return s

        def pe_op(items, reads, banks):
            def fn(e):
                ins = None
                for (b, l, r, st, sp_) in items:
                    ins = e.matmul(ps[:, b, :], lhsT=l, rhs=r, start=st, stop=sp_)
                return ins
            P.add("pe", fn, reads=reads, writes=[PSB[b] for b in banks])

        def mm_group(pairs, reads, nbank=7, split_reads=None):
            b = rot("bank", 7) if nbank == 7 else rot("bank4", 4)
            n = len(pairs)
            if split_reads is None:
                pe_op([(b, l, r, i == 0, i == n - 1) for i, (l, r) in enumerate(pairs)], reads, [b])
            else:
                for i, (l, r) in enumerate(pairs):
                    pe_op([(b, l, r, i == 0, i == n - 1)], reads + [split_reads[i]], [b])
            gcount["n"] += 1
            flush_stats()
            return b

        pend = []
        gcount = {"n": 0}

        def flush_stats(all_=False):
            while pend and (all_ or pend[0][0] < gcount["n"]):
                _, q, first, last = pend.pop(0)
                pe_op([(STATB, ones[:, :], sqbf[:, q, :], first, last)], [ONES, SQ[q]], [STATB])

        def stats_add(xb, c, first, last):
            q = rot("sq", NSQ)
            if any(p_[1] == q for p_ in pend):
                flush_stats(True)
            P.add("act", lambda e: e.activation(out=sqbf[:, q, :], in_=xres[xb][:, c, :], func=AF.Square),
                  reads=[XC[xb][c]], writes=[SQ[q]])
            pend.append((gcount["n"], q, first, last))

        def stats_finish():
            flush_stats(True)
            P.add("act", lambda e: e.activation(out=ssum[:, :], in_=ps[:, STATB, :], func=AF.Ln, scale=1.0 / D,
                                                bias=epsc[:, 0:1]), reads=[PSB[STATB], EPSC], writes=[SS])
            P.add("act", lambda e: e.activation(out=rstd[:, :], in_=ssum[:, :], func=AF.Exp, scale=-0.5),
                  reads=[SS], writes=[RS])

        def act_table_prefetch():
            P.add("act", lambda e: e.activation(out=dummy[:, 0:1], in_=epsc[:, 0:1], func=AF.Ln),
                  reads=[EPSC], writes=[DUMMY])

        def h_ops(xb, gc):
            for c in range(KC):
                eng = "dve"
                P.add(eng, lambda e, c=c: e.scalar_tensor_tensor(out=hbf[:, c, :], in0=xres[xb][:, c, :],
                                                                 scalar=pcol(gc + c), in1=rstd[:, :],
                                                                 op0=ALU.mult, op1=ALU.mult),
                      reads=[XC[xb][c], RS, PAR], writes=[HB[c]])

        def norm_full(xb, gc):
            for c in range(KC):
                stats_add(xb, c, c == 0, c == KC - 1)
            stats_finish()
            h_ops(xb, gc)

        def resid_add(xb, d, b):
            P.add("dve", lambda e: e.tensor_tensor(out=xres[xb][:, d, :], in0=xres[xb][:, d, :], in1=ps[:, b, :],
                                                   op=ALU.add), reads=[PSB[b], XC[xb][d]], writes=[XC[xb][d]])

        def emit_mixer0(xb):
            nproj = {"n": 0}

            def proj(blk):
                s = ring_load("win", blk, 1024)
                pairs = [(ring[s][:, k * 128:(k + 1) * 128], hbf[:, k, :]) for k in range(KC)]
                nproj["n"] += 1
                if nproj["n"] == 1:
                    return mm_group(pairs, [RING[s]], split_reads=HB)
                return mm_group(pairs, [RING[s]] + HB)
            for j in range(4):
                bgate = proj(4 + j)
                bval = proj(j)
                ta = rot("tmp", NTMP)
                P.add("act", lambda e, ta=ta, bgate=bgate: e.activation(out=tmp[ta][:, :], in_=ps[:, bgate, :],
                                                                        func=AF.Tanh, scale=0.5),
                      reads=[PSB[bgate]], writes=[TMP[ta]])
                tb = rot("tmp", NTMP)
                P.add("act", lambda e, tb=tb, bval=bval: e.activation(out=tmp[tb][:, :], in_=ps[:, bval, :],
                                                                      func=AF.Copy, scale=0.5),
                      reads=[PSB[bval]], writes=[TMP[tb]])
                P.add("dve", lambda e, ta=ta, tb=tb, j=j: e.scalar_tensor_tensor(
                    out=glu[:, j, 30:30 + T], in0=tmp[ta][:, :], scalar=1.0, in1=tmp[tb][:, :],
                    op0=ALU.add, op1=ALU.mult), reads=[TMP[ta], TMP[tb]], writes=[GLU[j]])
            for j in range(4):
                s0 = ring_load("wcv", 2 * j, 2048)
                s1 = ring_load("wcv", 2 * j + 1, 2048)
                pairs = []
                for i in range(31):
                    sl = s0 if i < 16 else s1
                    ii = i % 16
                    pairs.append((ring[sl][:, ii * 128:(ii + 1) * 128], glu[:, j, i:i + T]))
                b = mm_group(pairs, [RING[s0], RING[s1], GLU[j]])
                P.add("act", lambda e, j=j, b=b: e.activation(out=af_ap(j), in_=ps[:, b, :], func=AF.Copy),
                      reads=[PSB[b]], writes=[AF_R(j)])
                P.add("act", lambda e, j=j, b=b: e.activation(out=abf[:, j, :], in_=ps[:, b, :], func=AF.Copy),
                      reads=[PSB[b]], writes=[ABF[j]])
                P.add("act", lambda e, j=j, b=b: e.activation(out=asq[:, j, :], in_=ps[:, b, :], func=AF.Square),
                      reads=[PSB[b]], writes=[ASQ[j]])
                P.add("pool", lambda e, j=j: e.tensor_copy(out=glu[:, j, 0:30], in_=glu[:, j, T:T + 30]),
                      reads=[GLU[j]], writes=[GLU[j]])

            bbank = {}

            def b_pe(j):
                bc = proj(12 + j)
                bbc = proj(16 + j)
                bb = proj(8 + j)
                bbank[j] = (bc, bbc, bb)

            def b_ew(j):
                bc, bbc, bb = bbank[j]
                tc = rot("tmp", NTMP)
                ui = j // 2
                P.add("pool", lambda e: e.tensor_copy(out=cv_ap(j, 0, 2), in_=bhal[:, j, :]),
                      reads=[BHAL[j]], writes=[UBH[ui]])
                P.add("act", lambda e: e.activation(out=tmp[tc][:, :], in_=ps[:, bc, :], func=AF.Copy),
                      reads=[PSB[bc]], writes=[TMP[tc]])
                P.add("dve", lambda e: e.tensor_tensor(out=cv_ap(j, 2, 2 + T), in0=tmp[tc][:, :], in1=ps[:, bbc, :],
                                                       op=ALU.mult), reads=[TMP[tc], PSB[bbc]], writes=[CV_R(j)])
                P.add("pool", lambda e: e.tensor_copy(out=bhal[:, j, :], in_=cv_ap(j, T, T + 2)),
                      reads=[CV_R(j)], writes=[BHAL[j]])
                wc = PC_CONVB + 3 * j
                P.add("dve", lambda e: e.tensor_scalar(out=accb[:, :], in0=cv_ap(j, 2, 2 + T), scalar1=pcol(wc + 2),
                                                       scalar2=None, op0=ALU.mult),
                      reads=[CV_R(j), PAR], writes=[ACCB])
                for tap in (1, 0):
                    P.add("dve", lambda e, tap=tap: e.scalar_tensor_tensor(
                        out=accb[:, :], in0=cv_ap(j, tap, tap + T), scalar=pcol(wc + tap), in1=accb[:, :],
                        op0=ALU.mult, op1=ALU.add), reads=[CV_R(j), UBH[ui], PAR, ACCB], writes=[ACCB])
                P.add("dve", lambda e: e.tensor_tensor(out=yab[:, 4 + j, :], in0=accb[:, :], in1=ps[:, bb, :],
                                                       op=ALU.mult), reads=[ACCB, PSB[bb]], writes=[YAB[4 + j]])

            for j in (0, 1):
                b_pe(j)
                b_ew(j)
            b1 = mm_group([(ones[:, :], abf[:, j, :]) for j in range(4)], [ONES] + ABF)
            b2 = mm_group([(ones[:, :], asq[:, j, :]) for j in range(4)], [ONES] + ASQ)
            P.add("dve", lambda e: e.tensor_scalar(out=mean[:, :], in0=ps[:, b1, :], scalar1=1.0 / 512, scalar2=None,
                                                   op0=ALU.mult), reads=[PSB[b1]], writes=[MEAN])
            tm = rot("tmp", NTMP)
            P.add("dve", lambda e: e.tensor_tensor(out=tmp[tm][:, :], in0=mean[:, :], in1=mean[:, :], op=ALU.mult),
                  reads=[MEAN], writes=[TMP[tm]])
            P.add("dve", lambda e: e.scalar_tensor_tensor(out=var[:, :], in0=ps[:, b2, :], scalar=1.0 / 512,
                                                          in1=tmp[tm][:, :], op0=ALU.mult, op1=ALU.subtract),
                  reads=[PSB[b2], TMP[tm]], writes=[VAR])
            P.add("act", lambda e: e.activation(out=ssum[:, :], in_=var[:, :], func=AF.Ln, scale=1.0,
                                                bias=epsc[:, 1:2]), reads=[VAR, EPSC], writes=[SS])
            P.add("act", lambda e: e.activation(out=rstd2[:, :], in_=ssum[:, :], func=AF.Exp, scale=-0.5),
                  reads=[SS], writes=[RS2])
            for j in (2, 3):
                b_pe(j)
            for j in range(4):
                t1 = rot("tmp", NTMP)
                P.add("dve", lambda e, j=j, t1=t1: e.tensor_tensor(out=tmp[t1][:, :], in0=af_ap(j), in1=mean[:, :],
                                                                   op=ALU.subtract),
                      reads=[AF_R(j), MEAN], writes=[TMP[t1]])
                t2 = rot("tmp", NTMP)
                P.add("dve", lambda e, t1=t1, t2=t2: e.tensor_tensor(out=tmp[t2][:, :], in0=tmp[t1][:, :],
                                                                     in1=rstd2[:, :], op=ALU.mult),
                      reads=[TMP[t1], RS2], writes=[TMP[t2]])
                P.add("act", lambda e, j=j, t2=t2: e.activation(out=yab[:, j, :], in_=tmp[t2][:, :], func=AF.Silu,
                                                                scale=pcol(PC_LNG + j), bias=pcol(PC_LNB + j)),
                      reads=[TMP[t2], PAR], writes=[YAB[j]])
            act_table_prefetch()
            for j in (2, 3):
                b_ew(j)
            for d in range(KC):
                s = ring_load("wout", d, 1024)
                b = mm_group([(ring[s][:, k * 128:(k + 1) * 128], yab[:, k, :]) for k in range(KC)],
                             [RING[s]] + YAB)
                resid_add(xb, d, b)
                stats_add(xb, d, d == 0, d == KC - 1)

        def emit_ffn(xb, L, hoist=None):
            stats_finish()
            h_ops(xb, PC_GFFN0 if L == 0 else PC_GFFN1)
            kup = "wup0" if L == 0 else "wup1"
            kdn = "wdn0" if L == 0 else "wdn1"
            for d in range(NRA):
                P.add("sp", lambda e, d=d: e.dma_start(out=ringA[d][:, :], in_=wbf[kdn][d]),
                      reads=[WSG[GRP[(kdn, d)]]], writes=[RINGA[d]], dma=f"ringA{d}")
            slots = {}

            def stage1(f):
                s = ring_load(kup, f, 2048)
                if f == 0:
                    bg = mm_group([(ring[s][:, k * 256:k * 256 + 128], hbf[:, k, :]) for k in range(KC)],
                                  [RING[s]], nbank=4, split_reads=HB)
                else:
                    bg = mm_group([(ring[s][:, k * 256:k * 256 + 128], hbf[:, k, :]) for k in range(KC)],
                                  [RING[s]] + HB, nbank=4)
                bv = mm_group([(ring[s][:, k * 256 + 128:k * 256 + 256], hbf[:, k, :]) for k in range(KC)],
                              [RING[s]] + HB, nbank=4)
                ui = rot("ub", NUB)
                ci = rot("cb", NCB)
                slots[f] = ci
                hi = L * NF + f
                wg = PC_CFFN + (L * 44 + f) * 3
                wv = PC_CFFN + (L * 44 + NF + f) * 3
                P.add("pool", lambda e: e.tensor_copy(out=ubuf[ui][:, :, 0:2], in_=halF[:, hi, :, :]),
                      reads=[HALF[hi]], writes=[UBH[ui]])
                for q, bq, wq, R, CR in ((0, bg, wg, UBG, CBG), (1, bv, wv, UBV, CBV)):
                    P.add("act", lambda e, q=q, bq=bq: e.activation(out=ubuf[ui][:, q, 2:2 + T], in_=ps[:, bq, :],
                                                                    func=AF.Copy), reads=[PSB[bq]], writes=[R[ui]])
                    P.add("act", lambda e, q=q, bq=bq, wq=wq: e.activation(out=cbuf[ci][:, q, :], in_=ps[:, bq, :],
                                                                           func=AF.Copy, scale=pcol(wq + 2)),
                          reads=[PSB[bq], PAR], writes=[CR[ci]])
                P.add("pool", lambda e: e.tensor_copy(out=halF[:, hi, :, :], in_=ubuf[ui][:, :, T:T + 2]),
                      reads=[UBG[ui], UBV[ui]], writes=[HALF[hi]])
                for tap in (1, 0):
                    for q, wq, R, CR in ((0, wg, UBG, CBG), (1, wv, UBV, CBV)):
                        P.add("dve", lambda e, q=q, wq=wq, tap=tap: e.scalar_tensor_tensor(
                            out=cbuf[ci][:, q, :], in0=ubuf[ui][:, q, tap:tap + T], scalar=pcol(wq + tap),
                            in1=cbuf[ci][:, q, :], op0=ALU.mult, op1=ALU.add),
                            reads=[R[ui], UBH[ui], PAR, CR[ci]], writes=[CR[ci]])

            def stage2(f):
                ci = slots[f]
                si = rot("sg", NSG)
                P.add("act", lambda e: e.activation(out=sgb[si][:, :], in_=cbuf[ci][:, 0, :], func=AF.Silu),
                      reads=[CBG[ci]], writes=[SG[si]])
                P.add("pool", lambda e: e.tensor_tensor(out=actb[:, f, :], in0=sgb[si][:, :], in1=cbuf[ci][:, 1, :],
                                                        op=ALU.mult), reads=[SG[si], CBV[ci]], writes=[ACTR[f]])

            def stage3(f):
                pe_op([(4 + d, ringA[d][:, f * 128:(f + 1) * 128], actb[:, f, :], f == 0, f == NF - 1)
                       for d in range(NRA)], RINGA + [ACTR[f]], [4 + d for d in range(NRA)])

            for f in range(NF):
                stage1(f)
                if f >= 1:
                    stage2(f - 1)
                if f >= 2:
                    stage3(f - 2)
            stage2(NF - 1)
            act_table_prefetch()
            if hoist is not None:
                hoist()
            stage3(NF - 2)

            def second(d, first, last):
                s = ring_load(kdn, d, 2816)
                b = mm_group([(ring[s][:, f * 128:(f + 1) * 128], actb[:, f, :]) for f in range(NF)],
                             [RING[s]] + ACTR, nbank=4)
                resid_add(xb, d, b)
                stats_add(xb, d, first, last)

            for d in range(NRA, KC - 2):
                second(d, d == NRA, False)
            stage3(NF - 1)
            for d in range(NRA):
                resid_add(xb, d, 4 + d)
                stats_add(xb, d, False, False)
            for d in range(KC - 2, KC):
                second(d, False, d == KC - 1)

        def emit_mixer1(xb, first_tile):
            stats_finish()
            upd_pending = []
            L_ = T + 16
            for gi, w in enumerate(POOL_W):
                c0 = 2 * gi
                P.add("pool", lambda e, c0=c0: e.tensor_copy(out=h1g[:, :, 0:16], in_=h1hal[:, c0:c0 + 2, :]),
                      reads=[H1HAL[gi]], writes=[H1GH])
                for q in range(2):
                    P.add("dve", lambda e, q=q, c0=c0: e.scalar_tensor_tensor(
                        out=h1g[:, q, 16:16 + T], in0=xres[xb][:, c0 + q, :], scalar=pcol(PC_GMIX1 + c0 + q),
                        in1=rstd[:, :], op0=ALU.mult, op1=ALU.mult),
                        reads=[XC[xb][c0 + q], RS, PAR], writes=[H1G])
                P.add("pool", lambda e, c0=c0: e.tensor_copy(out=h1hal[:, c0:c0 + 2, :], in_=h1g[:, :, T:T + 16]),
                      reads=[H1G], writes=[H1HAL[gi]])
                src, srcR = h1g, [H1G, H1GH]
                bufs = [(sA, SA_), (sB, SB_)]
                step, lo = 1, 0
                nb = 0
                while step < w:
                    dst, dstR = bufs[nb % 2]
                    nlo = lo + step
                    P.add("dve", lambda e, src=src, dst=dst, nlo=nlo, step=step: e.tensor_tensor(
                        out=dst[:, :, nlo:L_], in0=src[:, :, nlo:L_], in1=src[:, :, nlo - step:L_ - step], op=ALU.add),
                        reads=srcR, writes=[dstR])
                    src, srcR = dst, [dstR]
                    lo = nlo
                    step *= 2
                    nb += 1
                P.add("dve", lambda e, src=src, c0=c0, w=w: e.scalar_tensor_tensor(
                    out=yab[:, c0:c0 + 2, :], in0=src[:, :, 16:16 + T], scalar=1.0 / w, in1=h1g[:, :, 16:16 + T],
                    op0=ALU.mult, op1=ALU.subtract), reads=srcR + [H1G], writes=[YAB[c0], YAB[c0 + 1]])
                if first_tile:
                    for q in range(2):
                        P.add("dve", lambda e, src=src, q=q, gi=gi: e.tensor_tensor(
                            out=t16[:, q, :], in0=src[:, q, 16:32], in1=invc[:, gi, :], op=ALU.mult),
                            reads=srcR + [INVC], writes=[T16])
                    P.add("dve", lambda e, c0=c0: e.tensor_tensor(out=yab[:, c0:c0 + 2, 0:16], in0=t16[:, :, :],
                                                                  in1=h1g[:, :, 16:32], op=ALU.subtract),
                          reads=[T16, H1G], writes=[YAB[c0], YAB[c0 + 1]])
                banks = []
                for oc in range(2):
                    pairs = []
                    for k in range(2):
                        o = ((gi * 2 + k) * 256) + oc * 128
                        pairs.append((wpool[:, o:o + 128], yab[:, c0 + k, :]))
                    banks.append(mm_group(pairs, [WPOOL, YAB[c0], YAB[c0 + 1]]))
                if upd_pending:
                    upd_pending.pop(0)()

                def upd(c0=c0, banks=banks):
                    for oc in range(2):
                        c = c0 + oc
                        b = banks[oc]
                        P.add("dve", lambda e, b=b, c=c: e.scalar_tensor_tensor(
                            out=xres[xb][:, c, :], in0=ps[:, b, :], scalar=pcol(PC_PSCALE + c),
                            in1=xres[xb][:, c, :], op0=ALU.mult, op1=ALU.add),
                            reads=[PSB[b], PAR, XC[xb][c]], writes=[XC[xb][c]])
                        stats_add(xb, c, c == 0, c == KC - 1)
                upd_pending.append(upd)
            while upd_pending:
                upd_pending.pop(0)()

        def emit_final(xb):
            stats_finish()
            for c in range(KC):
                P.add("dve", lambda e, c=c: e.scalar_tensor_tensor(out=xres[xb][:, c, :], in0=xres[xb][:, c, :],
                                                                   scalar=pcol(PC_GFIN + c), in1=rstd[:, :],
                                                                   op0=ALU.mult, op1=ALU.mult),
                      reads=[XC[xb][c], RS, PAR], writes=[XC[xb][c]])

        store_ops = []

        def emit_store(t):
            xb_ = t % 2
            op = P.add("sp", lambda e: e.dma_start(out=out_l[t].rearrange("p (c t) -> p c t", c=KC),
                                                   in_=xres[xb_][:, :, :]), reads=XC[xb_], dma=f"st{xb_}")
            store_ops.append(op)

        def emit_xload(t):
            xb_ = t % 2
            P.add("sp", lambda e: e.dma_start(out=xres[xb_][:, :, :],
                                              in_=x_l[t].rearrange("p (c t) -> p c t", c=KC)),
                  writes=XC[xb_], dma=f"x{xb_}")

        norm_full(0, PC_GMIX0)
        for t in range(nt_run):
            xb = t % 2
            emit_mixer0(xb)
            if t >= 1:
                emit_store(t - 1)
            if t == 0:
                emit_conv(["wup0b", "wdn0", "wpool", "wup1a"])
            emit_ffn(xb, 0)
            if t + 1 < nt_run:
                emit_xload(t + 1)
            if t == 0:
                emit_conv(["wup1b", "wdn1"])
                P.add("sp", lambda e: e.dma_start(out=wpool[:, :], in_=wbf["wpool"][0]), reads=[WSG["wpool"]],
                      writes=[WPOOL], dma="wpool")
            emit_mixer1(xb, t == 0)
            hoist = None
            if t + 1 < nt_run:
                hoist = (lambda nb=(t + 1) % 2: norm_full(nb, PC_GMIX0))
            emit_ffn(xb, 1, hoist=hoist)
            emit_final(xb)
        emit_store(nt_run - 1)
        fence = P.add("sp", None)
        fence.deps = set(store_ops)

        sem_names = P.finalize()
        sems = {n: es.enter_context(nc.semaphore(n)) for n in sem_names}
        with nc.Block() as block:
            @block.tensor
            def _(e):
                P.emit("pe", e, sems)

            @block.scalar
            def _(e):
                P.emit("act", e, sems)

            @block.vector
            def _(e):
                P.emit("dve", e, sems)

            @block.gpsimd
            def _(e):
                P.emit("pool", e, sems)

            @block.sync
            def _(e):
                P.emit("sp", e, sems)
    return nc


def _blk(w, bc):
    K, N = w.shape
    nk = K // 128
    return np.ascontiguousarray(w.reshape(nk, 128, N // bc, bc).transpose(2, 1, 0, 3)).reshape(N // bc, 128, nk * bc)


def _col(v):
    return np.ascontiguousarray(v.reshape(-1, 128).T)


def prepare_inputs(x, norm_mix_even, w_in, conv_a, ln_a_g, ln_a_b, conv_b, w_out, norm_mix_odd, w_pool,
                   pool_scale, norm_ffn, w_up, conv_ffn_w, w_down, norm_final):
    f = np.float32
    par = np.zeros((128, NPAR), f)
    par[:, PC_GMIX0:PC_GMIX0 + 8] = _col(norm_mix_even[0])
    par[:, PC_GFFN0:PC_GFFN0 + 8] = _col(norm_ffn[0])
    par[:, PC_GMIX1:PC_GMIX1 + 8] = _col(norm_mix_odd[0])
    par[:, PC_GFFN1:PC_GFFN1 + 8] = _col(norm_ffn[1])
    par[:, PC_GFIN:PC_GFIN + 8] = _col(norm_final)
    par[:, PC_LNG:PC_LNG + 4] = _col(ln_a_g[0])
    par[:, PC_LNB:PC_LNB + 4] = _col(ln_a_b[0])
    par[:, PC_CONVB:PC_CONVB + 12] = conv_b[0].reshape(3, 4, 128).transpose(2, 1, 0).reshape(128, 12)
    par[:, PC_PSCALE:PC_PSCALE + 8] = _col(pool_scale[0])
    for L in range(2):
        c = conv_ffn_w[L].reshape(3, 44, 128).transpose(2, 1, 0).reshape(128, 132)
        par[:, PC_CFFN + L * 132:PC_CFFN + (L + 1) * 132] = c
    shared = {"par": par}
    shared["win32"] = _blk(w_in[0], 128)
    shared["wout32"] = _blk(w_out[0], 128)
    wcv = np.zeros((4, 2, 128, 16, 128), f)
    ca = conv_a[0].reshape(31, 4, 128)
    idx = np.arange(128)
    for i in range(31):
        for j in range(4):
            wcv[j, i // 16, idx, i % 16, idx] = ca[i, j]
    shared["wcv32"] = wcv.reshape(8, 128, 2048)
    for L in range(2):
        wu = w_up[L]
        g = wu[:, :DFF].reshape(D, NF, 128)
        v = wu[:, DFF:].reshape(D, NF, 128)
        gv = np.concatenate([g, v], axis=2).reshape(D, NF * 256)
        shared[f"wup{L}32"] = _blk(gv, 256)
        shared[f"wdn{L}32"] = _blk(w_down[L], 128)
    shared["wpool32"] = np.ascontiguousarray(w_pool[0].reshape(4, 2, 128, 256).transpose(2, 0, 1, 3)).reshape(1, 128, 2048)
    in_maps = []
    for b in range(x.shape[0]):
        xl = np.ascontiguousarray(x[b].reshape(NT, T, KC, 128).transpose(0, 3, 2, 1)).reshape(NT, 128, KC * T)
        m = dict(shared)
        m["x_l"] = xl
        in_maps.append(m)
    return in_maps


def assemble_output(res_list):
    outs = []
    for r in res_list:
        o = r["out_l"].reshape(NT, 128, KC, T).transpose(0, 3, 2, 1).reshape(S, D)
        outs.append(o)
    return np.ascontiguousarray(np.stack(outs, axis=0)).astype(np.float32)


def kernel(**inputs):
    inputs = {k: np.asarray(v, dtype=np.float32) for k, v in inputs.items()}
    in_maps = prepare_inputs(**inputs)
    nc = build_program()
    res = run_bass_kernel_spmd(nc, in_maps, core_ids=list(range(NCORE)))
    return assemble_output(res.results)
```
